# Optimizing a Trainium2 kernel written in Bass

```python
import math
import jax
import jax.numpy as jnp
from jax import lax
import numpy as np

D_MODEL = 1024
BATCH = 2
SEQ = 16384
DEPTH = 2

GRID_W = 64
CTX_LEN = 256
NORM_EPS = 1e-6
LB_FLOOR = 1e-20
N_MOD = 9

DA_HEADS = 4
DA_HALF_DIM = 64
DA_V_DIM = 2 * DA_HALF_DIM
DA_QK_WIDTH = DA_HEADS * 2 * DA_HALF_DIM
DA_WIDTH = DA_HEADS * DA_V_DIM
ROPE_THETA = 10000.0
Q_BLOCK = 128

HG_HEADS = 4
HG_KEY_DIM = 64
HG_VAL_DIM = 64
HG_KEY_WIDTH = HG_HEADS * HG_KEY_DIM
HG_WIDTH = HG_HEADS * HG_VAL_DIM
HG_CHUNK = 64

POOL_WINDOWS = (2, 4, 8, 16)
POOL_GROUPS = 4
POOL_GROUP_DIM = 64
POOL_WIDTH = POOL_GROUPS * POOL_GROUP_DIM

N_BRANCH = 3
D_FF = 2816

IN_SIZES = (DA_QK_WIDTH, DA_QK_WIDTH, DA_WIDTH, HG_KEY_WIDTH, HG_KEY_WIDTH, HG_KEY_WIDTH, HG_WIDTH, HG_WIDTH, POOL_WIDTH, N_BRANCH * D_MODEL)
D_IN = sum(IN_SIZES)

kernel_name = 'hybrid_diffattn_hgrn2_pool_macaron_prefix'


def rms_norm(x, gain):
    xf = x.astype(jnp.float32)
    y = xf * lax.rsqrt(jnp.mean(xf * xf, axis=-1, keepdims=True) + NORM_EPS)
    return (y * gain.astype(jnp.float32)).astype(x.dtype)


def modulate(x, shift, scale):
    return x * (1 + scale) + shift


def swiglu(x, w1, w3, w2):
    return (jax.nn.silu(x @ w1) * (x @ w3)) @ w2


def split_columns(z):
    offsets = np.cumsum(IN_SIZES)[:-1].tolist()
    return jnp.split(z, offsets, axis=-1)


def axial_rope_tables(n_tokens):
    rows = n_tokens // GRID_W
    row = jnp.repeat(jnp.arange(rows, dtype=jnp.int32), GRID_W).astype(jnp.float32)
    col = jnp.tile(jnp.arange(GRID_W, dtype=jnp.int32), rows).astype(jnp.float32)
    axis_dim = DA_HALF_DIM // 2
    inv_freq = ROPE_THETA ** (-jnp.arange(0, axis_dim, 2, dtype=jnp.float32) / axis_dim)
    ang_r = row[:, None] * inv_freq[None, :]
    ang_c = col[:, None] * inv_freq[None, :]
    return (jnp.cos(ang_r), jnp.sin(ang_r), jnp.cos(ang_c), jnp.sin(ang_c))


def rotate_pairs(x, cos, sin):
    x1, x2 = jnp.split(x, 2, axis=-1)
    return jnp.concatenate([x1 * cos - x2 * sin, x1 * sin + x2 * cos], axis=-1)


def apply_axial_rope(x, rope):
    cos_r, sin_r, cos_c, sin_c = [t[None, :, None, None, :] for t in rope]
    xf = x.astype(jnp.float32)
    half = DA_HALF_DIM // 2
    out = jnp.concatenate([rotate_pairs(xf[..., :half], cos_r, sin_r),
                           rotate_pairs(xf[..., half:], cos_c, sin_c)], axis=-1)
    return out.astype(x.dtype)


def diff_lambda(lq1, lk1, lq2, lk2, lam_init):
    e1 = jnp.exp(jnp.sum(lq1.astype(jnp.float32) * lk1.astype(jnp.float32)))
    e2 = jnp.exp(jnp.sum(lq2.astype(jnp.float32) * lk2.astype(jnp.float32)))
    return e1 - e2 + lam_init


def diff_attend(q, k, v, lam):
    s = jnp.einsum('bqhcd,bkhcd->bhcqk', q, k).astype(jnp.float32) * (DA_HALF_DIM ** -0.5)
    p = jax.nn.softmax(s, axis=-1)
    w = p[:, :, 0] - lam * p[:, :, 1]
    return jnp.einsum('bhqk,bkhe->bqhe', w.astype(v.dtype), v)


def diff_attention_latent(q, k_all, v_all, lam):
    b, n, h, _, d = q.shape
    nb = n // Q_BLOCK
    qb = jnp.moveaxis(q.reshape(b, nb, Q_BLOCK, h, 2, d), 1, 0)
    ob = lax.map(lambda blk: diff_attend(blk, k_all, v_all, lam), qb)
    return jnp.moveaxis(ob, 0, 1).reshape(b, n, h, DA_V_DIM)


def diff_post(o, gain, lam_init):
    b, n = o.shape[:2]
    return (rms_norm(o, gain) * (1.0 - lam_init)).reshape(b, n, DA_WIDTH)


def heads_first(z, head_dim):
    b, n, _ = z.shape
    return z.reshape(b, n, -1, head_dim).transpose(0, 2, 1, 3).astype(jnp.float32)


def flip_time(a):
    return a[..., ::-1, :]


def hgrn_direction_inputs(zq, zf_fwd, zf_bwd, zi, lb):
    q = heads_first(jax.nn.silu(zq), HG_KEY_DIM)
    v = heads_first(zi, HG_VAL_DIM)
    keys, logfs = [], []
    for zf, lbd in ((zf_fwd, lb[0]), (zf_bwd, lb[1])):
        zf = heads_first(zf, HG_KEY_DIM)
        lbd = lbd.astype(jnp.float32).reshape(1, HG_HEADS, 1, HG_KEY_DIM)
        keys.append((1.0 - lbd) * jax.nn.sigmoid(-zf))
        log_lb = jnp.log(jnp.maximum(lbd, LB_FLOOR))
        logfs.append(jnp.logaddexp(log_lb, jnp.log1p(-lbd) + jax.nn.log_sigmoid(zf)))
    return (jnp.stack([q, flip_time(q)]),
            jnp.stack([keys[0], flip_time(keys[1])]),
            jnp.stack([logfs[0], flip_time(logfs[1])]),
            jnp.stack([v, flip_time(v)]))


def hgrn_chunk_scan(q, k, logf, v, s0):
    n = q.shape[-2]
    nc = n // HG_CHUNK

    def to_chunks(a):
        return jnp.moveaxis(a.reshape(a.shape[:-2] + (nc, HG_CHUNK, a.shape[-1])), -3, 0)

    in_chunk = jnp.tril(jnp.ones((HG_CHUNK, HG_CHUNK), dtype=bool))[:, :, None]

    def step(state, inp):
        qc, kc, lfc, vc = inp
        a = jnp.cumsum(lfc, axis=-2)
        a_end = a[..., -1:, :]
        o_inter = jnp.einsum('...tk,...kv->...tv', qc * jnp.exp(a), state)
        diff = a[..., :, None, :] - a[..., None, :, :]
        decay = jnp.where(in_chunk, jnp.exp(jnp.where(in_chunk, diff, 0.0)), 0.0)
        scores = jnp.einsum('...tk,...tsk,...sk->...ts', qc, decay, kc)
        o_intra = jnp.einsum('...ts,...sv->...tv', scores, vc)
        new_state = (jnp.exp(a_end[..., 0, :])[..., :, None] * state
                     + jnp.einsum('...sk,...sv->...kv', kc * jnp.exp(a_end - a), vc))
        return new_state, o_inter + o_intra

    final_state, o = lax.scan(step, s0, (to_chunks(q), to_chunks(k), to_chunks(logf), to_chunks(v)))
    o = jnp.moveaxis(o, 0, -3)
    return o.reshape(o.shape[:-3] + (n, o.shape[-1])), final_state


def hgrn_scans(zq, zff, zfb, zi, zqc, zffc, zfbc, zic, lb):
    b = zq.shape[0]
    s0 = jnp.zeros((2, b, HG_HEADS, HG_KEY_DIM, HG_VAL_DIM), jnp.float32)
    o_ctx, s_ctx = hgrn_chunk_scan(*hgrn_direction_inputs(zqc, zffc, zfbc, zic, lb), s0)
    o_lat, _ = hgrn_chunk_scan(*hgrn_direction_inputs(zq, zff, zfb, zi, lb), s_ctx)
    return o_lat, o_ctx


def hgrn_readout(o_dir, zg, gain):
    b, n, _ = zg.shape
    o = jnp.swapaxes(o_dir[0] + flip_time(o_dir[1]), 1, 2)
    o = rms_norm(o, gain).astype(zg.dtype)
    return (o * jax.nn.silu(zg.reshape(b, n, HG_HEADS, HG_VAL_DIM))).reshape(b, n, HG_WIDTH)


def hgrn_lower_bounds(logits):
    p = jax.nn.softmax(logits.astype(jnp.float32), axis=0)
    return jnp.cumsum(p, axis=0) - p[0:1]


def centred_mean_minus_self(x, window):
    b, n, ch = x.shape
    csum = jnp.concatenate([jnp.zeros((b, 1, ch), x.dtype), jnp.cumsum(x, axis=1)], axis=1)
    idx = jnp.arange(n)
    lo = jnp.clip(idx - window // 2, 0, n)
    hi = jnp.clip(idx + window // 2, 0, n)
    count = (hi - lo).astype(x.dtype)[None, :, None]
    return (csum[:, hi] - csum[:, lo]) / count - x


def pool_branch(zp, pool_w, pool_scale):
    b, n, _ = zp.shape
    groups = zp.reshape(b, n, POOL_GROUPS, POOL_GROUP_DIM).astype(jnp.float32)
    mixed = jnp.stack([centred_mean_minus_self(groups[:, :, g], w) for g, w in enumerate(POOL_WINDOWS)], axis=2)
    y = jnp.einsum('bngc,gcd->bngd', mixed.astype(zp.dtype), pool_w)
    return y.reshape(b, n, POOL_WIDTH) * pool_scale


def gated_merge(o_da, o_hg, o_pool, zgate, w_pa, w_ph, w_pp, w_o):
    b, n, _ = zgate.shape
    g = jax.nn.sigmoid(zgate.reshape(b, n, N_BRANCH, D_MODEL))
    y = g[:, :, 0] * (o_da @ w_pa) + g[:, :, 1] * (o_hg @ w_ph) + g[:, :, 2] * (o_pool @ w_pp)
    return y @ w_o


def token_mixer(u, uc, w_in, lam, lam_init, da_gain, lb, hg_gain, pool_w, pool_scale,
                w_pa, w_ph, w_pp, w_o, rope, need_ctx):
    b, n, _ = u.shape
    n_ctx = uc.shape[1]
    dq, dk, dv, hq, hff, hfb, hi, hgate, zp, zgate = split_columns(u @ w_in)
    dqc, dkc, dvc, hqc, hffc, hfbc, hic, hgatec, zpc, zgatec = split_columns(uc @ w_in)

    def da_heads(z, length):
        return z.reshape(b, length, DA_HEADS, 2, DA_HALF_DIM)

    q = apply_axial_rope(da_heads(dq, n), rope)
    k = apply_axial_rope(da_heads(dk, n), rope)
    kc = da_heads(dkc, n_ctx)
    v = dv.reshape(b, n, DA_HEADS, DA_V_DIM)
    vc = dvc.reshape(b, n_ctx, DA_HEADS, DA_V_DIM)
    o_da = diff_attention_latent(q, jnp.concatenate([k, kc], axis=1), jnp.concatenate([v, vc], axis=1), lam)
    o_hg_dir, o_hgc_dir = hgrn_scans(hq, hff, hfb, hi, hqc, hffc, hfbc, hic, lb)
    o_pool = pool_branch(zp, pool_w, pool_scale)
    mix = gated_merge(diff_post(o_da, da_gain, lam_init), hgrn_readout(o_hg_dir, hgate, hg_gain),
                      o_pool, zgate, w_pa, w_ph, w_pp, w_o)
    if not need_ctx:
        return mix, None
    o_dac = diff_attend(da_heads(dqc, n_ctx), kc, vc, lam)
    mix_c = gated_merge(diff_post(o_dac, da_gain, lam_init), hgrn_readout(o_hgc_dir, hgatec, hg_gain),
                        pool_branch(zpc, pool_w, pool_scale), zgatec, w_pa, w_ph, w_pp, w_o)
    return mix, mix_c


def setup_inputs(seed: int = 0) -> dict:
    key = jax.random.key(seed)
    ks = jax.random.split(key, 32)
    D = D_MODEL

    def nrm(k, shape, scale):
        return jax.random.normal(k, shape, jnp.float32) * scale

    return {
        'x': nrm(ks[0], (BATCH, SEQ, D), 1.0),
        'c': nrm(ks[1], (BATCH, D), 1.0),
        'ctx': nrm(ks[2], (BATCH, CTX_LEN, D), 1.0),
        'c_ctx': nrm(ks[3], (D,), 1.0),
        'w_ada': nrm(ks[4], (DEPTH, D, N_MOD * D), 0.5 * D ** -0.5),
        'b_ada': nrm(ks[5], (DEPTH, N_MOD * D), 0.02),
        'norm_ffn1': 1.0 + nrm(ks[6], (DEPTH, D), 0.02),
        'norm_mix': 1.0 + nrm(ks[7], (DEPTH, D), 0.02),
        'norm_ffn2': 1.0 + nrm(ks[8], (DEPTH, D), 0.02),
        'ffn1_w1': nrm(ks[9], (DEPTH, D, D_FF), D ** -0.5),
        'ffn1_w3': nrm(ks[10], (DEPTH, D, D_FF), D ** -0.5),
        'ffn1_w2': nrm(ks[11], (DEPTH, D_FF, D), D_FF ** -0.5),
        'ffn2_w1': nrm(ks[12], (DEPTH, D, D_FF), D ** -0.5),
        'ffn2_w3': nrm(ks[13], (DEPTH, D, D_FF), D ** -0.5),
        'ffn2_w2': nrm(ks[14], (DEPTH, D_FF, D), D_FF ** -0.5),
        'w_in': nrm(ks[15], (DEPTH, D, D_IN), D ** -0.5),
        'da_lambda_q1': nrm(ks[16], (DEPTH, DA_HALF_DIM), 0.1),
        'da_lambda_k1': nrm(ks[17], (DEPTH, DA_HALF_DIM), 0.1),
        'da_lambda_q2': nrm(ks[18], (DEPTH, DA_HALF_DIM), 0.1),
        'da_lambda_k2': nrm(ks[19], (DEPTH, DA_HALF_DIM), 0.1),
        'da_subln': 1.0 + nrm(ks[20], (DEPTH, DA_V_DIM), 0.02),
        'hg_lb_logits': nrm(ks[21], (DEPTH, 2, HG_KEY_WIDTH), 0.5),
        'hg_norm': 1.0 + nrm(ks[22], (DEPTH, HG_VAL_DIM), 0.02),
        'pool_w': nrm(ks[23], (DEPTH, POOL_GROUPS, POOL_GROUP_DIM, POOL_GROUP_DIM), POOL_GROUP_DIM ** -0.5),
        'pool_scale': 1.0 + nrm(ks[24], (DEPTH, POOL_WIDTH), 0.02),
        'w_proj_da': nrm(ks[25], (DEPTH, DA_WIDTH, D), DA_WIDTH ** -0.5),
        'w_proj_hg': nrm(ks[26], (DEPTH, HG_WIDTH, D), HG_WIDTH ** -0.5),
        'w_proj_pool': nrm(ks[27], (DEPTH, POOL_WIDTH, D), POOL_WIDTH ** -0.5),
        'w_out': nrm(ks[28], (DEPTH, D, D), D ** -0.5),
        'final_norm': 1.0 + nrm(ks[29], (D,), 0.02),
    }


def reference(x, c, ctx, c_ctx, w_ada, b_ada, norm_ffn1, norm_mix, norm_ffn2,
              ffn1_w1, ffn1_w3, ffn1_w2, ffn2_w1, ffn2_w3, ffn2_w2, w_in,
              da_lambda_q1, da_lambda_k1, da_lambda_q2, da_lambda_k2, da_subln,
              hg_lb_logits, hg_norm, pool_w, pool_scale,
              w_proj_da, w_proj_hg, w_proj_pool, w_out, final_norm):
    n = x.shape[1]
    rope = axial_rope_tables(n)
    lb_all = hgrn_lower_bounds(hg_lb_logits)
    h, hc = x, ctx
    for l in range(DEPTH):
        need_ctx = l < DEPTH - 1
        m = [t[:, None, :] for t in jnp.split(jax.nn.silu(c) @ w_ada[l] + b_ada[l], N_MOD, axis=-1)]
        mc = jnp.split(jax.nn.silu(c_ctx) @ w_ada[l] + b_ada[l], N_MOD, axis=-1)
        ffn1 = (ffn1_w1[l], ffn1_w3[l], ffn1_w2[l])
        ffn2 = (ffn2_w1[l], ffn2_w3[l], ffn2_w2[l])
        h = h + 0.5 * m[2] * swiglu(modulate(rms_norm(h, norm_ffn1[l]), m[0], m[1]), *ffn1)
        hc = hc + 0.5 * mc[2] * swiglu(modulate(rms_norm(hc, norm_ffn1[l]), mc[0], mc[1]), *ffn1)
        u = modulate(rms_norm(h, norm_mix[l]), m[3], m[4])
        uc = modulate(rms_norm(hc, norm_mix[l]), mc[3], mc[4])
        lam_init = 0.8 - 0.6 * math.exp(-0.3 * l)
        lam = diff_lambda(da_lambda_q1[l], da_lambda_k1[l], da_lambda_q2[l], da_lambda_k2[l], lam_init)
        mix, mix_c = token_mixer(u, uc, w_in[l], lam, lam_init, da_subln[l], lb_all[l], hg_norm[l],
                                 pool_w[l], pool_scale[l], w_proj_da[l], w_proj_hg[l], w_proj_pool[l],
                                 w_out[l], rope, need_ctx)
        h = h + m[5] * mix
        h = h + 0.5 * m[8] * swiglu(modulate(rms_norm(h, norm_ffn2[l]), m[6], m[7]), *ffn2)
        if need_ctx:
            hc = hc + mc[5] * mix_c
            hc = hc + 0.5 * mc[8] * swiglu(modulate(rms_norm(hc, norm_ffn2[l]), mc[6], mc[7]), *ffn2)
    return rms_norm(h, final_norm)
```

```python
import numpy as np
import ml_dtypes
import concourse.bass as bass
import concourse.mybir as mybir
from concourse.bass_utils import run_bass_kernel_spmd

F32 = mybir.dt.float32
BF16 = mybir.dt.bfloat16
AF = mybir.ActivationFunctionType
ALU = mybir.AluOpType
NPBF = ml_dtypes.bfloat16

D = 1024
SEQ = 16384
BATCH = 2
CTX = 256
DFF = 2816
DIN = 6144
NCORE = 8
TLAT = 4096
TCTX = 64
NT = TLAT + TCTX
NKEY = SEQ + CTX
EPS = 1e-6

ENGS = ('pe', 'act', 'dve', 'pool', 'sp')


class Tl:
    __slots__ = ('w', 'r', 'const', 'lsem', 'ssem', 'excl')

    def __init__(self, const=False, excl=False):
        self.w = None
        self.r = []
        self.const = const
        self.excl = excl
        self.lsem = None
        self.ssem = None


class Op:
    __slots__ = ('eng', 'fn', 'deps', 'sig', 'val', 'dsem')


class Prog:
    def __init__(self):
        self.ops = {e: [] for e in ENGS}
        self.dcnt = {}
        self.deng = {}

    def op(self, eng, meth, kw, reads=(), writes=(), dsem=None):
        o = Op()
        o.eng = eng
        o.fn = (meth, kw)
        o.sig = False
        o.val = 0
        o.dsem = dsem
        wr = {}
        rd = {}
        for t in reads:
            if t.w is not None:
                wr[id(t.w)] = t.w
            if t.excl:
                for r in t.r:
                    if r.eng != eng:
                        wr[id(r)] = r
        for t in writes:
            if t.w is not None:
                wr[id(t.w)] = t.w
            for r in t.r:
                rd[id(r)] = r
        deps = []
        for d in wr.values():
            if d.dsem is None and dsem is None and d.eng == eng and eng == 'pe':
                continue
            deps.append(d)
        for d in rd.values():
            if id(d) in wr:
                continue
            if d.dsem is None and dsem is None and d.eng == eng and eng == 'pe':
                continue
            deps.append(d)
        for d in deps:
            if d.dsem is None:
                d.sig = True
        o.deps = deps
        for t in reads:
            if not t.const:
                t.r.append(o)
        for t in writes:
            t.w = o
            t.r = []
        if dsem is not None:
            if len(writes) > 0:
                t = writes[0]
                if t.lsem is None:
                    t.lsem = 'l%d' % len(self.dcnt)
                    self.dcnt[t.lsem] = 0
                dsem = t.lsem
            else:
                t = reads[0]
                if t.ssem is None:
                    t.ssem = 's%d' % len(self.dcnt)
                    self.dcnt[t.ssem] = 0
                dsem = t.ssem
            o.dsem = dsem
            self.dcnt[dsem] = self.dcnt[dsem] + 16
            o.val = self.dcnt[dsem]
        self.ops[eng].append(o)
        return o

    def emit(self, nc):
        for e in ENGS:
            c = 0
            for o in self.ops[e]:
                if o.dsem is None and o.sig:
                    c += 1
                    o.val = c
        import contextlib
        with contextlib.ExitStack() as st:
            esem = {e: st.enter_context(nc.semaphore('es_' + e)) for e in ENGS}
            dsem = {k: st.enter_context(nc.semaphore('ds_' + k)) for k in self.dcnt}
            block = st.enter_context(nc.Block())
            prog = self

            def run(eng_name, e):
                waited = {}
                for o in prog.ops[eng_name]:
                    for d in o.deps:
                        if d.dsem is not None:
                            key = ('d', d.dsem)
                            sem = dsem[d.dsem]
                        else:
                            key = ('e', d.eng)
                            sem = esem[d.eng]
                        if waited.get(key, 0) >= d.val:
                            continue
                        waited[key] = d.val
                        e.wait_ge(sem, d.val)
                    ins = getattr(e, o.fn[0])(**o.fn[1])
                    if o.dsem is not None:
                        ins.then_inc(dsem[o.dsem], 16)
                    elif o.sig:
                        ins.then_inc(esem[eng_name], 1)
                if eng_name == 'sp':
                    for k, v in prog.dcnt.items():
                        e.wait_ge(dsem[k], v)

            @block.tensor
            def _(e):
                run('pe', e)

            @block.scalar
            def _(e):
                run('act', e)

            @block.vector
            def _(e):
                run('dve', e)

            @block.gpsimd
            def _(e):
                run('pool', e)

            @block.sync
            def _(e):
                run('sp', e)


class Ring:
    def __init__(self, n, excl=False):
        self.n = n
        self.i = 0
        self.t = [Tl(excl=excl) for _ in range(n)]

    def next(self):
        k = self.i % self.n
        self.i += 1
        return k, self.t[k]


def arr_w(W):
    K, N = W.shape
    return np.ascontiguousarray(W.reshape(K // 128, 128, N // 128, 128).transpose(2, 1, 0, 3))


def vec_p(v):
    return np.ascontiguousarray(v.reshape(-1, 128).T)


class Ops:
    def __init__(self, P):
        self.P = P

    def MM(self, out, lhsT, rhs, start, stop, reads, writes):
        self.P.op('pe', 'matmul', dict(out=out, lhsT=lhsT, rhs=rhs, start=start, stop=stop), reads, writes)

    def TR(self, out, in_, ident, reads, writes):
        self.P.op('pe', 'transpose', dict(out=out, in_=in_, identity=ident), reads, writes)

    def ACT(self, out, in_, func, reads, writes, **kw):
        self.P.op('act', 'activation', dict(out=out, in_=in_, func=func, **kw), reads, writes)

    def TT(self, out, in0, in1, op, reads, writes, eng='dve'):
        self.P.op(eng, 'tensor_tensor', dict(out=out, in0=in0, in1=in1, op=op), reads, writes)

    def TS(self, out, in0, s1, op0, reads, writes, s2=None, op1=None, eng='dve'):
        kw = dict(out=out, in0=in0, scalar1=s1, scalar2=s2, op0=op0)
        if op1 is not None:
            kw['op1'] = op1
        self.P.op(eng, 'tensor_scalar', kw, reads, writes)

    def STT(self, out, in0, scalar, in1, op0, op1, reads, writes):
        self.P.op('dve', 'scalar_tensor_tensor', dict(out=out, in0=in0, scalar=scalar, in1=in1, op0=op0, op1=op1),
                  reads, writes)

    def CP(self, out, in_, reads, writes, eng='dve'):
        self.P.op(eng, 'tensor_copy', dict(out=out, in_=in_), reads, writes)

    def RCP(self, out, in_, reads, writes, scratch=None):
        if scratch is None:
            self.P.op('dve', 'reciprocal', dict(out=out, in_=in_), reads, writes)
        else:
            self.P.op('dve', 'reciprocal_approx_accurate', dict(out=out, in_=in_, scratch=scratch), reads, writes)

    def MS(self, ap, val, writes, eng='dve'):
        self.P.op(eng, 'memset', dict(ap=ap, constant=val), (), writes)

    def DMA(self, eng, out, in_, reads, writes, dsem):
        self.P.op(eng, 'dma_start', dict(out=out, in_=in_), reads, writes, dsem)


def build_T(merge_layer, pre_layer, final, with_ctx, tiles_override=None, dbg=99):
    import contextlib
    nc = bass.Bass("TRN2", target_bir_lowering=False)
    P = Prog()
    O = Ops(P)
    MM, ACT, TT, TS, STT, CP, RCP, MS, DMA = O.MM, O.ACT, O.TT, O.TS, O.STT, O.CP, O.RCP, O.MS, O.DMA
    ntok = NT if with_ctx else TLAT

    def din(name, shape, dt=F32):
        return nc.dram_tensor(name, list(shape), dt, kind="ExternalInput").ap()

    def dout(name, shape, dt=F32):
        return nc.dram_tensor(name, list(shape), dt, kind="ExternalOutput").ap()

    hT_in = din("hT_in", [D, ntok])
    cs_d = din("cs", [128, 8, 2])
    ones_d = din("ones", [128, 128])
    bd64_d = din("bd64", [128, 128])
    rperm_d = din("rperm", [128, 128])
    layers = sorted(set(x for x in (merge_layer, pre_layer) if x is not None))
    wada_d = {l: din(f"wada{l}", [72, 128, 8, 128]) for l in layers}
    bada_d = {l: din(f"bada{l}", [128, 72]) for l in layers}
    if merge_layer is not None:
        ml = merge_layer
        nf2_d = din("nf2", [128, 8])
        f2w1_d = din("f2w1", [22, 128, 8, 128])
        f2w3_d = din("f2w3", [22, 128, 8, 128])
        f2w2_d = din("f2w2", [8, 128, 22, 128])
        wmrg_d = din("wmrg", [8, 128, 8, 128])
        wout_d = din("wout", [8, 128, 8, 128])
        hgn_d = din("hgn", [128, 1])
        dag_d = din("dag", [128, 1])
        odaT_d = din("odaT", [512, ntok], BF16)
        ohfT_d = din("ohfT", [256, ntok])
        ohbT_d = din("ohbT", [256, ntok])
        sgT_in_d = din("sgT_in", [256, ntok])
        opoolT_d = din("opoolT", [256, ntok], BF16)
        gateT_in_d = din("gateT_in", [3072, ntok])
    if pre_layer is not None:
        pl = pre_layer
        nf1_d = din("nf1", [128, 8])
        nmix_d = din("nmix", [128, 8])
        f1w1_d = din("f1w1", [22, 128, 8, 128])
        f1w3_d = din("f1w3", [22, 128, 8, 128])
        f1w2_d = din("f1w2", [8, 128, 22, 128])
        win_d = din("win", [48, 128, 8, 128])
        lbl_d = din("lbl", [128, 4, 2])
        cosT_d = din("cosT", [128, ntok])
        sinT_d = din("sinT", [128, ntok])
        hT_out = dout("hT_out", [D, ntok])
        qT_d = dout("qT", [512, ntok], BF16)
        kT_d = dout("kT", [512, ntok], BF16)
        vT_d = dout("vT", [512, ntok], BF16)
        hqT_d = dout("hqT", [256, ntok])
        kkfT_d = dout("kkfT", [256, ntok])
        kkbT_d = dout("kkbT", [256, ntok])
        lffT_d = dout("lffT", [256, ntok])
        lfbT_d = dout("lfbT", [256, ntok])
        hiT_d = dout("hiT", [256, ntok], BF16)
        sgT_d = dout("sgT", [256, ntok])
        zpT_d = dout("zpT", [256, ntok], BF16)
        gateT_d = dout("gateT", [3072, ntok])
    if final:
        fnorm_d = din("fnorm", [128, 8])
        outT_d = dout("outT", [D, TLAT])

    with contextlib.ExitStack() as st:
        def sb(name, shape, dt=F32):
            return st.enter_context(nc.sbuf_tensor("s_" + name, list(shape), dt))

        TS_ = 1024
        hT = sb("hT", [128, 8, TS_]); hT_t = [[Tl() for _ in range(2)] for _ in range(8)]
        xn = sb("xn", [128, 8, TS_], BF16); xn_t = [[Tl() for _ in range(2)] for _ in range(8)]
        hid = sb("hid", [128, 22, TS_], BF16); hid_t = [[Tl() for _ in range(2)] for _ in range(22)]
        NW8 = 6 if (merge_layer is not None and pre_layer is not None) else 8
        wk8 = sb("wk8", [128, NW8, 8, 128], BF16); wk8_r = Ring(NW8)
        NW22 = 3
        wk22 = sb("wk22", [128, NW22, 22, 128], BF16); wk22_r = Ring(NW22)
        tmpf = sb("tmpf", [128, 4, 512]); tmpf_r = Ring(4)
        sqb = sb("sqb", [128, 3, 512], BF16); sqb_r = Ring(3)
        rstd = sb("rstd", [128, 2, 512]); rstd_r = Ring(2)
        stgf = sb("stgf", [128, 4, 512]); stgf_r = Ring(4)
        stgb = sb("stgb", [128, 4, 512], BF16); stgb_r = Ring(4)
        ps = st.enter_context(nc.psum_tensor("ps", [128, 8, 512], F32)); ps_r = Ring(8, excl=True)
        ones_b = sb("ones_b", [128, 128], BF16); ones_t = Tl(const=True)
        bd64_b = sb("bd64_b", [128, 128], BF16); bd64_t = Tl(const=True)
        rperm_b = sb("rperm_b", [128, 128], BF16); rperm_t = Tl(const=True)
        cs = sb("cs", [128, 8, 2]); cs_t = Tl(const=True)
        wada = sb("wada", [128, 2, 8, 128]); wada_r = Ring(2)
        mod = {l: sb(f"mod{l}", [128, 72, 2]) for l in layers}; mod_t = Tl(const=True)
        bada = {l: sb(f"bada{l}", [128, 72]) for l in layers}
        vec_t = Tl(const=True)
        epsb = sb("epsb", [128, 1])
        oneb = sb("oneb", [128, 1])
        MS(epsb[:], EPS, [vec_t])
        MS(oneb[:], 1.0, [vec_t])

        DMA('pool', ones_b[:], ones_d, (), [ones_t], 'w')
        DMA('pool', bd64_b[:], bd64_d, (), [bd64_t], 'w')
        DMA('pool', rperm_b[:], rperm_d, (), [rperm_t], 'w')
        DMA('sp', cs[:], cs_d, (), [cs_t], 'l')
        cs_flat = cs[:].rearrange("p a b -> p (a b)")
        ACT(cs_flat, cs_flat, AF.Silu, [cs_t], [cs_t])

        for l in layers:
            DMA('sp', bada[l][:], bada_d[l], (), [mod_t], 'l')
            bk, bt = ps_r.next()
            need = set()
            if l == pre_layer:
                need.update(range(0, 40))
            if l == merge_layer:
                need.update(range(40, 72))
            MS(mod[l][:], 0.0, [mod_t])
            for fc in sorted(need):
                wi, wt = wada_r.next()
                DMA('sp', wada[:, wi], wada_d[l][fc], (), [wt], 'l')
                for kc in range(8):
                    MM(ps[:, bk, 2 * fc:2 * fc + 2], wada[:, wi, kc, :], cs[:, kc, :], kc == 0, kc == 7,
                       [wt, cs_t], [bt])
            lo, hi = min(need), max(need) + 1
            TT(mod[l][:, lo:hi, :], ps[:, bk, 2 * lo:2 * hi].rearrange("p (a b) -> p a b", b=2),
               bada[l][:, lo:hi].unsqueeze(2).to_broadcast([128, hi - lo, 2]), ALU.add, [bt, mod_t], [mod_t])

        def mv(l, k):
            return mod[l][:, 8 * k:8 * k + 8, :]

        vecs = {}

        def mk_AB(name, gain_d, l, kshift, kscale):
            g = sb("g_" + name, [128, 8])
            A = sb("A_" + name, [128, 8, 2])
            B = sb("B_" + name, [128, 8, 2])
            vecs[name + 'A'] = A
            vecs[name + 'B'] = B
            DMA('sp', g[:], gain_d, (), [vec_t], 'l')
            TS(A[:], mv(l, kscale), 1.0, ALU.add, [mod_t, vec_t], [vec_t])
            TT(A[:], A[:], g[:].unsqueeze(2).to_broadcast([128, 8, 2]), ALU.mult, [vec_t], [vec_t])
            CP(B[:], mv(l, kshift), [mod_t, vec_t], [vec_t])

        def mk_gate(name, l, k, mul):
            G = sb("G_" + name, [128, 8, 2])
            vecs[name] = G
            TS(G[:], mv(l, k), float(mul), ALU.mult, [mod_t, vec_t], [vec_t])

        if merge_layer is not None:
            mk_gate('g5', ml, 5, 1.0)
            mk_AB('f2', nf2_d, ml, 6, 7)
            mk_gate('g8', ml, 8, 0.5)
            hgn = sb("hgn", [128, 1])
            DMA('sp', hgn[:], hgn_d, (), [vec_t], 'l')
            dagn = sb("dagn", [128, 1])
            DMA('sp', dagn[:], dag_d, (), [vec_t], 'l')
            TS(dagn[:], dagn[:], float(1.0 - (0.8 - 0.6 * np.exp(-0.3 * ml))), ALU.mult, [vec_t], [vec_t])
        if pre_layer is not None:
            mk_AB('f1', nf1_d, pl, 0, 1)
            mk_gate('g2', pl, 2, 0.5)
            mk_AB('mx', nmix_d, pl, 3, 4)
            oml = sb("oml", [128, 4])
            if pl == 0:
                MS(oml[:], 1.0, [vec_t])
            else:
                lbl = sb("lbl", [128, 4, 2])
                DMA('sp', lbl[:], lbl_d, (), [vec_t], 'l')
                TT(oml[:], lbl[:, :, 0], lbl[:, :, 1], ALU.subtract, [vec_t], [vec_t])
                ACT(oml[:], oml[:], AF.Sigmoid, [vec_t], [vec_t])
        if final:
            fng = sb("fng", [128, 8])
            DMA('sp', fng[:], fnorm_d, (), [vec_t], 'l')

        def load_w8(src):
            wi, wt = wk8_r.next()
            DMA('pool', wk8[:, wi], src, (), [wt], 'w')
            return wi, wt

        def load_w22(src):
            wi, wt = wk22_r.next()
            DMA('pool', wk22[:, wi].rearrange("p a b -> p (a b)").rearrange("p (c d) -> p c d", d=704),
                src.rearrange("p a b -> p (a b)").rearrange("p (c d) -> p c d", d=704), (), [wt], 'w')
            return wi, wt

        def blocks(ts):
            bw = min(512, ts)
            return [(i * bw, bw) for i in range(ts // bw)]

        def sumsq_rstd(srcs, bw, grp_lhsT, grp_t, nfeat, use_ln=True):
            bk, bt = ps_r.next()
            nk = len(srcs)
            for k, (sap, stl) in enumerate(srcs):
                si, s_t = sqb_r.next()
                ACT(sqb[:, si, :bw], sap, AF.Square, [stl], [s_t])
                MM(ps[:, bk, :bw], grp_lhsT, sqb[:, si, :bw], k == 0, k == nk - 1, [s_t, grp_t], [bt])
            ri, r_t = rstd_r.next()
            if use_ln:
                ACT(rstd[:, ri, :bw], ps[:, bk, :bw], AF.Ln, [bt, vec_t], [r_t], scale=1.0 / nfeat, bias=epsb[:, 0:1])
                ACT(rstd[:, ri, :bw], rstd[:, ri, :bw], AF.Exp, [r_t], [r_t], scale=-0.5)
            else:
                ACT(rstd[:, ri, :bw], ps[:, bk, :bw], AF.Sqrt, [bt, vec_t], [r_t], scale=1.0 / nfeat, bias=epsb[:, 0:1])
                RCP(rstd[:, ri, :bw], rstd[:, ri, :bw], [r_t], [r_t])
            return ri, r_t

        def norm_mod(ts, A, B, ci):
            for bi, (t0, bw) in enumerate(blocks(ts)):
                ri, r_t = sumsq_rstd([(hT[:, k, t0:t0 + bw], hT_t[k][bi]) for k in range(8)], bw,
                                     ones_b[:], ones_t, D)
                for k in range(8):
                    ti, t_t = tmpf_r.next()
                    TT(tmpf[:, ti, :bw], hT[:, k, t0:t0 + bw], rstd[:, ri, :bw], ALU.mult, [hT_t[k][bi], r_t], [t_t])
                    ACT(xn[:, k, t0:t0 + bw], tmpf[:, ti, :bw], AF.Identity, [t_t, vec_t], [xn_t[k][bi]],
                        scale=A[:, k, ci:ci + 1], bias=B[:, k, ci:ci + 1])

        def ffn(ts, w1_d, w3_d, w2_d, G, ci):
            blks = blocks(ts)
            for j in range(22):
                w1i, w1t = load_w8(w1_d[j])
                w3i, w3t = load_w8(w3_d[j])
                for bi, (t0, bw) in enumerate(blks):
                    bka, bta = ps_r.next()
                    bkb, btb = ps_r.next()
                    for kc in range(8):
                        MM(ps[:, bka, :bw], wk8[:, w1i, kc, :], xn[:, kc, t0:t0 + bw], kc == 0, kc == 7,
                           [w1t, xn_t[kc][bi]], [bta])
                    for kc in range(8):
                        MM(ps[:, bkb, :bw], wk8[:, w3i, kc, :], xn[:, kc, t0:t0 + bw], kc == 0, kc == 7,
                           [w3t, xn_t[kc][bi]], [btb])
                    ti, t_t = tmpf_r.next()
                    ACT(tmpf[:, ti, :bw], ps[:, bka, :bw], AF.Silu, [bta], [t_t])
                    TT(hid[:, j, t0:t0 + bw], tmpf[:, ti, :bw], ps[:, bkb, :bw], ALU.mult, [t_t, btb], [hid_t[j][bi]])
            for i in range(8):
                w2i, w2t = load_w22(w2_d[i])
                for bi, (t0, bw) in enumerate(blks):
                    bk, bt = ps_r.next()
                    for j in range(22):
                        MM(ps[:, bk, :bw], wk22[:, w2i, j, :], hid[:, j, t0:t0 + bw], j == 0, j == 21,
                           [w2t, hid_t[j][bi]], [bt])
                    STT(hT[:, i, t0:t0 + bw], ps[:, bk, :bw], G[:, i, ci:ci + 1], hT[:, i, t0:t0 + bw],
                        ALU.mult, ALU.add, [bt, hT_t[i][bi], vec_t], [hT_t[i][bi]])

        def store(dram_ap, sbuf_ap, tl):
            DMA('sp', dram_ap, sbuf_ap, [tl], (), 's')

        if merge_layer is not None:
            oda = sb("oda", [128, 4, TS_], BF16); oda_t = Tl()
            opl = sb("opl", [128, 2, TS_], BF16); opl_t = Tl()
            ohg = sb("ohg", [128, 2, TS_], BF16); ohg_t = [[Tl() for _ in range(2)] for _ in range(2)]
            hgin = sb("hgin", [128, 2, 3, 512]); hgin_r = Ring(2)
            gts = sb("gts", [128, 2, 3, 512]); gts_r = Ring(2)
        if pre_layer is not None:
            cst = sb("cst", [128, 2, TS_]); cst_t = Tl()

        tiles = [(i * 1024, 1024, 0) for i in range(4)]
        if with_ctx:
            tiles.append((TLAT, TCTX, 1))
        if tiles_override is not None:
            tiles = tiles_override

        for (T0, ts, ci) in tiles:
            blks = blocks(ts)
            for k in range(8):
                DMA('act', hT[:, k, :ts], hT_in[k * 128:(k + 1) * 128, T0:T0 + ts], (), hT_t[k], 'l')
            if merge_layer is not None:
                DMA('act', oda[:, :, :ts], odaT_d[:, T0:T0 + ts].rearrange("(c p) t -> p c t", p=128), (), [oda_t], 'l')
                DMA('act', opl[:, :, :ts], opoolT_d[:, T0:T0 + ts].rearrange("(c p) t -> p c t", p=128), (), [opl_t], 'l')
                for c in range(4):
                    for bi, (t0, bw) in enumerate(blks):
                        oc = oda[:, c, t0:t0 + bw]
                        si, s_t = sqb_r.next()
                        TT(sqb[:, si, :bw], oc, oc, ALU.mult, [oda_t], [s_t])
                        bk, bt = ps_r.next()
                        MM(ps[:, bk, :bw], ones_b[:], sqb[:, si, :bw], True, True, [s_t, ones_t], [bt])
                        ri, r_t = rstd_r.next()
                        ACT(rstd[:, ri, :bw], ps[:, bk, :bw], AF.Ln, [bt, vec_t], [r_t], scale=1.0 / 128, bias=epsb[:, 0:1])
                        ACT(rstd[:, ri, :bw], rstd[:, ri, :bw], AF.Exp, [r_t], [r_t], scale=-0.5)
                        STT(oc, oc, dagn[:, 0:1], rstd[:, ri, :bw], ALU.mult, ALU.mult, [oda_t, r_t, vec_t], [oda_t])
                for c2 in range(2):
                    for bi, (t0, bw) in enumerate(blks):
                        hi_, h_t = hgin_r.next()
                        for s_i, src in enumerate((ohfT_d, ohbT_d, sgT_in_d)):
                            DMA('act', hgin[:, hi_, s_i, :bw], src[c2 * 128:(c2 + 1) * 128, T0 + t0:T0 + t0 + bw],
                                (), [h_t], 'l')
                        o0 = hgin[:, hi_, 0, :bw]
                        TT(o0, o0, hgin[:, hi_, 1, :bw], ALU.add, [h_t], [h_t])
                        ri, r_t = sumsq_rstd([(o0, h_t)], bw, bd64_b[:], bd64_t, 64, use_ln=False)
                        TT(o0, o0, rstd[:, ri, :bw], ALU.mult, [h_t, r_t], [h_t])
                        STT(ohg[:, c2, t0:t0 + bw], o0, hgn[:, 0:1], hgin[:, hi_, 2, :bw], ALU.mult, ALU.mult,
                            [h_t, vec_t], [ohg_t[c2][bi]])
                gview = gateT_in_d.rearrange("(b c p) t -> c p b t", b=3, p=128)
                for i in range(8):
                    wi, wt = load_w8(wmrg_d[i])
                    for bi, (t0, bw) in enumerate(blks):
                        gi, g_t = gts_r.next()
                        DMA('act', gts[:, gi, :, :bw], gview[i][:, :, T0 + t0:T0 + t0 + bw], (), [g_t], 'l')
                        bka, bta = ps_r.next()
                        bkh, bth = ps_r.next()
                        bkp, btp = ps_r.next()
                        for c in range(4):
                            MM(ps[:, bka, :bw], wk8[:, wi, c, :], oda[:, c, t0:t0 + bw], c == 0, c == 3, [wt, oda_t], [bta])
                        for c in range(2):
                            MM(ps[:, bkh, :bw], wk8[:, wi, 4 + c, :], ohg[:, c, t0:t0 + bw], c == 0, c == 1,
                               [wt, ohg_t[c][bi]], [bth])
                        for c in range(2):
                            MM(ps[:, bkp, :bw], wk8[:, wi, 6 + c, :], opl[:, c, t0:t0 + bw], c == 0, c == 1, [wt, opl_t], [btp])
                        t1, t1t = tmpf_r.next()
                        t2, t2t = tmpf_r.next()
                        TT(tmpf[:, t1, :bw], gts[:, gi, 0, :bw], ps[:, bka, :bw], ALU.mult, [g_t, bta], [t1t])
                        TT(tmpf[:, t2, :bw], gts[:, gi, 1, :bw], ps[:, bkh, :bw], ALU.mult, [g_t, bth], [t2t])
                        TT(tmpf[:, t1, :bw], tmpf[:, t1, :bw], tmpf[:, t2, :bw], ALU.add, [t1t, t2t], [t1t])
                        TT(tmpf[:, t2, :bw], gts[:, gi, 2, :bw], ps[:, bkp, :bw], ALU.mult, [g_t, btp, t2t], [t2t])
                        TT(xn[:, i, t0:t0 + bw], tmpf[:, t1, :bw], tmpf[:, t2, :bw], ALU.add, [t1t, t2t], [xn_t[i][bi]])
                G5 = vecs['g5']
                for i in range(8):
                    wi, wt = load_w8(wout_d[i])
                    for bi, (t0, bw) in enumerate(blks):
                        bk, bt = ps_r.next()
                        for kc in range(8):
                            MM(ps[:, bk, :bw], wk8[:, wi, kc, :], xn[:, kc, t0:t0 + bw], kc == 0, kc == 7,
                               [wt, xn_t[kc][bi]], [bt])
                        STT(hT[:, i, t0:t0 + bw], ps[:, bk, :bw], G5[:, i, ci:ci + 1], hT[:, i, t0:t0 + bw],
                            ALU.mult, ALU.add, [bt, hT_t[i][bi], vec_t], [hT_t[i][bi]])
                norm_mod(ts, vecs['f2A'], vecs['f2B'], ci)
                ffn(ts, f2w1_d, f2w3_d, f2w2_d, vecs['g8'], ci)
            if final:
                for bi, (t0, bw) in enumerate(blks):
                    ri, r_t = sumsq_rstd([(hT[:, k, t0:t0 + bw], hT_t[k][bi]) for k in range(8)], bw,
                                         ones_b[:], ones_t, D)
                    for k in range(8):
                        si, s_t = stgf_r.next()
                        STT(stgf[:, si, :bw], hT[:, k, t0:t0 + bw], fng[:, k:k + 1], rstd[:, ri, :bw],
                            ALU.mult, ALU.mult, [hT_t[k][bi], r_t, vec_t], [s_t])
                        store(outT_d[k * 128:(k + 1) * 128, T0 + t0:T0 + t0 + bw], stgf[:, si, :bw], s_t)
            if pre_layer is not None:
                if dbg >= 1:
                    norm_mod(ts, vecs['f1A'], vecs['f1B'], ci)
                if dbg >= 2:
                    ffn(ts, f1w1_d, f1w3_d, f1w2_d, vecs['g2'], ci)
                for k in range(8):
                    DMA('sp', hT_out[k * 128:(k + 1) * 128, T0:T0 + ts], hT[:, k, :ts], hT_t[k], (), 's')
                if dbg < 3:
                    continue
                norm_mod(ts, vecs['mxA'], vecs['mxB'], ci)
                DMA('act', cst[:, 0, :ts], cosT_d[:, T0:T0 + ts], (), [cst_t], 'l')
                DMA('act', cst[:, 1, :ts], sinT_d[:, T0:T0 + ts], (), [cst_t], 'l')
                for c in range(48):
                    wi, wt = load_w8(win_d[c])
                    for bi, (t0, bw) in enumerate(blks):
                        bk, bt = ps_r.next()
                        for kc in range(8):
                            MM(ps[:, bk, :bw], wk8[:, wi, kc, :], xn[:, kc, t0:t0 + bw], kc == 0, kc == 7,
                               [wt, xn_t[kc][bi]], [bt])
                        zin = ps[:, bk, :bw]
                        tok = slice(T0 + t0, T0 + t0 + bw)
                        r2 = slice((c % 2) * 128, (c % 2 + 1) * 128)
                        if c < 8:
                            dst = (qT_d if c < 4 else kT_d)[(c % 4) * 128:(c % 4 + 1) * 128, tok]
                            si, s_t = sqb_r.next()
                            ACT(sqb[:, si, :bw], zin, AF.Copy, [bt], [s_t])
                            bk2, bt2 = ps_r.next()
                            MM(ps[:, bk2, :bw], rperm_b[:], sqb[:, si, :bw], True, True, [s_t, rperm_t], [bt2])
                            t1, t1t = tmpf_r.next()
                            t2, t2t = tmpf_r.next()
                            TT(tmpf[:, t1, :bw], zin, cst[:, 0, t0:t0 + bw], ALU.mult, [bt, cst_t], [t1t])
                            TT(tmpf[:, t2, :bw], ps[:, bk2, :bw], cst[:, 1, t0:t0 + bw], ALU.mult, [bt2, cst_t], [t2t])
                            oi, o_t = stgb_r.next()
                            TT(stgb[:, oi, :bw], tmpf[:, t1, :bw], tmpf[:, t2, :bw], ALU.add, [t1t, t2t], [o_t])
                            store(dst, stgb[:, oi, :bw], o_t)
                        elif c < 12 or 18 <= c < 20 or 22 <= c < 24:
                            if c < 12:
                                dst = vT_d[(c - 8) * 128:(c - 7) * 128, tok]
                            elif c < 20:
                                dst = hiT_d[r2, tok]
                            else:
                                dst = zpT_d[r2, tok]
                            oi, o_t = stgb_r.next()
                            CP(stgb[:, oi, :bw], zin, [bt], [o_t])
                            store(dst, stgb[:, oi, :bw], o_t)
                        elif c < 14 or 20 <= c < 22:
                            dst = (hqT_d if c < 14 else sgT_d)[r2, tok]
                            oi, o_t = stgf_r.next()
                            ACT(stgf[:, oi, :bw], zin, AF.Silu, [bt], [o_t])
                            store(dst, stgf[:, oi, :bw], o_t)
                        elif c < 18:
                            di = (c - 14) // 2
                            col = c - 14
                            kd = (kkfT_d, kkbT_d)[di][r2, tok]
                            ld = (lffT_d, lfbT_d)[di][r2, tok]
                            oi, o_t = stgf_r.next()
                            ACT(stgf[:, oi, :bw], zin, AF.Sigmoid, [bt], [o_t], scale=-1.0)
                            TS(stgf[:, oi, :bw], stgf[:, oi, :bw], oml[:, col:col + 1], ALU.mult, [o_t, vec_t], [o_t])
                            store(kd, stgf[:, oi, :bw], o_t)
                            o2, o2_t = stgf_r.next()
                            ACT(stgf[:, o2, :bw], stgf[:, oi, :bw], AF.Ln, [o_t, vec_t], [o2_t], scale=-1.0,
                                bias=oneb[:, 0:1])
                            store(ld, stgf[:, o2, :bw], o2_t)
                        else:
                            dst = gateT_d[(c - 24) * 128:(c - 23) * 128, tok]
                            oi, o_t = stgf_r.next()
                            ACT(stgf[:, oi, :bw], zin, AF.Sigmoid, [bt], [o_t])
                            store(dst, stgf[:, oi, :bw], o_t)
        P.emit(nc)
    return nc


def host_consts():
    ones = np.ones((128, 128), np.float32)
    bd64 = np.zeros((128, 128), np.float32)
    bd64[:64, :64] = 1
    bd64[64:, 64:] = 1
    R = np.zeros((128, 128), np.float32)
    for blk in (0, 64):
        for j in range(16):
            R[blk + j, blk + 16 + j] = -1
            R[blk + 16 + j, blk + j] = 1
            R[blk + 32 + j, blk + 48 + j] = -1
            R[blk + 48 + j, blk + 32 + j] = 1
    return ones, bd64, np.ascontiguousarray(R.T)


def rope_tables():
    t = np.arange(SEQ)
    row = (t // 64).astype(np.float32)
    col = (t % 64).astype(np.float32)
    inv = (np.float32(10000.0) ** (-np.arange(0, 32, 2, dtype=np.float32) / np.float32(32))).astype(np.float32)
    ar = (row[:, None] * inv[None]).astype(np.float32)
    ac = (col[:, None] * inv[None]).astype(np.float32)
    cos64 = np.concatenate([np.cos(ar), np.cos(ar), np.cos(ac), np.cos(ac)], 1).astype(np.float32)
    sin64 = np.concatenate([np.sin(ar), np.sin(ar), np.sin(ac), np.sin(ac)], 1).astype(np.float32)
    return np.tile(cos64.T, (2, 1)), np.tile(sin64.T, (2, 1))


def core_bq(c):
    return c // 4, c % 4


class HostW:
    def __init__(self, inp):
        self.inp = inp
        self.ones, self.bd64, self.rperm = host_consts()
        self.cosT, self.sinT = rope_tables()
        self.cache = {}

    def get(self, key, fn):
        if key not in self.cache:
            self.cache[key] = fn()
        return self.cache[key]

    def common(self, c, layers):
        inp = self.inp
        b, q = core_bq(c)
        d = dict(ones=self.ones, bd64=self.bd64, rperm=self.rperm)
        d['cs'] = np.ascontiguousarray(np.stack([vec_p(inp['c'][b]), vec_p(inp['c_ctx'])], axis=2))
        for l in layers:
            d[f'wada{l}'] = self.get(('wada', l), lambda: arr_w(inp['w_ada'][l]))
            d[f'bada{l}'] = self.get(('bada', l), lambda: vec_p(inp['b_ada'][l]))
        return d

    def pre(self, c, l, with_ctx=True):
        inp = self.inp
        b, q = core_bq(c)
        d = {}
        d['nf1'] = vec_p(inp['norm_ffn1'][l])
        d['nmix'] = vec_p(inp['norm_mix'][l])
        d['f1w1'] = self.get(('f1w1', l), lambda: arr_w(inp['ffn1_w1'][l]))
        d['f1w3'] = self.get(('f1w3', l), lambda: arr_w(inp['ffn1_w3'][l]))
        d['f1w2'] = self.get(('f1w2', l), lambda: arr_w(inp['ffn1_w2'][l]))
        d['win'] = self.get(('win', l), lambda: arr_w(inp['w_in'][l]))
        lg = inp['hg_lb_logits']
        lbl = np.zeros((128, 4, 2), np.float32)
        for di in range(2):
            for ch in range(2):
                for dep in range(2):
                    lbl[:, di * 2 + ch, dep] = lg[dep, di, ch * 128:(ch + 1) * 128]
        d['lbl'] = lbl
        cosT = self.cosT[:, q * TLAT:(q + 1) * TLAT]
        sinT = self.sinT[:, q * TLAT:(q + 1) * TLAT]
        if with_ctx:
            cosT = np.concatenate([cosT, np.ones((128, TCTX), np.float32)], 1)
            sinT = np.concatenate([sinT, np.zeros((128, TCTX), np.float32)], 1)
        d['cosT'] = np.ascontiguousarray(cosT)
        d['sinT'] = np.ascontiguousarray(sinT)
        return d

    def mrg(self, c, l):
        inp = self.inp
        d = {}
        d['nf2'] = vec_p(inp['norm_ffn2'][l])
        d['f2w1'] = self.get(('f2w1', l), lambda: arr_w(inp['ffn2_w1'][l]))
        d['f2w3'] = self.get(('f2w3', l), lambda: arr_w(inp['ffn2_w3'][l]))
        d['f2w2'] = self.get(('f2w2', l), lambda: arr_w(inp['ffn2_w2'][l]))
        d['wmrg'] = self.get(('wmrg', l), lambda: arr_w(np.concatenate(
            [inp['w_proj_da'][l], inp['w_proj_hg'][l], inp['w_proj_pool'][l]], 0)))
        d['wout'] = self.get(('wout', l), lambda: arr_w(inp['w_out'][l]))
        d['hgn'] = np.ascontiguousarray(np.tile(inp['hg_norm'][l], 2)[:, None])
        d['dag'] = np.ascontiguousarray(inp['da_subln'][l][:, None])
        return d


def initial_hT(inp, c):
    b, q = core_bq(c)
    xs = inp['x'][b, q * TLAT:(q + 1) * TLAT]
    cx = inp['ctx'][b, q * TCTX:(q + 1) * TCTX]
    return np.ascontiguousarray(np.concatenate([xs, cx], 0).T)


NQC = SEQ + CTX


def build_A(nq_tiles=32, nkb=130, with_ctxq=True):
    import contextlib
    nc = bass.Bass("TRN2", target_bir_lowering=False)
    P = Prog()
    O = Ops(P)
    MM, ACT, TT, TS, STT, CP, RCP, MS, DMA = O.MM, O.ACT, O.TT, O.TS, O.STT, O.CP, O.RCP, O.MS, O.DMA

    def din(name, shape, dt=F32):
        return nc.dram_tensor(name, list(shape), dt, kind="ExternalInput").ap()

    qT_d = din("qT", [128, NQC], BF16)
    kT_d = din("kT", [128, NKEY], BF16)
    v_d = din("v", [128, 130, 128], BF16)
    lamv_d = din("lamv", [128, 4, 64])
    lami_d = din("lami", [128, 2])
    gain_d = din("gain", [128, 1])
    ones_d = din("ones", [128, 128])
    oT_d = nc.dram_tensor("oT", [128, NQC], BF16, kind="ExternalOutput").ap()

    with contextlib.ExitStack() as st:
        def sb(name, shape, dt=F32):
            return st.enter_context(nc.sbuf_tensor("s_" + name, list(shape), dt))

        qT = sb("qT", [128, NQC], BF16)
        kT = sb("kT", [128, NKEY], BF16); kT_t = Tl(const=True)
        v = sb("v", [128, 130, 128], BF16); v_t = Tl(const=True)
        NPB = 12
        pb = sb("pb", [128, NPB, 2, 512], BF16); pb_r = Ring(NPB)
        tq = sb("tq", [128, 4, 2, 512], BF16); tq_r = Ring(4)
        acc = sb("acc", [128, 2, 512]); acc_t = Tl()
        fin = sb("fin", [128, 4, 512]); fin_r = Ring(4)
        ocp = sb("ocp", [128, 2, 2, 512]); oc_r = Ring(2)
        sqb = sb("sqb", [128, 512], BF16); sqb_t = Tl()
        ob = sb("ob", [128, 2, 512], BF16); ob_r = Ring(2)
        ones_f = sb("ones_f", [128, 128]); ones_b = sb("ones_b", [128, 128], BF16); c_t = Tl(const=True)
        lamv = sb("lamv", [128, 4, 64]); lami = sb("lami", [128, 2]); gain = sb("gain", [128, 1])
        lw = sb("lw", [128, 2, 64]); ls = sb("ls", [128, 2]); neglam = sb("neglam", [128, 1]); gsc = sb("gsc", [128, 1])
        epsb = sb("epsb", [128, 1])
        ps = st.enter_context(nc.psum_tensor("ps", [128, 8, 512], F32))
        s_r = Ring(6, excl=True)
        o_t = [Tl(excl=True), Tl(excl=True)]

        MS(epsb[:], EPS, [c_t])
        DMA('sp', ones_f[:], ones_d, (), [c_t], 'l')
        onesb_t = Tl(const=True)
        DMA('pool', ones_b[:], ones_d, (), [onesb_t], 'l')
        DMA('sp', lamv[:], lamv_d, (), [c_t], 'l')
        DMA('sp', lami[:], lami_d, (), [c_t], 'l')
        DMA('sp', gain[:], gain_d, (), [c_t], 'l')
        TT(lw[:], lamv[:, 0:4:2, :], lamv[:, 1:4:2, :], ALU.mult, [c_t], [c_t])
        P.op('dve', 'tensor_reduce', dict(out=ls[:], in_=lw[:], axis=mybir.AxisListType.X, op=ALU.add), [c_t], [c_t])
        ACT(ls[:], ls[:], AF.Exp, [c_t], [c_t])
        TT(neglam[:], ls[:, 0:1], ls[:, 1:2], ALU.subtract, [c_t], [c_t])
        STT(neglam[:], neglam[:], -1.0, lami[:, 0:1], ALU.mult, ALU.subtract, [c_t], [c_t])
        TT(gsc[:], gain[:], lami[:, 1:2], ALU.mult, [c_t], [c_t])

        for i in range(0, NKEY, 2080):
            DMA('sp', kT[:, i:i + 2080], kT_d[:, i:i + 2080], (), [kT_t], 'l')
        for i in range(0, 130, 13):
            DMA('sp', v[:, i:i + 13, :], v_d[:, i:i + 13, :], (), [v_t], 'l')

        qtiles = [(i * 512, 512, 0, nkb) for i in range(nq_tiles)]
        if with_ctxq:
            qtiles.append((SEQ, CTX, 128, 130))
        q_t = [Tl() for _ in qtiles]
        for qi, (q0, qw, kb0, kb1) in enumerate(qtiles):
            DMA('sp', qT[:, q0:q0 + qw], qT_d[:, q0:q0 + qw], (), [q_t[qi]], 'l')

        blocks = [(qi, kb) for qi, (q0, qw, kb0, kb1) in enumerate(qtiles) for kb in range(kb0, kb1)]
        LA = 2
        sbanks = {}

        def emit_qk(bi):
            qi, kb = blocks[bi]
            q0, qw, kb0, kb1 = qtiles[qi]
            banks = [s_r.next(), s_r.next()]
            sbanks[bi] = banks
            for c in range(2):
                bk, bt = banks[c]
                MM(ps[:, bk, :qw], kT[c * 64:(c + 1) * 64, kb * 128:(kb + 1) * 128],
                   qT[c * 64:(c + 1) * 64, q0:q0 + qw], True, True, [kT_t, q_t[qi]], [bt])

        pend = []
        accst = {'first': True, 'prev': None}

        def accumulate(xap, x_t, qw):
            if accst['first']:
                CP(acc[:, :, :qw], xap, [x_t], [acc_t])
                accst['first'] = False
            else:
                TT(acc[:, :, :qw], acc[:, :, :qw], xap, ALU.add, [x_t, acc_t], [acc_t])

        def emit_rest(bi):
            qi, kb = blocks[bi]
            q0, qw, kb0, kb1 = qtiles[qi]
            first = kb == kb0
            last = kb == kb1 - 1
            banks = sbanks.pop(bi)
            (bk0, bt0), (bk1, bt1) = banks
            assert bk1 == bk0 + 1
            pi, p_t = pb_r.next()
            ACT(pb[:, pi, :, :qw], ps[:, bk0:bk0 + 2, :qw], AF.Exp, [bt0, bt1], [p_t], scale=0.125)
            for c in range(2):
                MM(ps[:, 6 + c, :qw], v[:, kb, :], pb[:, pi, c, :qw], first, last, [v_t, p_t], [o_t[c]])
            if first:
                accst['first'] = True
                accst['prev'] = None
            if accst['prev'] is None and not last:
                accst['prev'] = (pi, p_t)
            else:
                if accst['prev'] is None:
                    pend.append((pb[:, pi, :, :qw], p_t))
                else:
                    ppi, pp_t = accst['prev']
                    accst['prev'] = None
                    ti, t_t = tq_r.next()
                    TT(tq[:, ti, :, :qw], pb[:, ppi, :, :qw], pb[:, pi, :, :qw], ALU.add, [pp_t, p_t], [t_t])
                    pend.append((tq[:, ti, :, :qw], t_t))
                if len(pend) == 2:
                    (xa, xa_t), (xb_, xb_t) = pend
                    TT(xa, xa, xb_, ALU.add, [xa_t, xb_t], [xa_t])
                    accumulate(xa, xa_t, qw)
                    del pend[:]
                if last and pend:
                    accumulate(pend[0][0], pend[0][1], qw)
                    del pend[:]
            if not last:
                return
            oci, oc_t = oc_r.next()
            for c in range(2):
                ACT(ocp[:, oci, c, :qw], ps[:, 6 + c, :qw], AF.Copy, [o_t[c]], [oc_t])
            ts_ = []
            for c in range(2):
                bk, bt = s_r.next()
                MM(ps[:, bk, :qw], ones_f[:], acc[:, c, :qw], True, True, [c_t, acc_t], [bt])
                fi, f_t = fin_r.next()
                RCP(fin[:, fi, :qw], ps[:, bk, :qw], [bt], [f_t])
                TT(fin[:, fi, :qw], ocp[:, oci, c, :qw], fin[:, fi, :qw], ALU.mult, [oc_t, f_t], [f_t])
                ts_.append((fi, f_t))
            (f0, f0t), (f1, f1t) = ts_
            oi, ob_t = ob_r.next()
            STT(ob[:, oi, :qw], fin[:, f1, :qw], neglam[:, 0:1], fin[:, f0, :qw], ALU.mult, ALU.add,
                [f0t, f1t, c_t], [ob_t])
            DMA('sp', oT_d[:, q0:q0 + qw], ob[:, oi, :qw], [ob_t], (), 's')

        base = 0
        for qi, (q0, qw, kb0, kb1) in enumerate(qtiles):
            n = kb1 - kb0
            if s_r.i % 2:
                s_r.next()
            for j in range(n + LA):
                if j < n:
                    emit_qk(base + j)
                if j - LA >= 0:
                    emit_rest(base + j - LA)
            base += n
        P.emit(nc)
    return nc


NCH = NKEY // 64


def build_H(groups=None, pool_tiles=128, do_pool=True):
    import contextlib
    nc = bass.Bass("TRN2", target_bir_lowering=False)
    P = Prog()
    O = Ops(P)
    MM, TR, ACT, TT, TS, STT, CP, RCP, MS, DMA = O.MM, O.TR, O.ACT, O.TT, O.TS, O.STT, O.CP, O.RCP, O.MS, O.DMA
    if groups is None:
        groups = [(0, 4)] + [(4 + 8 * i, 8) for i in range(32)]

    def din(name, shape, dt=F32):
        return nc.dram_tensor(name, list(shape), dt, kind="ExternalInput").ap()

    hq_d = [din(f"hq{s}", [64, NKEY]) for s in range(2)]
    kk_d = [din(f"kk{s}", [64, NKEY]) for s in range(2)]
    lf_d = [din(f"lf{s}", [64, NKEY]) for s in range(2)]
    vt_d = [din(f"vt{s}", [64, NCH, 64], BF16) for s in range(2)]
    reset_d = din("reset", [64, 512])
    mask_d = din("mask", [64, 64])
    ident_d = din("ident", [64, 64])
    oT_d = [nc.dram_tensor(f"oT{s}", [64, NKEY], F32, kind="ExternalOutput").ap() for s in range(2)]
    if do_pool:
        zp_d = din("zp", [128, 130, 64], BF16)
        band_d = din("band", [5, 128, 128])
        pw_d = din("pw", [64, 64])
        psc_d = din("psc", [64, 1])
        opT_d = nc.dram_tensor("opT", [64, NKEY], BF16, kind="ExternalOutput").ap()

    with contextlib.ExitStack() as st:
        def sb(name, shape, dt=F32):
            return st.enter_context(nc.sbuf_tensor("s_" + name, list(shape), dt))

        reset = sb("reset", [64, 512]); mask = sb("mask", [64, 64]); c_t = Tl(const=True)
        ident = sb("ident", [64, 64], BF16); cb_t = Tl(const=True)
        DMA('sp', reset[:], reset_d, (), [c_t], 'l')
        DMA('sp', mask[:], mask_d, (), [c_t], 'l')
        DMA('pool', ident[:], ident_d, (), [cb_t], 'l')
        ps = st.enter_context(nc.psum_tensor("ps", [128, 8, 512], F32))
        sc_t = Tl(excl=True)
        kt_t = Tl(excl=True)
        ot_t = [Tl(excl=True), Tl(excl=True)]
        kv_t = [Tl(excl=True), Tl(excl=True)]
        kt_bf = ps[:, 1, :].bitcast(BF16)

        S = []
        for s in range(2):
            d = {}
            d['gin'] = sb(f"gin{s}", [64, 2, 3, 512]); d['gin_r'] = Ring(2)
            d['a'] = sb(f"a{s}", [64, 2, 512]); d['a_r'] = Ring(2)
            d['e1'] = sb(f"e1{s}", [64, 2, 512]); d['e1_r'] = Ring(2)
            d['e2'] = sb(f"e2{s}", [64, 2, 512]); d['e2_r'] = Ring(2)
            d['qp'] = sb(f"qp{s}", [64, 2, 512], BF16); d['qp_r'] = Ring(2)
            d['kp'] = sb(f"kp{s}", [64, 2, 512], BF16); d['kp_r'] = Ring(2)
            d['sm'] = sb(f"sm{s}", [64, 2, 512], BF16); d['sm_r'] = Ring(2)
            d['ktok'] = sb(f"ktok{s}", [64, 2, 512], BF16); d['ktok_r'] = Ring(2)
            d['sc1'] = sb(f"sc1{s}", [64, 2, 8]); d['sc1_r'] = Ring(2)
            d['ser'] = sb(f"ser{s}", [64, 2, 8]); d['ser_r'] = Ring(2)
            d['v'] = sb(f"v{s}", [64, NCH, 64], BF16); d['v_t'] = Tl(const=True)
            d['state'] = sb(f"state{s}", [64, 2, 64]); d['state_t'] = [Tl(), Tl()]; d['sp'] = 0
            d['sr'] = sb(f"sr{s}", [64, 64], BF16); d['sr_t'] = Tl()
            d['ost'] = sb(f"ost{s}", [64, 2, 512]); d['ost_r'] = Ring(2)
            MS(d['state'][:], 0.0, d['state_t'])
            for t_ in d['sm_r'].t:
                pass
            MS(d['sm'][:], 0.0, d['sm_r'].t)
            for i in range(0, NCH, 52):
                DMA('sp', d['v'][:, i:i + 52, :], vt_d[s][:, i:i + 52, :], (), [d['v_t']], 'l')
            S.append(d)

        def pre(s, g, out):
            d = S[s]
            c0, n = groups[g]
            W = 64 * n
            t0 = 64 * c0
            gi, g_t = d['gin_r'].next()
            for j, src in enumerate((hq_d[s], kk_d[s], lf_d[s])):
                DMA('sp', d['gin'][:, gi, j, :W], src[:, t0:t0 + W], (), [g_t], 'l')
            yield
            ai, a_t = d['a_r'].next()
            a = d['a'][:, ai, :W]
            P.op('dve', 'tensor_tensor_scan', dict(out=a, data0=reset[:, :W], data1=d['gin'][:, gi, 2, :W], initial=0.0,
                                                   op0=ALU.mult, op1=ALU.add), [g_t, c_t], [a_t])
            a3 = a.rearrange("p (c t) -> p c t", t=64)
            yield
            s1i, s1_t = d['sc1_r'].next()
            eri, er_t = d['ser_r'].next()
            ACT(d['sc1'][:, s1i, :n], a3[:, :, 63], AF.Exp, [a_t], [s1_t])
            ACT(d['ser'][:, eri, :n], a3[:, :, 31], AF.Exp, [a_t], [er_t])
            e2i, e2_t = d['e2_r'].next()
            dd = d['e2'][:, e2i, :W]
            TT(dd.rearrange("p (c t) -> p c t", t=64), a3, a3[:, :, 31:32].to_broadcast([64, n, 64]), ALU.subtract,
               [a_t], [e2_t])
            yield
            e1i, e1_t = d['e1_r'].next()
            e1 = d['e1'][:, e1i, :W]
            ACT(e1, dd, AF.Exp, [e2_t], [e1_t])
            ACT(dd, dd, AF.Exp, [e2_t], [e2_t], scale=-1.0)
            yield
            qi, q_t = d['qp_r'].next()
            ki, k_t = d['kp_r'].next()
            TT(d['qp'][:, qi, :W], d['gin'][:, gi, 0, :W], e1, ALU.mult, [g_t, e1_t], [q_t])
            TT(d['kp'][:, ki, :W], d['gin'][:, gi, 1, :W], dd, ALU.mult, [g_t, e2_t], [k_t])
            yield
            for c in range(n):
                MM(ps[0:64, 0, c * 64 + 32:(c + 1) * 64], d['kp'][:, ki, c * 64:(c + 1) * 64],
                   d['qp'][:, qi, c * 64 + 32:(c + 1) * 64], True, True, [k_t, q_t], [sc_t])
                MM(ps[0:32, 0, c * 64:c * 64 + 32], d['kp'][:, ki, c * 64:c * 64 + 32],
                   d['qp'][:, qi, c * 64:c * 64 + 32], True, True, [k_t, q_t], [sc_t])
            for c in range(n):
                TR(kt_bf[0:64, c * 64:(c + 1) * 64], d['kp'][:, ki, c * 64:(c + 1) * 64], ident[:], [k_t, cb_t], [kt_t])
            smi, sm_t = d['sm_r'].next()
            sm3 = d['sm'][:, smi, :W].rearrange("p (c t) -> p c t", t=64)
            sc3 = ps[0:64, 0, :W].rearrange("p (c t) -> p c t", t=64)
            TT(sm3[:, :, 32:64], sc3[:, :, 32:64], mask[:, 32:64].unsqueeze(1).to_broadcast([64, n, 32]), ALU.mult,
               [sc_t, c_t], [sm_t])
            TT(sm3[0:32, :, 0:32], sc3[0:32, :, 0:32], mask[0:32, 0:32].unsqueeze(1).to_broadcast([32, n, 32]),
               ALU.mult, [sc_t, c_t, sm_t], [sm_t])
            kti, kt2_t = d['ktok_r'].next()
            CP(d['ktok'][:, kti, :W], kt_bf[0:64, :W], [kt_t], [kt2_t])
            out.update(dict(c0=c0, n=n, W=W, t0=t0, qi=qi, q_t=q_t, smi=smi, sm_t=sm_t, kti=kti, kt2_t=kt2_t,
                            s1i=s1i, s1_t=s1_t, eri=eri, er_t=er_t, e1i=e1i, e1_t=e1_t))

        def step(s, pr, c):
            d = S[s]
            ch = pr['c0'] + c
            cs_ = slice(c * 64, (c + 1) * 64)
            po = d['sp']
            pn = 1 - po
            d['sp'] = pn
            st_o, st_n = d['state'][:, po, :], d['state'][:, pn, :]
            so_t, sn_t = d['state_t'][po], d['state_t'][pn]
            ACT(d['sr'][:], st_o, AF.Copy, [so_t, pr['er_t']], [d['sr_t']],
                scale=d['ser'][:, pr['eri'], c:c + 1])
            MM(ps[0:64, 2 + s, cs_], d['v'][:, ch, :], d['sm'][:, pr['smi'], cs_], True, False,
               [d['v_t'], pr['sm_t']], [ot_t[s]])
            MM(ps[0:64, 2 + s, cs_], d['sr'][:], d['qp'][:, pr['qi'], cs_], False, True,
               [d['sr_t'], pr['q_t']], [ot_t[s]])
            MM(ps[0:64, 4 + s, 0:64], d['ktok'][:, pr['kti'], cs_], d['v'][:, ch, :], True, True,
               [pr['kt2_t'], d['v_t']], [kv_t[s]])
            TS(st_n, st_o, d['sc1'][:, pr['s1i'], c:c + 1], ALU.mult, [so_t, pr['s1_t']], [sn_t])
            STT(st_n, ps[0:64, 4 + s, 0:64], d['e1'][:, pr['e1i'], c * 64 + 63:c * 64 + 64], st_n,
                ALU.mult, ALU.add, [kv_t[s], pr['e1_t'], sn_t], [sn_t])

        def fin(s, pr):
            d = S[s]
            W = pr['W']
            oi, o_t = d['ost_r'].next()
            ACT(d['ost'][:, oi, :W], ps[0:64, 2 + s, :W], AF.Copy, [ot_t[s]], [o_t])
            DMA('pool', oT_d[s][:, pr['t0']:pr['t0'] + W], d['ost'][:, oi, :W], [o_t], (), 's')

        def drain(gens):
            for gg in gens:
                for _ in gg:
                    pass

        prs = [{}, {}]
        drain([pre(0, 0, prs[0]), pre(1, 0, prs[1])])
        for g in range(len(groups)):
            nxt, gens = None, []
            if g + 1 < len(groups):
                nxt = [{}, {}]
                gens = [pre(0, g + 1, nxt[0]), pre(1, g + 1, nxt[1])]
            for c in range(groups[g][1]):
                step(0, prs[0], c)
                step(1, prs[1], c)
                for gg in gens:
                    next(gg, None)
            drain(gens)
            fin(0, prs[0])
            fin(1, prs[1])
            prs = nxt

        if do_pool:
            zp = sb("zp", [128, 130, 64], BF16); zp_t = Tl(const=True)
            band = sb("band", [128, 5, 128], BF16); pw = sb("pw", [64, 64], BF16); pc_t = Tl(const=True)
            psc = sb("psc", [64, 1]); pcs_t = Tl(const=True)
            mxb = sb("mxb", [64, 2, 512], BF16); mxb_r = Ring(2)
            pob = sb("pob", [64, 2, 512], BF16); pob_r = Ring(2)
            for i in range(0, 130, 26):
                DMA('sp', zp[:, i:i + 26, :], zp_d[:, i:i + 26, :], (), [zp_t], 'l')
            for i in range(5):
                DMA('pool', band[:, i, :], band_d[i], (), [pc_t], 'l')
            DMA('pool', pw[:], pw_d, (), [pc_t], 'l')
            DMA('sp', psc[:], psc_d, (), [pcs_t], 'l')
            mx_t = Tl(excl=True)
            py_t = Tl(excl=True)
            seqs = [(0, pool_tiles, 256 // 1)] if False else None
            plan = []
            for (tb, nt, tok0) in ((0, pool_tiles, CTX), (128, 2, 0)):
                for g0 in range(0, nt, 4):
                    plan.append((tb, nt, tok0, g0, min(4, nt - g0)))
            for (tb, nt, tok0, g0, gn) in plan:
                for j in range(gn):
                    ti = g0 + j
                    terms = []
                    if ti == 0:
                        terms.append((ti, 1))
                    elif ti == nt - 1:
                        terms.append((ti, 2))
                    else:
                        terms.append((ti, 0))
                    if ti > 0:
                        terms.append((ti - 1, 3))
                    if ti < nt - 1:
                        terms.append((ti + 1, 4))
                    for k, (src, bi) in enumerate(terms):
                        MM(ps[0:64, 6, j * 128:(j + 1) * 128], zp[:, tb + src, :], band[:, bi, :], k == 0,
                           k == len(terms) - 1, [zp_t, pc_t], [mx_t])
                mi, m_t = mxb_r.next()
                CP(mxb[:, mi, :gn * 128], ps[0:64, 6, :gn * 128], [mx_t], [m_t])
                MM(ps[0:64, 7, :gn * 128], pw[:], mxb[:, mi, :gn * 128], True, True, [pc_t, m_t], [py_t])
                pi, p_t = pob_r.next()
                ACT(pob[:, pi, :gn * 128], ps[0:64, 7, :gn * 128], AF.Copy, [py_t, pcs_t], [p_t], scale=psc[:, 0:1])
                DMA('pool', opT_d[:, tok0 + g0 * 128:tok0 + (g0 + gn) * 128], pob[:, pi, :gn * 128], [p_t], (), 's')
        P.emit(nc)
    return nc


def h_consts():
    reset = np.ones((64, 512), np.float32)
    reset[:, ::64] = 0.0
    s = np.arange(64)
    mask = (s[:, None] <= s[None, :]).astype(np.float32)
    return dict(reset=reset, mask=mask, ident=np.eye(64, dtype=np.float32))


def zp_tiles(z_lat, z_ctx):
    a = z_lat.reshape(-1, 128, 64).transpose(1, 0, 2)
    b = z_ctx.reshape(-1, 128, 64).transpose(1, 0, 2)
    return np.ascontiguousarray(np.concatenate([a, b], 1))


def band_mats(w):
    h = w // 2
    n = 128 * 3
    t = np.arange(n)
    full = np.zeros((n, n), np.float64)
    for tt in range(n):
        lo, hi = tt - h, tt + h
        for ss in range(max(lo, 0), min(hi, n)):
            full[ss, tt] = 1.0 / w
    Bc = full[128:256, 128:256] - np.eye(128)
    Bp = full[0:128, 128:256]
    Bn = full[256:384, 128:256]
    first = np.zeros((128, 128))
    last = np.zeros((128, 128))
    for tt in range(128):
        lo, hi = max(tt - h, 0), tt + h
        cnt = hi - lo
        for ss in range(lo, min(hi, 128)):
            first[ss, tt] = 1.0 / cnt
        lo2, hi2 = tt - h, min(tt + h, 128)
        cnt2 = hi2 - lo2
        for ss in range(max(lo2, 0), hi2):
            last[ss, tt] = 1.0 / cnt2
    first -= np.eye(128)
    last -= np.eye(128)
    return np.ascontiguousarray(np.stack([Bc, first, last, Bp, Bn]).astype(np.float32))


_PROGS = {}


def _prog(key, fn):
    if key not in _PROGS:
        _PROGS[key] = fn()
    return _PROGS[key]


def _run(nc, maps):
    res = run_bass_kernel_spmd(nc, maps, core_ids=list(range(NCORE)))
    return [{k: np.asarray(v) for k, v in r.items()} for r in res.results]


def _gather_tok(rs, b, key, rows):
    lat = np.concatenate([rs[b * 4 + q][key][rows, :TLAT] for q in range(4)], 1)
    ctx = np.concatenate([rs[b * 4 + q][key][rows, TLAT:] for q in range(4)], 1)
    return lat, ctx


def _mixer_inputs(inp, rs, l):
    lam_init = 0.8 - 0.6 * float(np.exp(-0.3 * l))
    lamv = np.stack([inp['da_lambda_q1'][l], inp['da_lambda_k1'][l], inp['da_lambda_q2'][l], inp['da_lambda_k2'][l]])
    hc = h_consts()
    ones = np.ones((128, 128), np.float32)
    mapsA, mapsH = [], []
    for c in range(NCORE):
        b, h = core_bq(c)
        r128 = slice(h * 128, (h + 1) * 128)
        r64 = slice(h * 64, (h + 1) * 64)
        ql, qc = _gather_tok(rs, b, 'qT', r128)
        kl, kc = _gather_tok(rs, b, 'kT', r128)
        vl, vc = _gather_tok(rs, b, 'vT', r128)
        vtok = np.concatenate([vl, vc], 1).T
        dA = dict(qT=np.ascontiguousarray(np.concatenate([ql, qc], 1)),
                  kT=np.ascontiguousarray(np.concatenate([kl, kc], 1)),
                  v=np.ascontiguousarray(vtok.reshape(130, 128, 128).transpose(1, 0, 2)),
                  lamv=np.ascontiguousarray(np.broadcast_to(lamv[None], (128, 4, 64))).astype(np.float32),
                  lami=np.ascontiguousarray(np.broadcast_to(
                      np.array([lam_init, 1.0 - lam_init], np.float32)[None], (128, 2))),
                  gain=np.ascontiguousarray(inp['da_subln'][l][:, None]), ones=ones)
        mapsA.append(dA)
        dH = dict(hc)

        def scan_order(key, flip):
            lat, ctx = _gather_tok(rs, b, key, r64)
            if flip:
                lat, ctx = lat[:, ::-1], ctx[:, ::-1]
            return np.ascontiguousarray(np.concatenate([ctx, lat], 1))

        for s, (kkey, lkey) in enumerate((('kkfT', 'lffT'), ('kkbT', 'lfbT'))):
            dH[f'hq{s}'] = scan_order('hqT', s == 1)
            dH[f'kk{s}'] = scan_order(kkey, s == 1)
            dH[f'lf{s}'] = scan_order(lkey, s == 1)
            vi = scan_order('hiT', s == 1).T
            dH[f'vt{s}'] = np.ascontiguousarray(vi.reshape(NCH, 64, 64).transpose(1, 0, 2))
        zl, zc = _gather_tok(rs, b, 'zpT', r64)
        dH['zp'] = zp_tiles(np.ascontiguousarray(zl.T), np.ascontiguousarray(zc.T))
        dH['band'] = band_mats(2 ** (h + 1))
        dH['pw'] = np.ascontiguousarray(inp['pool_w'][l][h])
        dH['psc'] = np.ascontiguousarray(inp['pool_scale'][l][r64][:, None])
        mapsH.append(dH)
    return mapsA, mapsH


def _merge_inputs(rs, rA, rH, with_ctx):
    out = []
    for c in range(NCORE):
        b, q = core_bq(c)
        lat = slice(q * TLAT, (q + 1) * TLAT)
        cx = slice(q * TCTX, (q + 1) * TCTX)

        def cat(lat_part, ctx_part):
            return np.ascontiguousarray(np.concatenate([lat_part, ctx_part], 1) if with_ctx else lat_part)

        oda = [cat(rA[b * 4 + h]['oT'][:, lat], rA[b * 4 + h]['oT'][:, SEQ + q * TCTX:SEQ + (q + 1) * TCTX]) for h in range(4)]
        ohf, ohb, opl = [], [], []
        for h in range(4):
            r = rH[b * 4 + h]
            f = r['oT0']
            ohf.append(cat(f[:, CTX:][:, lat], f[:, :CTX][:, cx]))
            bw = r['oT1']
            ohb.append(cat(bw[:, CTX:][:, ::-1][:, lat], bw[:, :CTX][:, ::-1][:, cx]))
            p = r['opT']
            opl.append(cat(p[:, CTX:][:, lat], p[:, :CTX][:, cx]))
        n = NT if with_ctx else TLAT
        d = dict(odaT=np.concatenate(oda, 0), ohfT=np.concatenate(ohf, 0), ohbT=np.concatenate(ohb, 0),
                 opoolT=np.concatenate(opl, 0),
                 sgT_in=np.ascontiguousarray(rs[c]['sgT'][:, :n]),
                 gateT_in=np.ascontiguousarray(rs[c]['gateT'][:, :n]),
                 hT_in=np.ascontiguousarray(rs[c]['hT_out'][:, :n]))
        out.append(d)
    return out


def kernel(**inputs):
    return _forward(inputs)


def _forward(inputs, dbg=None):
    inp = {k: np.asarray(v) for k, v in inputs.items()}
    dbg = dbg or (lambda name, val: None)
    H = HostW(inp)
    ncT1 = _prog('T1', lambda: build_T(None, 0, False, True))
    maps = []
    for c in range(NCORE):
        d = H.common(c, [0])
        d.update(H.pre(c, 0))
        d['hT_in'] = initial_hT(inp, c)
        maps.append(d)
    r1 = _run(ncT1, maps)
    dbg('r1', r1)
    ncA = _prog('A', build_A)
    ncH = _prog('H', build_H)
    mA, mH = _mixer_inputs(inp, r1, 0)
    rA = _run(ncA, mA)
    dbg('rA0', rA)
    rH = _run(ncH, mH)
    dbg('rH0', rH)
    del mA, mH
    ncT2 = _prog('T2', lambda: build_T(0, 1, False, True))
    mm = _merge_inputs(r1, rA, rH, True)
    maps = []
    for c in range(NCORE):
        d = H.common(c, [0, 1])
        d.update(H.mrg(c, 0))
        d.update(H.pre(c, 1))
        d.update(mm[c])
        maps.append(d)
    del r1, rA, rH
    r2 = _run(ncT2, maps)
    dbg('r2', r2)
    mA, mH = _mixer_inputs(inp, r2, 1)
    rA = _run(ncA, mA)
    dbg('rA1', rA)
    rH = _run(ncH, mH)
    dbg('rH1', rH)
    del mA, mH
    ncT3 = _prog('T3', lambda: build_T(1, None, True, False))
    mm = _merge_inputs(r2, rA, rH, False)
    maps = []
    for c in range(NCORE):
        d = H.common(c, [1])
        d.update(H.mrg(c, 1))
        d.update(mm[c])
        d['fnorm'] = vec_p(inp['final_norm'])
        maps.append(d)
    r3 = _run(ncT3, maps)
    out = np.empty((BATCH, SEQ, D), np.float32)
    for c in range(NCORE):
        b, q = core_bq(c)
        out[b, q * TLAT:(q + 1) * TLAT] = r3[c]['outT'].T
    return out
```

```python
import numpy as np
import ml_dtypes
import concourse.bass as bass
import concourse.mybir as mybir
from concourse.bass_utils import run_bass_kernel_spmd

F32 = mybir.dt.float32
BF16 = mybir.dt.bfloat16
AF = mybir.ActivationFunctionType
ALU = mybir.AluOpType
NPBF = ml_dtypes.bfloat16

D = 1024
SEQ = 16384
BATCH = 2
CTX = 256
DFF = 2816
DIN = 6144
NCORE = 8
TLAT = 4096
TCTX = 64
NT = TLAT + TCTX
NKEY = SEQ + CTX
EPS = 1e-6

ENGS = ('pe', 'act', 'dve', 'pool', 'sp')


class Tl:
    __slots__ = ('w', 'r', 'const', 'lsem', 'ssem', 'excl')

    def __init__(self, const=False, excl=False):
        self.w = None
        self.r = []
        self.const = const
        self.excl = excl
        self.lsem = None
        self.ssem = None


class Op:
    __slots__ = ('eng', 'fn', 'deps', 'sig', 'val', 'dsem')


class Prog:
    def __init__(self):
        self.ops = {e: [] for e in ENGS}
        self.dcnt = {}
        self.deng = {}

    def op(self, eng, meth, kw, reads=(), writes=(), dsem=None):
        o = Op()
        o.eng = eng
        o.fn = (meth, kw)
        o.sig = False
        o.val = 0
        o.dsem = dsem
        wr = {}
        rd = {}
        for t in reads:
            if t.w is not None:
                wr[id(t.w)] = t.w
            if t.excl:
                for r in t.r:
                    if r.eng != eng:
                        wr[id(r)] = r
        for t in writes:
            if t.w is not None:
                wr[id(t.w)] = t.w
            for r in t.r:
                rd[id(r)] = r
        deps = []
        for d in wr.values():
            if d.dsem is None and dsem is None and d.eng == eng and eng == 'pe':
                continue
            deps.append(d)
        for d in rd.values():
            if id(d) in wr:
                continue
            if d.dsem is None and dsem is None and d.eng == eng and eng == 'pe':
                continue
            deps.append(d)
        for d in deps:
            if d.dsem is None:
                d.sig = True
        o.deps = deps
        for t in reads:
            if not t.const:
                t.r.append(o)
        for t in writes:
            t.w = o
            t.r = []
        if dsem is not None:
            if len(writes) > 0:
                t = writes[0]
                if t.lsem is None:
                    t.lsem = 'l%d' % len(self.dcnt)
                    self.dcnt[t.lsem] = 0
                dsem = t.lsem
            else:
                t = reads[0]
                if t.ssem is None:
                    t.ssem = 's%d' % len(self.dcnt)
                    self.dcnt[t.ssem] = 0
                dsem = t.ssem
            o.dsem = dsem
            self.dcnt[dsem] = self.dcnt[dsem] + 16
            o.val = self.dcnt[dsem]
        self.ops[eng].append(o)
        return o

    def emit(self, nc):
        for e in ENGS:
            c = 0
            for o in self.ops[e]:
                if o.dsem is None and o.sig:
                    c += 1
                    o.val = c
        import contextlib
        with contextlib.ExitStack() as st:
            esem = {e: st.enter_context(nc.semaphore('es_' + e)) for e in ENGS}
            dsem = {k: st.enter_context(nc.semaphore('ds_' + k)) for k in self.dcnt}
            block = st.enter_context(nc.Block())
            prog = self

            def run(eng_name, e):
                waited = {}
                for o in prog.ops[eng_name]:
                    for d in o.deps:
                        if d.dsem is not None:
                            key = ('d', d.dsem)
                            sem = dsem[d.dsem]
                        else:
                            key = ('e', d.eng)
                            sem = esem[d.eng]
                        if waited.get(key, 0) >= d.val:
                            continue
                        waited[key] = d.val
                        e.wait_ge(sem, d.val)
                    ins = getattr(e, o.fn[0])(**o.fn[1])
                    if o.dsem is not None:
                        ins.then_inc(dsem[o.dsem], 16)
                    elif o.sig:
                        ins.then_inc(esem[eng_name], 1)
                if eng_name == 'sp':
                    for k, v in prog.dcnt.items():
                        e.wait_ge(dsem[k], v)

            @block.tensor
            def _(e):
                run('pe', e)

            @block.scalar
            def _(e):
                run('act', e)

            @block.vector
            def _(e):
                run('dve', e)

            @block.gpsimd
            def _(e):
                run('pool', e)

            @block.sync
            def _(e):
                run('sp', e)


class Ring:
    def __init__(self, n, excl=False):
        self.n = n
        self.i = 0
        self.t = [Tl(excl=excl) for _ in range(n)]

    def next(self):
        k = self.i % self.n
        self.i += 1
        return k, self.t[k]


def arr_w(W):
    K, N = W.shape
    return np.ascontiguousarray(W.reshape(K // 128, 128, N // 128, 128).transpose(2, 1, 0, 3))


def vec_p(v):
    return np.ascontiguousarray(v.reshape(-1, 128).T)


class Ops:
    def __init__(self, P):
        self.P = P

    def MM(self, out, lhsT, rhs, start, stop, reads, writes):
        self.P.op('pe', 'matmul', dict(out=out, lhsT=lhsT, rhs=rhs, start=start, stop=stop), reads, writes)

    def TR(self, out, in_, ident, reads, writes):
        self.P.op('pe', 'transpose', dict(out=out, in_=in_, identity=ident), reads, writes)

    def ACT(self, out, in_, func, reads, writes, **kw):
        self.P.op('act', 'activation', dict(out=out, in_=in_, func=func, **kw), reads, writes)

    def TT(self, out, in0, in1, op, reads, writes, eng='dve'):
        self.P.op(eng, 'tensor_tensor', dict(out=out, in0=in0, in1=in1, op=op), reads, writes)

    def TS(self, out, in0, s1, op0, reads, writes, s2=None, op1=None, eng='dve'):
        kw = dict(out=out, in0=in0, scalar1=s1, scalar2=s2, op0=op0)
        if op1 is not None:
            kw['op1'] = op1
        self.P.op(eng, 'tensor_scalar', kw, reads, writes)

    def STT(self, out, in0, scalar, in1, op0, op1, reads, writes):
        self.P.op('dve', 'scalar_tensor_tensor', dict(out=out, in0=in0, scalar=scalar, in1=in1, op0=op0, op1=op1),
                  reads, writes)

    def CP(self, out, in_, reads, writes, eng='dve'):
        self.P.op(eng, 'tensor_copy', dict(out=out, in_=in_), reads, writes)

    def RCP(self, out, in_, reads, writes, scratch=None):
        if scratch is None:
            self.P.op('dve', 'reciprocal', dict(out=out, in_=in_), reads, writes)
        else:
            self.P.op('dve', 'reciprocal_approx_accurate', dict(out=out, in_=in_, scratch=scratch), reads, writes)

    def MS(self, ap, val, writes, eng='dve'):
        self.P.op(eng, 'memset', dict(ap=ap, constant=val), (), writes)

    def DMA(self, eng, out, in_, reads, writes, dsem):
        self.P.op(eng, 'dma_start', dict(out=out, in_=in_), reads, writes, dsem)


def build_T(merge_layer, pre_layer, final, with_ctx, tiles_override=None, dbg=99):
    import contextlib
    nc = bass.Bass("TRN2", target_bir_lowering=False)
    P = Prog()
    O = Ops(P)
    MM, ACT, TT, TS, STT, CP, RCP, MS, DMA = O.MM, O.ACT, O.TT, O.TS, O.STT, O.CP, O.RCP, O.MS, O.DMA
    ntok = NT if with_ctx else TLAT

    def din(name, shape, dt=F32):
        return nc.dram_tensor(name, list(shape), dt, kind="ExternalInput").ap()

    def dout(name, shape, dt=F32):
        return nc.dram_tensor(name, list(shape), dt, kind="ExternalOutput").ap()

    hT_in = din("hT_in", [D, ntok])
    cs_d = din("cs", [128, 8, 2])
    ones_d = din("ones", [128, 128])
    bd64_d = din("bd64", [128, 128])
    rperm_d = din("rperm", [128, 128])
    layers = sorted(set(x for x in (merge_layer, pre_layer) if x is not None))
    wada_d = {l: din(f"wada{l}", [72, 128, 8, 128]) for l in layers}
    bada_d = {l: din(f"bada{l}", [128, 72]) for l in layers}
    if merge_layer is not None:
        ml = merge_layer
        nf2_d = din("nf2", [128, 8])
        f2w1_d = din("f2w1", [22, 128, 8, 128])
        f2w3_d = din("f2w3", [22, 128, 8, 128])
        f2w2_d = din("f2w2", [8, 128, 22, 128])
        wmrg_d = din("wmrg", [8, 128, 8, 128])
        wout_d = din("wout", [8, 128, 8, 128])
        hgn_d = din("hgn", [128, 1])
        dag_d = din("dag", [128, 1])
        odaT_d = din("odaT", [512, ntok], BF16)
        ohfT_d = din("ohfT", [256, ntok])
        ohbT_d = din("ohbT", [256, ntok])
        sgT_in_d = din("sgT_in", [256, ntok])
        opoolT_d = din("opoolT", [256, ntok], BF16)
        gateT_in_d = din("gateT_in", [3072, ntok])
    if pre_layer is not None:
        pl = pre_layer
        nf1_d = din("nf1", [128, 8])
        nmix_d = din("nmix", [128, 8])
        f1w1_d = din("f1w1", [22, 128, 8, 128])
        f1w3_d = din("f1w3", [22, 128, 8, 128])
        f1w2_d = din("f1w2", [8, 128, 22, 128])
        win_d = din("win", [48, 128, 8, 128])
        lbl_d = din("lbl", [128, 4, 2])
        cosT_d = din("cosT", [128, ntok])
        sinT_d = din("sinT", [128, ntok])
        hT_out = dout("hT_out", [D, ntok])
        qT_d = dout("qT", [512, ntok], BF16)
        kT_d = dout("kT", [512, ntok], BF16)
        vT_d = dout("vT", [512, ntok], BF16)
        hqT_d = dout("hqT", [256, ntok])
        kkfT_d = dout("kkfT", [256, ntok])
        kkbT_d = dout("kkbT", [256, ntok])
        lffT_d = dout("lffT", [256, ntok])
        lfbT_d = dout("lfbT", [256, ntok])
        hiT_d = dout("hiT", [256, ntok], BF16)
        sgT_d = dout("sgT", [256, ntok])
        zpT_d = dout("zpT", [256, ntok], BF16)
        gateT_d = dout("gateT", [3072, ntok])
    if final:
        fnorm_d = din("fnorm", [128, 8])
        outT_d = dout("outT", [D, TLAT])

    with contextlib.ExitStack() as st:
        def sb(name, shape, dt=F32):
            return st.enter_context(nc.sbuf_tensor("s_" + name, list(shape), dt))

        TS_ = 1024
        hT = sb("hT", [128, 8, TS_]); hT_t = [[Tl() for _ in range(2)] for _ in range(8)]
        xn = sb("xn", [128, 8, TS_], BF16); xn_t = [[Tl() for _ in range(2)] for _ in range(8)]
        hid = sb("hid", [128, 22, TS_], BF16); hid_t = [[Tl() for _ in range(2)] for _ in range(22)]
        NW8 = 6 if (merge_layer is not None and pre_layer is not None) else 8
        wk8 = sb("wk8", [128, NW8, 8, 128], BF16); wk8_r = Ring(NW8)
        NW22 = 3
        wk22 = sb("wk22", [128, NW22, 22, 128], BF16); wk22_r = Ring(NW22)
        tmpf = sb("tmpf", [128, 4, 512]); tmpf_r = Ring(4)
        sqb = sb("sqb", [128, 3, 512], BF16); sqb_r = Ring(3)
        rstd = sb("rstd", [128, 2, 512]); rstd_r = Ring(2)
        stgf = sb("stgf", [128, 4, 512]); stgf_r = Ring(4)
        stgb = sb("stgb", [128, 4, 512], BF16); stgb_r = Ring(4)
        ps = st.enter_context(nc.psum_tensor("ps", [128, 8, 512], F32)); ps_r = Ring(8, excl=True)
        ones_b = sb("ones_b", [128, 128], BF16); ones_t = Tl(const=True)
        bd64_b = sb("bd64_b", [128, 128], BF16); bd64_t = Tl(const=True)
        rperm_b = sb("rperm_b", [128, 128], BF16); rperm_t = Tl(const=True)
        cs = sb("cs", [128, 8, 2]); cs_t = Tl(const=True)
        wada = sb("wada", [128, 2, 8, 128]); wada_r = Ring(2)
        mod = {l: sb(f"mod{l}", [128, 72, 2]) for l in layers}; mod_t = Tl(const=True)
        bada = {l: sb(f"bada{l}", [128, 72]) for l in layers}
        vec_t = Tl(const=True)
        epsb = sb("epsb", [128, 1])
        oneb = sb("oneb", [128, 1])
        MS(epsb[:], EPS, [vec_t])
        MS(oneb[:], 1.0, [vec_t])

        DMA('pool', ones_b[:], ones_d, (), [ones_t], 'w')
        DMA('pool', bd64_b[:], bd64_d, (), [bd64_t], 'w')
        DMA('pool', rperm_b[:], rperm_d, (), [rperm_t], 'w')
        DMA('sp', cs[:], cs_d, (), [cs_t], 'l')
        cs_flat = cs[:].rearrange("p a b -> p (a b)")
        ACT(cs_flat, cs_flat, AF.Silu, [cs_t], [cs_t])

        for l in layers:
            DMA('sp', bada[l][:], bada_d[l], (), [mod_t], 'l')
            bk, bt = ps_r.next()
            need = set()
            if l == pre_layer:
                need.update(range(0, 40))
            if l == merge_layer:
                need.update(range(40, 72))
            MS(mod[l][:], 0.0, [mod_t])
            for fc in sorted(need):
                wi, wt = wada_r.next()
                DMA('sp', wada[:, wi], wada_d[l][fc], (), [wt], 'l')
                for kc in range(8):
                    MM(ps[:, bk, 2 * fc:2 * fc + 2], wada[:, wi, kc, :], cs[:, kc, :], kc == 0, kc == 7,
                       [wt, cs_t], [bt])
            lo, hi = min(need), max(need) + 1
            TT(mod[l][:, lo:hi, :], ps[:, bk, 2 * lo:2 * hi].rearrange("p (a b) -> p a b", b=2),
               bada[l][:, lo:hi].unsqueeze(2).to_broadcast([128, hi - lo, 2]), ALU.add, [bt, mod_t], [mod_t])

        def mv(l, k):
            return mod[l][:, 8 * k:8 * k + 8, :]

        vecs = {}

        def mk_AB(name, gain_d, l, kshift, kscale):
            g = sb("g_" + name, [128, 8])
            A = sb("A_" + name, [128, 8, 2])
            B = sb("B_" + name, [128, 8, 2])
            vecs[name + 'A'] = A
            vecs[name + 'B'] = B
            DMA('sp', g[:], gain_d, (), [vec_t], 'l')
            TS(A[:], mv(l, kscale), 1.0, ALU.add, [mod_t, vec_t], [vec_t])
            TT(A[:], A[:], g[:].unsqueeze(2).to_broadcast([128, 8, 2]), ALU.mult, [vec_t], [vec_t])
            CP(B[:], mv(l, kshift), [mod_t, vec_t], [vec_t])

        def mk_gate(name, l, k, mul):
            G = sb("G_" + name, [128, 8, 2])
            vecs[name] = G
            TS(G[:], mv(l, k), float(mul), ALU.mult, [mod_t, vec_t], [vec_t])

        if merge_layer is not None:
            mk_gate('g5', ml, 5, 1.0)
            mk_AB('f2', nf2_d, ml, 6, 7)
            mk_gate('g8', ml, 8, 0.5)
            hgn = sb("hgn", [128, 1])
            DMA('sp', hgn[:], hgn_d, (), [vec_t], 'l')
            dagn = sb("dagn", [128, 1])
            DMA('sp', dagn[:], dag_d, (), [vec_t], 'l')
            TS(dagn[:], dagn[:], float(1.0 - (0.8 - 0.6 * np.exp(-0.3 * ml))), ALU.mult, [vec_t], [vec_t])
        if pre_layer is not None:
            mk_AB('f1', nf1_d, pl, 0, 1)
            mk_gate('g2', pl, 2, 0.5)
            mk_AB('mx', nmix_d, pl, 3, 4)
            oml = sb("oml", [128, 4])
            if pl == 0:
                MS(oml[:], 1.0, [vec_t])
            else:
                lbl = sb("lbl", [128, 4, 2])
                DMA('sp', lbl[:], lbl_d, (), [vec_t], 'l')
                TT(oml[:], lbl[:, :, 0], lbl[:, :, 1], ALU.subtract, [vec_t], [vec_t])
                ACT(oml[:], oml[:], AF.Sigmoid, [vec_t], [vec_t])
        if final:
            fng = sb("fng", [128, 8])
            DMA('sp', fng[:], fnorm_d, (), [vec_t], 'l')

        def load_w8(src):
            wi, wt = wk8_r.next()
            DMA('pool', wk8[:, wi], src, (), [wt], 'w')
            return wi, wt

        def load_w22(src):
            wi, wt = wk22_r.next()
            DMA('pool', wk22[:, wi].rearrange("p a b -> p (a b)").rearrange("p (c d) -> p c d", d=704),
                src.rearrange("p a b -> p (a b)").rearrange("p (c d) -> p c d", d=704), (), [wt], 'w')
            return wi, wt

        def blocks(ts):
            bw = min(512, ts)
            return [(i * bw, bw) for i in range(ts // bw)]

        def sumsq_rstd(srcs, bw, grp_lhsT, grp_t, nfeat, use_ln=True):
            bk, bt = ps_r.next()
            nk = len(srcs)
            for k, (sap, stl) in enumerate(srcs):
                si, s_t = sqb_r.next()
                ACT(sqb[:, si, :bw], sap, AF.Square, [stl], [s_t])
                MM(ps[:, bk, :bw], grp_lhsT, sqb[:, si, :bw], k == 0, k == nk - 1, [s_t, grp_t], [bt])
            ri, r_t = rstd_r.next()
            if use_ln:
                ACT(rstd[:, ri, :bw], ps[:, bk, :bw], AF.Ln, [bt, vec_t], [r_t], scale=1.0 / nfeat, bias=epsb[:, 0:1])
                ACT(rstd[:, ri, :bw], rstd[:, ri, :bw], AF.Exp, [r_t], [r_t], scale=-0.5)
            else:
                ACT(rstd[:, ri, :bw], ps[:, bk, :bw], AF.Sqrt, [bt, vec_t], [r_t], scale=1.0 / nfeat, bias=epsb[:, 0:1])
                RCP(rstd[:, ri, :bw], rstd[:, ri, :bw], [r_t], [r_t])
            return ri, r_t

        def norm_mod(ts, A, B, ci):
            for bi, (t0, bw) in enumerate(blocks(ts)):
                ri, r_t = sumsq_rstd([(hT[:, k, t0:t0 + bw], hT_t[k][bi]) for k in range(8)], bw,
                                     ones_b[:], ones_t, D)
                for k in range(8):
                    ti, t_t = tmpf_r.next()
                    TT(tmpf[:, ti, :bw], hT[:, k, t0:t0 + bw], rstd[:, ri, :bw], ALU.mult, [hT_t[k][bi], r_t], [t_t])
                    ACT(xn[:, k, t0:t0 + bw], tmpf[:, ti, :bw], AF.Identity, [t_t, vec_t], [xn_t[k][bi]],
                        scale=A[:, k, ci:ci + 1], bias=B[:, k, ci:ci + 1])

        def ffn(ts, w1_d, w3_d, w2_d, G, ci):
            blks = blocks(ts)
            for j in range(22):
                w1i, w1t = load_w8(w1_d[j])
                w3i, w3t = load_w8(w3_d[j])
                for bi, (t0, bw) in enumerate(blks):
                    bka, bta = ps_r.next()
                    bkb, btb = ps_r.next()
                    for kc in range(8):
                        MM(ps[:, bka, :bw], wk8[:, w1i, kc, :], xn[:, kc, t0:t0 + bw], kc == 0, kc == 7,
                           [w1t, xn_t[kc][bi]], [bta])
                    for kc in range(8):
                        MM(ps[:, bkb, :bw], wk8[:, w3i, kc, :], xn[:, kc, t0:t0 + bw], kc == 0, kc == 7,
                           [w3t, xn_t[kc][bi]], [btb])
                    ti, t_t = tmpf_r.next()
                    ACT(tmpf[:, ti, :bw], ps[:, bka, :bw], AF.Silu, [bta], [t_t])
                    TT(hid[:, j, t0:t0 + bw], tmpf[:, ti, :bw], ps[:, bkb, :bw], ALU.mult, [t_t, btb], [hid_t[j][bi]])
            for i in range(8):
                w2i, w2t = load_w22(w2_d[i])
                for bi, (t0, bw) in enumerate(blks):
                    bk, bt = ps_r.next()
                    for j in range(22):
                        MM(ps[:, bk, :bw], wk22[:, w2i, j, :], hid[:, j, t0:t0 + bw], j == 0, j == 21,
                           [w2t, hid_t[j][bi]], [bt])
                    STT(hT[:, i, t0:t0 + bw], ps[:, bk, :bw], G[:, i, ci:ci + 1], hT[:, i, t0:t0 + bw],
                        ALU.mult, ALU.add, [bt, hT_t[i][bi], vec_t], [hT_t[i][bi]])

        def store(dram_ap, sbuf_ap, tl):
            DMA('sp', dram_ap, sbuf_ap, [tl], (), 's')

        if merge_layer is not None:
            oda = sb("oda", [128, 4, TS_], BF16); oda_t = [[Tl() for _ in range(2)] for _ in range(4)]
            opl = sb("opl", [128, 2, TS_], BF16); opl_t = Tl()
            ohg = sb("ohg", [128, 2, TS_], BF16); ohg_t = [[Tl() for _ in range(2)] for _ in range(2)]
            hgin = sb("hgin", [128, 2, 3, 512]); hgin_r = Ring(2)
            gts = sb("gts", [128, 2, 3, 512]); gts_r = Ring(2)
        if pre_layer is not None:
            cst = sb("cst", [128, 2, TS_]); cst_t = Tl()

        tiles = [(i * 1024, 1024, 0) for i in range(4)]
        if with_ctx:
            tiles.append((TLAT, TCTX, 1))
        if tiles_override is not None:
            tiles = tiles_override

        for (T0, ts, ci) in tiles:
            blks = blocks(ts)
            for k in range(8):
                DMA('act', hT[:, k, :ts], hT_in[k * 128:(k + 1) * 128, T0:T0 + ts], (), hT_t[k], 'l')
            if merge_layer is not None:
                DMA('act', oda[:, :, :ts], odaT_d[:, T0:T0 + ts].rearrange("(c p) t -> p c t", p=128), (),
                    [t_ for row in oda_t for t_ in row], 'l')
                DMA('act', opl[:, :, :ts], opoolT_d[:, T0:T0 + ts].rearrange("(c p) t -> p c t", p=128), (), [opl_t], 'l')
                for c in range(4):
                    for bi, (t0, bw) in enumerate(blks):
                        oc = oda[:, c, t0:t0 + bw]
                        si, s_t = sqb_r.next()
                        TT(sqb[:, si, :bw], oc, oc, ALU.mult, [oda_t[c][bi]], [s_t])
                        bk, bt = ps_r.next()
                        MM(ps[:, bk, :bw], ones_b[:], sqb[:, si, :bw], True, True, [s_t, ones_t], [bt])
                        ri, r_t = rstd_r.next()
                        ACT(rstd[:, ri, :bw], ps[:, bk, :bw], AF.Ln, [bt, vec_t], [r_t], scale=1.0 / 128, bias=epsb[:, 0:1])
                        ACT(rstd[:, ri, :bw], rstd[:, ri, :bw], AF.Exp, [r_t], [r_t], scale=-0.5)
                        STT(oc, oc, dagn[:, 0:1], rstd[:, ri, :bw], ALU.mult, ALU.mult, [oda_t[c][bi], r_t, vec_t],
                            [oda_t[c][bi]])
                for c2 in range(2):
                    for bi, (t0, bw) in enumerate(blks):
                        hi_, h_t = hgin_r.next()
                        for s_i, src in enumerate((ohfT_d, ohbT_d, sgT_in_d)):
                            DMA('act', hgin[:, hi_, s_i, :bw], src[c2 * 128:(c2 + 1) * 128, T0 + t0:T0 + t0 + bw],
                                (), [h_t], 'l')
                        o0 = hgin[:, hi_, 0, :bw]
                        TT(o0, o0, hgin[:, hi_, 1, :bw], ALU.add, [h_t], [h_t])
                        ri, r_t = sumsq_rstd([(o0, h_t)], bw, bd64_b[:], bd64_t, 64, use_ln=False)
                        TT(o0, o0, rstd[:, ri, :bw], ALU.mult, [h_t, r_t], [h_t])
                        STT(ohg[:, c2, t0:t0 + bw], o0, hgn[:, 0:1], hgin[:, hi_, 2, :bw], ALU.mult, ALU.mult,
                            [h_t, vec_t], [ohg_t[c2][bi]])
                gview = gateT_in_d.rearrange("(b c p) t -> c p b t", b=3, p=128)
                for i in range(8):
                    wi, wt = load_w8(wmrg_d[i])
                    for bi, (t0, bw) in enumerate(blks):
                        gi, g_t = gts_r.next()
                        DMA('act', gts[:, gi, :, :bw], gview[i][:, :, T0 + t0:T0 + t0 + bw], (), [g_t], 'l')
                        bka, bta = ps_r.next()
                        bkh, bth = ps_r.next()
                        bkp, btp = ps_r.next()
                        for c in range(4):
                            MM(ps[:, bka, :bw], wk8[:, wi, c, :], oda[:, c, t0:t0 + bw], c == 0, c == 3,
                               [wt, oda_t[c][bi]], [bta])
                        for c in range(2):
                            MM(ps[:, bkh, :bw], wk8[:, wi, 4 + c, :], ohg[:, c, t0:t0 + bw], c == 0, c == 1,
                               [wt, ohg_t[c][bi]], [bth])
                        for c in range(2):
                            MM(ps[:, bkp, :bw], wk8[:, wi, 6 + c, :], opl[:, c, t0:t0 + bw], c == 0, c == 1, [wt, opl_t], [btp])
                        t1, t1t = tmpf_r.next()
                        t2, t2t = tmpf_r.next()
                        TT(tmpf[:, t1, :bw], gts[:, gi, 0, :bw], ps[:, bka, :bw], ALU.mult, [g_t, bta], [t1t])
                        TT(tmpf[:, t2, :bw], gts[:, gi, 1, :bw], ps[:, bkh, :bw], ALU.mult, [g_t, bth], [t2t])
                        TT(tmpf[:, t1, :bw], tmpf[:, t1, :bw], tmpf[:, t2, :bw], ALU.add, [t1t, t2t], [t1t])
                        TT(tmpf[:, t2, :bw], gts[:, gi, 2, :bw], ps[:, bkp, :bw], ALU.mult, [g_t, btp, t2t], [t2t])
                        TT(xn[:, i, t0:t0 + bw], tmpf[:, t1, :bw], tmpf[:, t2, :bw], ALU.add, [t1t, t2t], [xn_t[i][bi]])
                G5 = vecs['g5']
                for i in range(8):
                    wi, wt = load_w8(wout_d[i])
                    for bi, (t0, bw) in enumerate(blks):
                        bk, bt = ps_r.next()
                        for kc in range(8):
                            MM(ps[:, bk, :bw], wk8[:, wi, kc, :], xn[:, kc, t0:t0 + bw], kc == 0, kc == 7,
                               [wt, xn_t[kc][bi]], [bt])
                        STT(hT[:, i, t0:t0 + bw], ps[:, bk, :bw], G5[:, i, ci:ci + 1], hT[:, i, t0:t0 + bw],
                            ALU.mult, ALU.add, [bt, hT_t[i][bi], vec_t], [hT_t[i][bi]])
                norm_mod(ts, vecs['f2A'], vecs['f2B'], ci)
                ffn(ts, f2w1_d, f2w3_d, f2w2_d, vecs['g8'], ci)
            if final:
                for bi, (t0, bw) in enumerate(blks):
                    ri, r_t = sumsq_rstd([(hT[:, k, t0:t0 + bw], hT_t[k][bi]) for k in range(8)], bw,
                                         ones_b[:], ones_t, D)
                    for k in range(8):
                        si, s_t = stgf_r.next()
                        STT(stgf[:, si, :bw], hT[:, k, t0:t0 + bw], fng[:, k:k + 1], rstd[:, ri, :bw],
                            ALU.mult, ALU.mult, [hT_t[k][bi], r_t, vec_t], [s_t])
                        store(outT_d[k * 128:(k + 1) * 128, T0 + t0:T0 + t0 + bw], stgf[:, si, :bw], s_t)
            if pre_layer is not None:
                if dbg >= 1:
                    norm_mod(ts, vecs['f1A'], vecs['f1B'], ci)
                if dbg >= 2:
                    ffn(ts, f1w1_d, f1w3_d, f1w2_d, vecs['g2'], ci)
                for k in range(8):
                    DMA('sp', hT_out[k * 128:(k + 1) * 128, T0:T0 + ts], hT[:, k, :ts], hT_t[k], (), 's')
                if dbg < 3:
                    continue
                norm_mod(ts, vecs['mxA'], vecs['mxB'], ci)
                DMA('act', cst[:, 0, :ts], cosT_d[:, T0:T0 + ts], (), [cst_t], 'l')
                DMA('act', cst[:, 1, :ts], sinT_d[:, T0:T0 + ts], (), [cst_t], 'l')
                for c in range(48):
                    wi, wt = load_w8(win_d[c])
                    for bi, (t0, bw) in enumerate(blks):
                        bk, bt = ps_r.next()
                        for kc in range(8):
                            MM(ps[:, bk, :bw], wk8[:, wi, kc, :], xn[:, kc, t0:t0 + bw], kc == 0, kc == 7,
                               [wt, xn_t[kc][bi]], [bt])
                        zin = ps[:, bk, :bw]
                        tok = slice(T0 + t0, T0 + t0 + bw)
                        r2 = slice((c % 2) * 128, (c % 2 + 1) * 128)
                        if c < 8:
                            dst = (qT_d if c < 4 else kT_d)[(c % 4) * 128:(c % 4 + 1) * 128, tok]
                            si, s_t = sqb_r.next()
                            ACT(sqb[:, si, :bw], zin, AF.Copy, [bt], [s_t])
                            bk2, bt2 = ps_r.next()
                            MM(ps[:, bk2, :bw], rperm_b[:], sqb[:, si, :bw], True, True, [s_t, rperm_t], [bt2])
                            t1, t1t = tmpf_r.next()
                            t2, t2t = tmpf_r.next()
                            TT(tmpf[:, t1, :bw], zin, cst[:, 0, t0:t0 + bw], ALU.mult, [bt, cst_t], [t1t])
                            TT(tmpf[:, t2, :bw], ps[:, bk2, :bw], cst[:, 1, t0:t0 + bw], ALU.mult, [bt2, cst_t], [t2t])
                            oi, o_t = stgb_r.next()
                            TT(stgb[:, oi, :bw], tmpf[:, t1, :bw], tmpf[:, t2, :bw], ALU.add, [t1t, t2t], [o_t])
                            store(dst, stgb[:, oi, :bw], o_t)
                        elif c < 12 or 18 <= c < 20 or 22 <= c < 24:
                            if c < 12:
                                dst = vT_d[(c - 8) * 128:(c - 7) * 128, tok]
                            elif c < 20:
                                dst = hiT_d[r2, tok]
                            else:
                                dst = zpT_d[r2, tok]
                            oi, o_t = stgb_r.next()
                            CP(stgb[:, oi, :bw], zin, [bt], [o_t])
                            store(dst, stgb[:, oi, :bw], o_t)
                        elif c < 14 or 20 <= c < 22:
                            dst = (hqT_d if c < 14 else sgT_d)[r2, tok]
                            oi, o_t = stgf_r.next()
                            ACT(stgf[:, oi, :bw], zin, AF.Silu, [bt], [o_t])
                            store(dst, stgf[:, oi, :bw], o_t)
                        elif c < 18:
                            di = (c - 14) // 2
                            col = c - 14
                            kd = (kkfT_d, kkbT_d)[di][r2, tok]
                            ld = (lffT_d, lfbT_d)[di][r2, tok]
                            oi, o_t = stgf_r.next()
                            ACT(stgf[:, oi, :bw], zin, AF.Sigmoid, [bt], [o_t], scale=-1.0)
                            TS(stgf[:, oi, :bw], stgf[:, oi, :bw], oml[:, col:col + 1], ALU.mult, [o_t, vec_t], [o_t])
                            store(kd, stgf[:, oi, :bw], o_t)
                            o2, o2_t = stgf_r.next()
                            ACT(stgf[:, o2, :bw], stgf[:, oi, :bw], AF.Ln, [o_t, vec_t], [o2_t], scale=-1.0,
                                bias=oneb[:, 0:1])
                            store(ld, stgf[:, o2, :bw], o2_t)
                        else:
                            dst = gateT_d[(c - 24) * 128:(c - 23) * 128, tok]
                            oi, o_t = stgf_r.next()
                            ACT(stgf[:, oi, :bw], zin, AF.Sigmoid, [bt], [o_t])
                            store(dst, stgf[:, oi, :bw], o_t)
        P.emit(nc)
    return nc


def host_consts():
    ones = np.ones((128, 128), np.float32)
    bd64 = np.zeros((128, 128), np.float32)
    bd64[:64, :64] = 1
    bd64[64:, 64:] = 1
    R = np.zeros((128, 128), np.float32)
    for blk in (0, 64):
        for j in range(16):
            R[blk + j, blk + 16 + j] = -1
            R[blk + 16 + j, blk + j] = 1
            R[blk + 32 + j, blk + 48 + j] = -1
            R[blk + 48 + j, blk + 32 + j] = 1
    return ones, bd64, np.ascontiguousarray(R.T)


def rope_tables():
    t = np.arange(SEQ)
    row = (t // 64).astype(np.float32)
    col = (t % 64).astype(np.float32)
    inv = (np.float32(10000.0) ** (-np.arange(0, 32, 2, dtype=np.float32) / np.float32(32))).astype(np.float32)
    ar = (row[:, None] * inv[None]).astype(np.float32)
    ac = (col[:, None] * inv[None]).astype(np.float32)
    cos64 = np.concatenate([np.cos(ar), np.cos(ar), np.cos(ac), np.cos(ac)], 1).astype(np.float32)
    sin64 = np.concatenate([np.sin(ar), np.sin(ar), np.sin(ac), np.sin(ac)], 1).astype(np.float32)
    return np.tile(cos64.T, (2, 1)), np.tile(sin64.T, (2, 1))


def core_bq(c):
    return c // 4, c % 4


class HostW:
    def __init__(self, inp):
        self.inp = inp
        self.ones, self.bd64, self.rperm = host_consts()
        self.cosT, self.sinT = rope_tables()
        self.cache = {}

    def get(self, key, fn):
        if key not in self.cache:
            self.cache[key] = fn()
        return self.cache[key]

    def common(self, c, layers):
        inp = self.inp
        b, q = core_bq(c)
        d = dict(ones=self.ones, bd64=self.bd64, rperm=self.rperm)
        d['cs'] = np.ascontiguousarray(np.stack([vec_p(inp['c'][b]), vec_p(inp['c_ctx'])], axis=2))
        for l in layers:
            d[f'wada{l}'] = self.get(('wada', l), lambda: arr_w(inp['w_ada'][l]))
            d[f'bada{l}'] = self.get(('bada', l), lambda: vec_p(inp['b_ada'][l]))
        return d

    def pre(self, c, l, with_ctx=True):
        inp = self.inp
        b, q = core_bq(c)
        d = {}
        d['nf1'] = vec_p(inp['norm_ffn1'][l])
        d['nmix'] = vec_p(inp['norm_mix'][l])
        d['f1w1'] = self.get(('f1w1', l), lambda: arr_w(inp['ffn1_w1'][l]))
        d['f1w3'] = self.get(('f1w3', l), lambda: arr_w(inp['ffn1_w3'][l]))
        d['f1w2'] = self.get(('f1w2', l), lambda: arr_w(inp['ffn1_w2'][l]))
        d['win'] = self.get(('win', l), lambda: arr_w(inp['w_in'][l]))
        lg = inp['hg_lb_logits']
        lbl = np.zeros((128, 4, 2), np.float32)
        for di in range(2):
            for ch in range(2):
                for dep in range(2):
                    lbl[:, di * 2 + ch, dep] = lg[dep, di, ch * 128:(ch + 1) * 128]
        d['lbl'] = lbl
        cosT = self.cosT[:, q * TLAT:(q + 1) * TLAT]
        sinT = self.sinT[:, q * TLAT:(q + 1) * TLAT]
        if with_ctx:
            cosT = np.concatenate([cosT, np.ones((128, TCTX), np.float32)], 1)
            sinT = np.concatenate([sinT, np.zeros((128, TCTX), np.float32)], 1)
        d['cosT'] = np.ascontiguousarray(cosT)
        d['sinT'] = np.ascontiguousarray(sinT)
        return d

    def mrg(self, c, l):
        inp = self.inp
        d = {}
        d['nf2'] = vec_p(inp['norm_ffn2'][l])
        d['f2w1'] = self.get(('f2w1', l), lambda: arr_w(inp['ffn2_w1'][l]))
        d['f2w3'] = self.get(('f2w3', l), lambda: arr_w(inp['ffn2_w3'][l]))
        d['f2w2'] = self.get(('f2w2', l), lambda: arr_w(inp['ffn2_w2'][l]))
        d['wmrg'] = self.get(('wmrg', l), lambda: arr_w(np.concatenate(
            [inp['w_proj_da'][l], inp['w_proj_hg'][l], inp['w_proj_pool'][l]], 0)))
        d['wout'] = self.get(('wout', l), lambda: arr_w(inp['w_out'][l]))
        d['hgn'] = np.ascontiguousarray(np.tile(inp['hg_norm'][l], 2)[:, None])
        d['dag'] = np.ascontiguousarray(inp['da_subln'][l][:, None])
        return d


def initial_hT(inp, c):
    b, q = core_bq(c)
    xs = inp['x'][b, q * TLAT:(q + 1) * TLAT]
    cx = inp['ctx'][b, q * TCTX:(q + 1) * TCTX]
    return np.ascontiguousarray(np.concatenate([xs, cx], 0).T)


NQC = SEQ + CTX


def build_A(nq_tiles=32, nkb=130, with_ctxq=True):
    import contextlib
    nc = bass.Bass("TRN2", target_bir_lowering=False)
    P = Prog()
    O = Ops(P)
    MM, ACT, TT, TS, STT, CP, RCP, MS, DMA = O.MM, O.ACT, O.TT, O.TS, O.STT, O.CP, O.RCP, O.MS, O.DMA

    def din(name, shape, dt=F32):
        return nc.dram_tensor(name, list(shape), dt, kind="ExternalInput").ap()

    qT_d = din("qT", [128, NQC], BF16)
    kT_d = din("kT", [128, NKEY], BF16)
    v_d = din("v", [128, 130, 128], BF16)
    lamv_d = din("lamv", [128, 4, 64])
    lami_d = din("lami", [128, 2])
    gain_d = din("gain", [128, 1])
    ones_d = din("ones", [128, 128])
    oT_d = nc.dram_tensor("oT", [128, NQC], BF16, kind="ExternalOutput").ap()

    with contextlib.ExitStack() as st:
        def sb(name, shape, dt=F32):
            return st.enter_context(nc.sbuf_tensor("s_" + name, list(shape), dt))

        qT = sb("qT", [128, NQC], BF16)
        kT = sb("kT", [128, NKEY], BF16); kT_t = Tl(const=True)
        v = sb("v", [128, 130, 128], BF16); v_t = Tl(const=True)
        NPB = 12
        pb = sb("pb", [128, NPB, 2, 512], BF16); pb_r = Ring(NPB)
        tq = sb("tq", [128, 4, 2, 512], BF16); tq_r = Ring(4)
        acc = sb("acc", [128, 2, 512]); acc_t = Tl()
        fin = sb("fin", [128, 4, 512]); fin_r = Ring(4)
        ocp = sb("ocp", [128, 2, 2, 512]); oc_r = Ring(2)
        sqb = sb("sqb", [128, 512], BF16); sqb_t = Tl()
        ob = sb("ob", [128, 2, 512], BF16); ob_r = Ring(2)
        ones_f = sb("ones_f", [128, 128]); ones_b = sb("ones_b", [128, 128], BF16); c_t = Tl(const=True)
        lamv = sb("lamv", [128, 4, 64]); lami = sb("lami", [128, 2]); gain = sb("gain", [128, 1])
        lw = sb("lw", [128, 2, 64]); ls = sb("ls", [128, 2]); neglam = sb("neglam", [128, 1]); gsc = sb("gsc", [128, 1])
        epsb = sb("epsb", [128, 1])
        ps = st.enter_context(nc.psum_tensor("ps", [128, 8, 512], F32))
        s_r = Ring(6, excl=True)
        o_t = [Tl(excl=True), Tl(excl=True)]

        MS(epsb[:], EPS, [c_t])
        DMA('sp', ones_f[:], ones_d, (), [c_t], 'l')
        onesb_t = Tl(const=True)
        DMA('pool', ones_b[:], ones_d, (), [onesb_t], 'l')
        DMA('sp', lamv[:], lamv_d, (), [c_t], 'l')
        DMA('sp', lami[:], lami_d, (), [c_t], 'l')
        DMA('sp', gain[:], gain_d, (), [c_t], 'l')
        TT(lw[:], lamv[:, 0:4:2, :], lamv[:, 1:4:2, :], ALU.mult, [c_t], [c_t])
        P.op('dve', 'tensor_reduce', dict(out=ls[:], in_=lw[:], axis=mybir.AxisListType.X, op=ALU.add), [c_t], [c_t])
        ACT(ls[:], ls[:], AF.Exp, [c_t], [c_t])
        TT(neglam[:], ls[:, 0:1], ls[:, 1:2], ALU.subtract, [c_t], [c_t])
        STT(neglam[:], neglam[:], -1.0, lami[:, 0:1], ALU.mult, ALU.subtract, [c_t], [c_t])
        TT(gsc[:], gain[:], lami[:, 1:2], ALU.mult, [c_t], [c_t])

        for i in range(0, NKEY, 2080):
            DMA('sp', kT[:, i:i + 2080], kT_d[:, i:i + 2080], (), [kT_t], 'l')
        for i in range(0, 130, 13):
            DMA('sp', v[:, i:i + 13, :], v_d[:, i:i + 13, :], (), [v_t], 'l')

        qtiles = [(i * 512, 512, 0, nkb) for i in range(nq_tiles)]
        if with_ctxq:
            qtiles.append((SEQ, CTX, 128, 130))
        q_t = [Tl() for _ in qtiles]
        for qi, (q0, qw, kb0, kb1) in enumerate(qtiles):
            DMA('sp', qT[:, q0:q0 + qw], qT_d[:, q0:q0 + qw], (), [q_t[qi]], 'l')

        blocks = [(qi, kb) for qi, (q0, qw, kb0, kb1) in enumerate(qtiles) for kb in range(kb0, kb1)]
        LA = 2
        sbanks = {}

        def emit_qk(bi):
            qi, kb = blocks[bi]
            q0, qw, kb0, kb1 = qtiles[qi]
            banks = [s_r.next(), s_r.next()]
            sbanks[bi] = banks
            for c in range(2):
                bk, bt = banks[c]
                MM(ps[:, bk, :qw], kT[c * 64:(c + 1) * 64, kb * 128:(kb + 1) * 128],
                   qT[c * 64:(c + 1) * 64, q0:q0 + qw], True, True, [kT_t, q_t[qi]], [bt])

        pend = []
        accst = {'first': True, 'prev': None}

        def accumulate(xap, x_t, qw):
            if accst['first']:
                CP(acc[:, :, :qw], xap, [x_t], [acc_t])
                accst['first'] = False
            else:
                TT(acc[:, :, :qw], acc[:, :, :qw], xap, ALU.add, [x_t, acc_t], [acc_t])

        def emit_rest(bi):
            qi, kb = blocks[bi]
            q0, qw, kb0, kb1 = qtiles[qi]
            first = kb == kb0
            last = kb == kb1 - 1
            banks = sbanks.pop(bi)
            (bk0, bt0), (bk1, bt1) = banks
            assert bk1 == bk0 + 1
            pi, p_t = pb_r.next()
            ACT(pb[:, pi, :, :qw], ps[:, bk0:bk0 + 2, :qw], AF.Exp, [bt0, bt1], [p_t], scale=0.125)
            for c in range(2):
                MM(ps[:, 6 + c, :qw], v[:, kb, :], pb[:, pi, c, :qw], first, last, [v_t, p_t], [o_t[c]])
            if first:
                accst['first'] = True
                accst['prev'] = None
            if accst['prev'] is None and not last:
                accst['prev'] = (pi, p_t)
            else:
                if accst['prev'] is None:
                    pend.append((pb[:, pi, :, :qw], p_t))
                else:
                    ppi, pp_t = accst['prev']
                    accst['prev'] = None
                    ti, t_t = tq_r.next()
                    TT(tq[:, ti, :, :qw], pb[:, ppi, :, :qw], pb[:, pi, :, :qw], ALU.add, [pp_t, p_t], [t_t])
                    pend.append((tq[:, ti, :, :qw], t_t))
                if len(pend) == 2:
                    (xa, xa_t), (xb_, xb_t) = pend
                    TT(xa, xa, xb_, ALU.add, [xa_t, xb_t], [xa_t])
                    accumulate(xa, xa_t, qw)
                    del pend[:]
                if last and pend:
                    accumulate(pend[0][0], pend[0][1], qw)
                    del pend[:]
            if not last:
                return
            oci, oc_t = oc_r.next()
            for c in range(2):
                ACT(ocp[:, oci, c, :qw], ps[:, 6 + c, :qw], AF.Copy, [o_t[c]], [oc_t])
            ts_ = []
            for c in range(2):
                bk, bt = s_r.next()
                MM(ps[:, bk, :qw], ones_f[:], acc[:, c, :qw], True, True, [c_t, acc_t], [bt])
                fi, f_t = fin_r.next()
                RCP(fin[:, fi, :qw], ps[:, bk, :qw], [bt], [f_t])
                TT(fin[:, fi, :qw], ocp[:, oci, c, :qw], fin[:, fi, :qw], ALU.mult, [oc_t, f_t], [f_t])
                ts_.append((fi, f_t))
            (f0, f0t), (f1, f1t) = ts_
            oi, ob_t = ob_r.next()
            STT(ob[:, oi, :qw], fin[:, f1, :qw], neglam[:, 0:1], fin[:, f0, :qw], ALU.mult, ALU.add,
                [f0t, f1t, c_t], [ob_t])
            DMA('sp', oT_d[:, q0:q0 + qw], ob[:, oi, :qw], [ob_t], (), 's')

        base = 0
        for qi, (q0, qw, kb0, kb1) in enumerate(qtiles):
            n = kb1 - kb0
            if s_r.i % 2:
                s_r.next()
            for j in range(n + LA):
                if j < n:
                    emit_qk(base + j)
                if j - LA >= 0:
                    emit_rest(base + j - LA)
            base += n
        P.emit(nc)
    return nc


NCH = NKEY // 64


def build_H(groups=None, pool_tiles=128, do_pool=True):
    import contextlib
    nc = bass.Bass("TRN2", target_bir_lowering=False)
    P = Prog()
    O = Ops(P)
    MM, TR, ACT, TT, TS, STT, CP, RCP, MS, DMA = O.MM, O.TR, O.ACT, O.TT, O.TS, O.STT, O.CP, O.RCP, O.MS, O.DMA
    if groups is None:
        groups = [(0, 4)] + [(4 + 8 * i, 8) for i in range(32)]

    def din(name, shape, dt=F32):
        return nc.dram_tensor(name, list(shape), dt, kind="ExternalInput").ap()

    hq_d = [din(f"hq{s}", [64, NKEY]) for s in range(2)]
    kk_d = [din(f"kk{s}", [64, NKEY]) for s in range(2)]
    lf_d = [din(f"lf{s}", [64, NKEY]) for s in range(2)]
    vt_d = [din(f"vt{s}", [64, NCH, 64], BF16) for s in range(2)]
    reset_d = din("reset", [64, 512])
    mask_d = din("mask", [64, 64])
    ident_d = din("ident", [64, 64])
    oT_d = [nc.dram_tensor(f"oT{s}", [64, NKEY], F32, kind="ExternalOutput").ap() for s in range(2)]
    if do_pool:
        zp_d = din("zp", [128, 130, 64], BF16)
        band_d = din("band", [5, 128, 128])
        pw_d = din("pw", [64, 64])
        psc_d = din("psc", [64, 1])
        opT_d = nc.dram_tensor("opT", [64, NKEY], BF16, kind="ExternalOutput").ap()

    with contextlib.ExitStack() as st:
        def sb(name, shape, dt=F32):
            return st.enter_context(nc.sbuf_tensor("s_" + name, list(shape), dt))

        reset = sb("reset", [64, 512]); mask = sb("mask", [64, 64]); c_t = Tl(const=True)
        ident = sb("ident", [64, 64], BF16); cb_t = Tl(const=True)
        DMA('sp', reset[:], reset_d, (), [c_t], 'l')
        DMA('sp', mask[:], mask_d, (), [c_t], 'l')
        DMA('pool', ident[:], ident_d, (), [cb_t], 'l')
        ps = st.enter_context(nc.psum_tensor("ps", [128, 8, 512], F32))
        sc_t = Tl(excl=True)
        kt_t = Tl(excl=True)
        ot_t = [Tl(excl=True), Tl(excl=True)]
        kv_t = [Tl(excl=True), Tl(excl=True)]
        kt_bf = ps[:, 1, :].bitcast(BF16)

        S = []
        for s in range(2):
            d = {}
            d['gin'] = sb(f"gin{s}", [64, 2, 3, 512]); d['gin_r'] = Ring(2)
            d['a'] = sb(f"a{s}", [64, 2, 512]); d['a_r'] = Ring(2)
            d['e1'] = sb(f"e1{s}", [64, 2, 512]); d['e1_r'] = Ring(2)
            d['e2'] = sb(f"e2{s}", [64, 2, 512]); d['e2_r'] = Ring(2)
            d['qp'] = sb(f"qp{s}", [64, 2, 512], BF16); d['qp_r'] = Ring(2)
            d['kp'] = sb(f"kp{s}", [64, 2, 512], BF16); d['kp_r'] = Ring(2)
            d['sm'] = sb(f"sm{s}", [64, 2, 512], BF16); d['sm_r'] = Ring(2)
            d['ktok'] = sb(f"ktok{s}", [64, 2, 512], BF16); d['ktok_r'] = Ring(2)
            d['sc1'] = sb(f"sc1{s}", [64, 2, 8]); d['sc1_r'] = Ring(2)
            d['ser'] = sb(f"ser{s}", [64, 2, 8]); d['ser_r'] = Ring(2)
            d['v'] = sb(f"v{s}", [64, NCH, 64], BF16); d['v_t'] = Tl(const=True)
            d['state'] = sb(f"state{s}", [64, 2, 64]); d['state_t'] = [Tl(), Tl()]; d['sp'] = 0
            d['sr'] = sb(f"sr{s}", [64, 64], BF16); d['sr_t'] = Tl()
            d['ost'] = sb(f"ost{s}", [64, 2, 512]); d['ost_r'] = Ring(2)
            MS(d['state'][:], 0.0, d['state_t'])
            for t_ in d['sm_r'].t:
                pass
            MS(d['sm'][:], 0.0, d['sm_r'].t)
            for i in range(0, NCH, 52):
                DMA('sp', d['v'][:, i:i + 52, :], vt_d[s][:, i:i + 52, :], (), [d['v_t']], 'l')
            S.append(d)

        def pre(s, g, out):
            d = S[s]
            c0, n = groups[g]
            W = 64 * n
            t0 = 64 * c0
            gi, g_t = d['gin_r'].next()
            for j, src in enumerate((hq_d[s], kk_d[s], lf_d[s])):
                DMA('sp', d['gin'][:, gi, j, :W], src[:, t0:t0 + W], (), [g_t], 'l')
            yield
            ai, a_t = d['a_r'].next()
            a = d['a'][:, ai, :W]
            P.op('dve', 'tensor_tensor_scan', dict(out=a, data0=reset[:, :W], data1=d['gin'][:, gi, 2, :W], initial=0.0,
                                                   op0=ALU.mult, op1=ALU.add), [g_t, c_t], [a_t])
            a3 = a.rearrange("p (c t) -> p c t", t=64)
            yield
            s1i, s1_t = d['sc1_r'].next()
            eri, er_t = d['ser_r'].next()
            ACT(d['sc1'][:, s1i, :n], a3[:, :, 63], AF.Exp, [a_t], [s1_t])
            ACT(d['ser'][:, eri, :n], a3[:, :, 31], AF.Exp, [a_t], [er_t])
            e2i, e2_t = d['e2_r'].next()
            dd = d['e2'][:, e2i, :W]
            TT(dd.rearrange("p (c t) -> p c t", t=64), a3, a3[:, :, 31:32].to_broadcast([64, n, 64]), ALU.subtract,
               [a_t], [e2_t])
            yield
            e1i, e1_t = d['e1_r'].next()
            e1 = d['e1'][:, e1i, :W]
            ACT(e1, dd, AF.Exp, [e2_t], [e1_t])
            ACT(dd, dd, AF.Exp, [e2_t], [e2_t], scale=-1.0)
            yield
            qi, q_t = d['qp_r'].next()
            ki, k_t = d['kp_r'].next()
            TT(d['qp'][:, qi, :W], d['gin'][:, gi, 0, :W], e1, ALU.mult, [g_t, e1_t], [q_t])
            TT(d['kp'][:, ki, :W], d['gin'][:, gi, 1, :W], dd, ALU.mult, [g_t, e2_t], [k_t])
            yield
            for c in range(n):
                MM(ps[0:64, 0, c * 64 + 32:(c + 1) * 64], d['kp'][:, ki, c * 64:(c + 1) * 64],
                   d['qp'][:, qi, c * 64 + 32:(c + 1) * 64], True, True, [k_t, q_t], [sc_t])
                MM(ps[0:32, 0, c * 64:c * 64 + 32], d['kp'][:, ki, c * 64:c * 64 + 32],
                   d['qp'][:, qi, c * 64:c * 64 + 32], True, True, [k_t, q_t], [sc_t])
            for c in range(n):
                TR(kt_bf[0:64, c * 64:(c + 1) * 64], d['kp'][:, ki, c * 64:(c + 1) * 64], ident[:], [k_t, cb_t], [kt_t])
            smi, sm_t = d['sm_r'].next()
            sm3 = d['sm'][:, smi, :W].rearrange("p (c t) -> p c t", t=64)
            sc3 = ps[0:64, 0, :W].rearrange("p (c t) -> p c t", t=64)
            TT(sm3[:, :, 32:64], sc3[:, :, 32:64], mask[:, 32:64].unsqueeze(1).to_broadcast([64, n, 32]), ALU.mult,
               [sc_t, c_t], [sm_t])
            TT(sm3[0:32, :, 0:32], sc3[0:32, :, 0:32], mask[0:32, 0:32].unsqueeze(1).to_broadcast([32, n, 32]),
               ALU.mult, [sc_t, c_t, sm_t], [sm_t])
            kti, kt2_t = d['ktok_r'].next()
            CP(d['ktok'][:, kti, :W], kt_bf[0:64, :W], [kt_t], [kt2_t])
            out.update(dict(c0=c0, n=n, W=W, t0=t0, qi=qi, q_t=q_t, smi=smi, sm_t=sm_t, kti=kti, kt2_t=kt2_t,
                            s1i=s1i, s1_t=s1_t, eri=eri, er_t=er_t, e1i=e1i, e1_t=e1_t))

        def step(s, pr, c):
            d = S[s]
            ch = pr['c0'] + c
            cs_ = slice(c * 64, (c + 1) * 64)
            po = d['sp']
            pn = 1 - po
            d['sp'] = pn
            st_o, st_n = d['state'][:, po, :], d['state'][:, pn, :]
            so_t, sn_t = d['state_t'][po], d['state_t'][pn]
            ACT(d['sr'][:], st_o, AF.Copy, [so_t, pr['er_t']], [d['sr_t']],
                scale=d['ser'][:, pr['eri'], c:c + 1])
            MM(ps[0:64, 2 + s, cs_], d['v'][:, ch, :], d['sm'][:, pr['smi'], cs_], True, False,
               [d['v_t'], pr['sm_t']], [ot_t[s]])
            MM(ps[0:64, 2 + s, cs_], d['sr'][:], d['qp'][:, pr['qi'], cs_], False, True,
               [d['sr_t'], pr['q_t']], [ot_t[s]])
            MM(ps[0:64, 4 + s, 0:64], d['ktok'][:, pr['kti'], cs_], d['v'][:, ch, :], True, True,
               [pr['kt2_t'], d['v_t']], [kv_t[s]])
            TS(st_n, st_o, d['sc1'][:, pr['s1i'], c:c + 1], ALU.mult, [so_t, pr['s1_t']], [sn_t])
            STT(st_n, ps[0:64, 4 + s, 0:64], d['e1'][:, pr['e1i'], c * 64 + 63:c * 64 + 64], st_n,
                ALU.mult, ALU.add, [kv_t[s], pr['e1_t'], sn_t], [sn_t])

        def fin(s, pr):
            d = S[s]
            W = pr['W']
            oi, o_t = d['ost_r'].next()
            ACT(d['ost'][:, oi, :W], ps[0:64, 2 + s, :W], AF.Copy, [ot_t[s]], [o_t])
            DMA('pool', oT_d[s][:, pr['t0']:pr['t0'] + W], d['ost'][:, oi, :W], [o_t], (), 's')

        def drain(gens):
            for gg in gens:
                for _ in gg:
                    pass

        prs = [{}, {}]
        drain([pre(0, 0, prs[0]), pre(1, 0, prs[1])])
        for g in range(len(groups)):
            nxt, gens = None, []
            if g + 1 < len(groups):
                nxt = [{}, {}]
                gens = [pre(0, g + 1, nxt[0]), pre(1, g + 1, nxt[1])]
            for c in range(groups[g][1]):
                step(0, prs[0], c)
                step(1, prs[1], c)
                for gg in gens:
                    next(gg, None)
            drain(gens)
            fin(0, prs[0])
            fin(1, prs[1])
            prs = nxt

        if do_pool:
            zp = sb("zp", [128, 130, 64], BF16); zp_t = Tl(const=True)
            band = sb("band", [128, 5, 128], BF16); pw = sb("pw", [64, 64], BF16); pc_t = Tl(const=True)
            psc = sb("psc", [64, 1]); pcs_t = Tl(const=True)
            mxb = sb("mxb", [64, 2, 512], BF16); mxb_r = Ring(2)
            pob = sb("pob", [64, 2, 512], BF16); pob_r = Ring(2)
            for i in range(0, 130, 26):
                DMA('sp', zp[:, i:i + 26, :], zp_d[:, i:i + 26, :], (), [zp_t], 'l')
            for i in range(5):
                DMA('pool', band[:, i, :], band_d[i], (), [pc_t], 'l')
            DMA('pool', pw[:], pw_d, (), [pc_t], 'l')
            DMA('sp', psc[:], psc_d, (), [pcs_t], 'l')
            mx_t = Tl(excl=True)
            py_t = Tl(excl=True)
            seqs = [(0, pool_tiles, 256 // 1)] if False else None
            plan = []
            for (tb, nt, tok0) in ((0, pool_tiles, CTX), (128, 2, 0)):
                for g0 in range(0, nt, 4):
                    plan.append((tb, nt, tok0, g0, min(4, nt - g0)))
            for (tb, nt, tok0, g0, gn) in plan:
                for j in range(gn):
                    ti = g0 + j
                    terms = []
                    if ti == 0:
                        terms.append((ti, 1))
                    elif ti == nt - 1:
                        terms.append((ti, 2))
                    else:
                        terms.append((ti, 0))
                    if ti > 0:
                        terms.append((ti - 1, 3))
                    if ti < nt - 1:
                        terms.append((ti + 1, 4))
                    for k, (src, bi) in enumerate(terms):
                        MM(ps[0:64, 6, j * 128:(j + 1) * 128], zp[:, tb + src, :], band[:, bi, :], k == 0,
                           k == len(terms) - 1, [zp_t, pc_t], [mx_t])
                mi, m_t = mxb_r.next()
                CP(mxb[:, mi, :gn * 128], ps[0:64, 6, :gn * 128], [mx_t], [m_t])
                MM(ps[0:64, 7, :gn * 128], pw[:], mxb[:, mi, :gn * 128], True, True, [pc_t, m_t], [py_t])
                pi, p_t = pob_r.next()
                ACT(pob[:, pi, :gn * 128], ps[0:64, 7, :gn * 128], AF.Copy, [py_t, pcs_t], [p_t], scale=psc[:, 0:1])
                DMA('pool', opT_d[:, tok0 + g0 * 128:tok0 + (g0 + gn) * 128], pob[:, pi, :gn * 128], [p_t], (), 's')
        P.emit(nc)
    return nc


def h_consts():
    reset = np.ones((64, 512), np.float32)
    reset[:, ::64] = 0.0
    s = np.arange(64)
    mask = (s[:, None] <= s[None, :]).astype(np.float32)
    return dict(reset=reset, mask=mask, ident=np.eye(64, dtype=np.float32))


def zp_tiles(z_lat, z_ctx):
    a = z_lat.reshape(-1, 128, 64).transpose(1, 0, 2)
    b = z_ctx.reshape(-1, 128, 64).transpose(1, 0, 2)
    return np.ascontiguousarray(np.concatenate([a, b], 1))


def band_mats(w):
    h = w // 2
    n = 128 * 3
    t = np.arange(n)
    full = np.zeros((n, n), np.float64)
    for tt in range(n):
        lo, hi = tt - h, tt + h
        for ss in range(max(lo, 0), min(hi, n)):
            full[ss, tt] = 1.0 / w
    Bc = full[128:256, 128:256] - np.eye(128)
    Bp = full[0:128, 128:256]
    Bn = full[256:384, 128:256]
    first = np.zeros((128, 128))
    last = np.zeros((128, 128))
    for tt in range(128):
        lo, hi = max(tt - h, 0), tt + h
        cnt = hi - lo
        for ss in range(lo, min(hi, 128)):
            first[ss, tt] = 1.0 / cnt
        lo2, hi2 = tt - h, min(tt + h, 128)
        cnt2 = hi2 - lo2
        for ss in range(max(lo2, 0), hi2):
            last[ss, tt] = 1.0 / cnt2
    first -= np.eye(128)
    last -= np.eye(128)
    return np.ascontiguousarray(np.stack([Bc, first, last, Bp, Bn]).astype(np.float32))


_PROGS = {}


def _prog(key, fn):
    if key not in _PROGS:
        _PROGS[key] = fn()
    return _PROGS[key]


def _run(nc, maps):
    res = run_bass_kernel_spmd(nc, maps, core_ids=list(range(NCORE)))
    return [{k: np.asarray(v) for k, v in r.items()} for r in res.results]


def _gather_tok(rs, b, key, rows):
    lat = np.concatenate([rs[b * 4 + q][key][rows, :TLAT] for q in range(4)], 1)
    ctx = np.concatenate([rs[b * 4 + q][key][rows, TLAT:] for q in range(4)], 1)
    return lat, ctx


def _mixer_inputs(inp, rs, l):
    lam_init = 0.8 - 0.6 * float(np.exp(-0.3 * l))
    lamv = np.stack([inp['da_lambda_q1'][l], inp['da_lambda_k1'][l], inp['da_lambda_q2'][l], inp['da_lambda_k2'][l]])
    hc = h_consts()
    ones = np.ones((128, 128), np.float32)
    mapsA, mapsH = [], []
    for c in range(NCORE):
        b, h = core_bq(c)
        r128 = slice(h * 128, (h + 1) * 128)
        r64 = slice(h * 64, (h + 1) * 64)
        ql, qc = _gather_tok(rs, b, 'qT', r128)
        kl, kc = _gather_tok(rs, b, 'kT', r128)
        vl, vc = _gather_tok(rs, b, 'vT', r128)
        vtok = np.concatenate([vl, vc], 1).T
        dA = dict(qT=np.ascontiguousarray(np.concatenate([ql, qc], 1)),
                  kT=np.ascontiguousarray(np.concatenate([kl, kc], 1)),
                  v=np.ascontiguousarray(vtok.reshape(130, 128, 128).transpose(1, 0, 2)),
                  lamv=np.ascontiguousarray(np.broadcast_to(lamv[None], (128, 4, 64))).astype(np.float32),
                  lami=np.ascontiguousarray(np.broadcast_to(
                      np.array([lam_init, 1.0 - lam_init], np.float32)[None], (128, 2))),
                  gain=np.ascontiguousarray(inp['da_subln'][l][:, None]), ones=ones)
        mapsA.append(dA)
        dH = dict(hc)

        def scan_order(key, flip):
            lat, ctx = _gather_tok(rs, b, key, r64)
            if flip:
                lat, ctx = lat[:, ::-1], ctx[:, ::-1]
            return np.ascontiguousarray(np.concatenate([ctx, lat], 1))

        for s, (kkey, lkey) in enumerate((('kkfT', 'lffT'), ('kkbT', 'lfbT'))):
            dH[f'hq{s}'] = scan_order('hqT', s == 1)
            dH[f'kk{s}'] = scan_order(kkey, s == 1)
            dH[f'lf{s}'] = scan_order(lkey, s == 1)
            vi = scan_order('hiT', s == 1).T
            dH[f'vt{s}'] = np.ascontiguousarray(vi.reshape(NCH, 64, 64).transpose(1, 0, 2))
        zl, zc = _gather_tok(rs, b, 'zpT', r64)
        dH['zp'] = zp_tiles(np.ascontiguousarray(zl.T), np.ascontiguousarray(zc.T))
        dH['band'] = band_mats(2 ** (h + 1))
        dH['pw'] = np.ascontiguousarray(inp['pool_w'][l][h])
        dH['psc'] = np.ascontiguousarray(inp['pool_scale'][l][r64][:, None])
        mapsH.append(dH)
    return mapsA, mapsH


def _merge_inputs(rs, rA, rH, with_ctx):
    out = []
    for c in range(NCORE):
        b, q = core_bq(c)
        lat = slice(q * TLAT, (q + 1) * TLAT)
        cx = slice(q * TCTX, (q + 1) * TCTX)

        def cat(lat_part, ctx_part):
            return np.ascontiguousarray(np.concatenate([lat_part, ctx_part], 1) if with_ctx else lat_part)

        oda = [cat(rA[b * 4 + h]['oT'][:, lat], rA[b * 4 + h]['oT'][:, SEQ + q * TCTX:SEQ + (q + 1) * TCTX]) for h in range(4)]
        ohf, ohb, opl = [], [], []
        for h in range(4):
            r = rH[b * 4 + h]
            f = r['oT0']
            ohf.append(cat(f[:, CTX:][:, lat], f[:, :CTX][:, cx]))
            bw = r['oT1']
            ohb.append(cat(bw[:, CTX:][:, ::-1][:, lat], bw[:, :CTX][:, ::-1][:, cx]))
            p = r['opT']
            opl.append(cat(p[:, CTX:][:, lat], p[:, :CTX][:, cx]))
        n = NT if with_ctx else TLAT
        d = dict(odaT=np.concatenate(oda, 0), ohfT=np.concatenate(ohf, 0), ohbT=np.concatenate(ohb, 0),
                 opoolT=np.concatenate(opl, 0),
                 sgT_in=np.ascontiguousarray(rs[c]['sgT'][:, :n]),
                 gateT_in=np.ascontiguousarray(rs[c]['gateT'][:, :n]),
                 hT_in=np.ascontiguousarray(rs[c]['hT_out'][:, :n]))
        out.append(d)
    return out


def kernel(**inputs):
    return _forward(inputs)


def _forward(inputs, dbg=None):
    inp = {k: np.asarray(v) for k, v in inputs.items()}
    dbg = dbg or (lambda name, val: None)
    H = HostW(inp)
    ncT1 = _prog('T1', lambda: build_T(None, 0, False, True))
    maps = []
    for c in range(NCORE):
        d = H.common(c, [0])
        d.update(H.pre(c, 0))
        d['hT_in'] = initial_hT(inp, c)
        maps.append(d)
    r1 = _run(ncT1, maps)
    dbg('r1', r1)
    ncA = _prog('A', build_A)
    ncH = _prog('H', build_H)
    mA, mH = _mixer_inputs(inp, r1, 0)
    rA = _run(ncA, mA)
    dbg('rA0', rA)
    rH = _run(ncH, mH)
    dbg('rH0', rH)
    del mA, mH
    ncT2 = _prog('T2', lambda: build_T(0, 1, False, True))
    mm = _merge_inputs(r1, rA, rH, True)
    maps = []
    for c in range(NCORE):
        d = H.common(c, [0, 1])
        d.update(H.mrg(c, 0))
        d.update(H.pre(c, 1))
        d.update(mm[c])
        maps.append(d)
    del r1, rA, rH
    r2 = _run(ncT2, maps)
    dbg('r2', r2)
    mA, mH = _mixer_inputs(inp, r2, 1)
    rA = _run(ncA, mA)
    dbg('rA1', rA)
    rH = _run(ncH, mH)
    dbg('rH1', rH)
    del mA, mH
    ncT3 = _prog('T3', lambda: build_T(1, None, True, False))
    mm = _merge_inputs(r2, rA, rH, False)
    maps = []
    for c in range(NCORE):
        d = H.common(c, [1])
        d.update(H.mrg(c, 1))
        d.update(mm[c])
        d['fnorm'] = vec_p(inp['final_norm'])
        maps.append(d)
    r3 = _run(ncT3, maps)
    out = np.empty((BATCH, SEQ, D), np.float32)
    for c in range(NCORE):
        b, q = core_bq(c)
        out[b, q * TLAT:(q + 1) * TLAT] = r3[c]['outT'].T
    return out
```

```python
import numpy as np
import ml_dtypes
import concourse.bass as bass
import concourse.mybir as mybir
from concourse.bass_utils import run_bass_kernel_spmd

F32 = mybir.dt.float32
BF16 = mybir.dt.bfloat16
AF = mybir.ActivationFunctionType
ALU = mybir.AluOpType
NPBF = ml_dtypes.bfloat16

D = 1024
SEQ = 16384
BATCH = 2
CTX = 256
DFF = 2816
DIN = 6144
NCORE = 8
TLAT = 4096
TCTX = 64
NT = TLAT + TCTX
NKEY = SEQ + CTX
EPS = 1e-6

ENGS = ('pe', 'act', 'dve', 'pool', 'sp')


class Tl:
    __slots__ = ('w', 'r', 'const', 'lsem', 'ssem', 'excl')

    def __init__(self, const=False, excl=False):
        self.w = None
        self.r = []
        self.const = const
        self.excl = excl
        self.lsem = None
        self.ssem = None


class Op:
    __slots__ = ('eng', 'fn', 'deps', 'sig', 'val', 'dsem')


class Prog:
    def __init__(self):
        self.ops = {e: [] for e in ENGS}
        self.dcnt = {}
        self.deng = {}

    def op(self, eng, meth, kw, reads=(), writes=(), dsem=None):
        o = Op()
        o.eng = eng
        o.fn = (meth, kw)
        o.sig = False
        o.val = 0
        o.dsem = dsem
        wr = {}
        rd = {}
        for t in reads:
            if t.w is not None:
                wr[id(t.w)] = t.w
            if t.excl:
                for r in t.r:
                    if r.eng != eng:
                        wr[id(r)] = r
        for t in writes:
            if t.w is not None:
                wr[id(t.w)] = t.w
            for r in t.r:
                rd[id(r)] = r
        deps = []
        for d in wr.values():
            if d.dsem is None and dsem is None and d.eng == eng and eng == 'pe':
                continue
            deps.append(d)
        for d in rd.values():
            if id(d) in wr:
                continue
            if d.dsem is None and dsem is None and d.eng == eng and eng == 'pe':
                continue
            deps.append(d)
        for d in deps:
            if d.dsem is None:
                d.sig = True
        o.deps = deps
        for t in reads:
            if not t.const:
                t.r.append(o)
        for t in writes:
            t.w = o
            t.r = []
        if dsem is not None:
            if len(writes) > 0:
                t = writes[0]
                if t.lsem is None:
                    t.lsem = 'l%d' % len(self.dcnt)
                    self.dcnt[t.lsem] = 0
                dsem = t.lsem
            else:
                t = reads[0]
                if t.ssem is None:
                    t.ssem = 's%d' % len(self.dcnt)
                    self.dcnt[t.ssem] = 0
                dsem = t.ssem
            o.dsem = dsem
            self.dcnt[dsem] = self.dcnt[dsem] + 16
            o.val = self.dcnt[dsem]
        self.ops[eng].append(o)
        return o

    def emit(self, nc):
        for e in ENGS:
            c = 0
            for o in self.ops[e]:
                if o.dsem is None and o.sig:
                    c += 1
                    o.val = c
        import contextlib
        with contextlib.ExitStack() as st:
            esem = {e: st.enter_context(nc.semaphore('es_' + e)) for e in ENGS}
            dsem = {k: st.enter_context(nc.semaphore('ds_' + k)) for k in self.dcnt}
            block = st.enter_context(nc.Block())
            prog = self

            def run(eng_name, e):
                waited = {}
                for o in prog.ops[eng_name]:
                    for d in o.deps:
                        if d.dsem is not None:
                            key = ('d', d.dsem)
                            sem = dsem[d.dsem]
                        else:
                            key = ('e', d.eng)
                            sem = esem[d.eng]
                        if waited.get(key, 0) >= d.val:
                            continue
                        waited[key] = d.val
                        e.wait_ge(sem, d.val)
                    ins = getattr(e, o.fn[0])(**o.fn[1])
                    if o.dsem is not None:
                        ins.then_inc(dsem[o.dsem], 16)
                    elif o.sig:
                        ins.then_inc(esem[eng_name], 1)
                if eng_name == 'sp':
                    for k, v in prog.dcnt.items():
                        e.wait_ge(dsem[k], v)

            @block.tensor
            def _(e):
                run('pe', e)

            @block.scalar
            def _(e):
                run('act', e)

            @block.vector
            def _(e):
                run('dve', e)

            @block.gpsimd
            def _(e):
                run('pool', e)

            @block.sync
            def _(e):
                run('sp', e)


class Ring:
    def __init__(self, n, excl=False):
        self.n = n
        self.i = 0
        self.t = [Tl(excl=excl) for _ in range(n)]

    def next(self):
        k = self.i % self.n
        self.i += 1
        return k, self.t[k]


def arr_w(W):
    K, N = W.shape
    return np.ascontiguousarray(W.reshape(K // 128, 128, N // 128, 128).transpose(2, 1, 0, 3))


def vec_p(v):
    return np.ascontiguousarray(v.reshape(-1, 128).T)


class Ops:
    def __init__(self, P):
        self.P = P

    def MM(self, out, lhsT, rhs, start, stop, reads, writes):
        self.P.op('pe', 'matmul', dict(out=out, lhsT=lhsT, rhs=rhs, start=start, stop=stop), reads, writes)

    def TR(self, out, in_, ident, reads, writes):
        self.P.op('pe', 'transpose', dict(out=out, in_=in_, identity=ident), reads, writes)

    def ACT(self, out, in_, func, reads, writes, **kw):
        self.P.op('act', 'activation', dict(out=out, in_=in_, func=func, **kw), reads, writes)

    def TT(self, out, in0, in1, op, reads, writes, eng='dve'):
        self.P.op(eng, 'tensor_tensor', dict(out=out, in0=in0, in1=in1, op=op), reads, writes)

    def TS(self, out, in0, s1, op0, reads, writes, s2=None, op1=None, eng='dve'):
        kw = dict(out=out, in0=in0, scalar1=s1, scalar2=s2, op0=op0)
        if op1 is not None:
            kw['op1'] = op1
        self.P.op(eng, 'tensor_scalar', kw, reads, writes)

    def STT(self, out, in0, scalar, in1, op0, op1, reads, writes):
        self.P.op('dve', 'scalar_tensor_tensor', dict(out=out, in0=in0, scalar=scalar, in1=in1, op0=op0, op1=op1),
                  reads, writes)

    def CP(self, out, in_, reads, writes, eng='dve'):
        self.P.op(eng, 'tensor_copy', dict(out=out, in_=in_), reads, writes)

    def RCP(self, out, in_, reads, writes, scratch=None):
        if scratch is None:
            self.P.op('dve', 'reciprocal', dict(out=out, in_=in_), reads, writes)
        else:
            self.P.op('dve', 'reciprocal_approx_accurate', dict(out=out, in_=in_, scratch=scratch), reads, writes)

    def MS(self, ap, val, writes, eng='dve'):
        self.P.op(eng, 'memset', dict(ap=ap, constant=val), (), writes)

    def DMA(self, eng, out, in_, reads, writes, dsem):
        self.P.op(eng, 'dma_start', dict(out=out, in_=in_), reads, writes, dsem)


def build_T(merge_layer, pre_layer, final, with_ctx, tiles_override=None, dbg=99):
    import contextlib
    nc = bass.Bass("TRN2", target_bir_lowering=False)
    P = Prog()
    O = Ops(P)
    MM, ACT, TT, TS, STT, CP, RCP, MS, DMA = O.MM, O.ACT, O.TT, O.TS, O.STT, O.CP, O.RCP, O.MS, O.DMA
    ntok = NT if with_ctx else TLAT

    def din(name, shape, dt=F32):
        return nc.dram_tensor(name, list(shape), dt, kind="ExternalInput").ap()

    def dout(name, shape, dt=F32):
        return nc.dram_tensor(name, list(shape), dt, kind="ExternalOutput").ap()

    hT_in = din("hT_in", [D, ntok])
    cs_d = din("cs", [128, 8, 2])
    ones_d = din("ones", [128, 128])
    bd64_d = din("bd64", [128, 128])
    rperm_d = din("rperm", [128, 128])
    layers = sorted(set(x for x in (merge_layer, pre_layer) if x is not None))
    wada_d = {l: din(f"wada{l}", [72, 128, 8, 128]) for l in layers}
    bada_d = {l: din(f"bada{l}", [128, 72]) for l in layers}
    if merge_layer is not None:
        ml = merge_layer
        nf2_d = din("nf2", [128, 8])
        f2w1_d = din("f2w1", [22, 128, 8, 128])
        f2w3_d = din("f2w3", [22, 128, 8, 128])
        f2w2_d = din("f2w2", [8, 128, 22, 128])
        wmrg_d = din("wmrg", [8, 128, 8, 128])
        wout_d = din("wout", [8, 128, 8, 128])
        hgn_d = din("hgn", [128, 1])
        dag_d = din("dag", [128, 1])
        odaT_d = din("odaT", [512, ntok], BF16)
        ohfT_d = din("ohfT", [256, ntok])
        ohbT_d = din("ohbT", [256, ntok])
        sgT_in_d = din("sgT_in", [256, ntok])
        opoolT_d = din("opoolT", [256, ntok], BF16)
        gateT_in_d = din("gateT_in", [3072, ntok])
    if pre_layer is not None:
        pl = pre_layer
        nf1_d = din("nf1", [128, 8])
        nmix_d = din("nmix", [128, 8])
        f1w1_d = din("f1w1", [22, 128, 8, 128])
        f1w3_d = din("f1w3", [22, 128, 8, 128])
        f1w2_d = din("f1w2", [8, 128, 22, 128])
        win_d = din("win", [48, 128, 8, 128])
        lbl_d = din("lbl", [128, 4, 2])
        cosT_d = din("cosT", [128, ntok])
        sinT_d = din("sinT", [128, ntok])
        hT_out = dout("hT_out", [D, ntok])
        qT_d = dout("qT", [512, ntok], BF16)
        kT_d = dout("kT", [512, ntok], BF16)
        vT_d = dout("vT", [512, ntok], BF16)
        hqT_d = dout("hqT", [256, ntok])
        kkfT_d = dout("kkfT", [256, ntok])
        kkbT_d = dout("kkbT", [256, ntok])
        lffT_d = dout("lffT", [256, ntok])
        lfbT_d = dout("lfbT", [256, ntok])
        hiT_d = dout("hiT", [256, ntok], BF16)
        sgT_d = dout("sgT", [256, ntok])
        zpT_d = dout("zpT", [256, ntok], BF16)
        gateT_d = dout("gateT", [3072, ntok])
    if final:
        fnorm_d = din("fnorm", [128, 8])
        outT_d = dout("outT", [D, TLAT])

    with contextlib.ExitStack() as st:
        def sb(name, shape, dt=F32):
            return st.enter_context(nc.sbuf_tensor("s_" + name, list(shape), dt))

        TS_ = 1024
        hT = sb("hT", [128, 8, TS_]); hT_t = [[Tl() for _ in range(2)] for _ in range(8)]
        xn = sb("xn", [128, 8, TS_], BF16); xn_t = [[Tl() for _ in range(2)] for _ in range(8)]
        hid = sb("hid", [128, 22, TS_], BF16); hid_t = [[Tl() for _ in range(2)] for _ in range(22)]
        NW8 = 6 if (merge_layer is not None and pre_layer is not None) else 8
        wk8 = sb("wk8", [128, NW8, 8, 128], BF16); wk8_r = Ring(NW8)
        NW22 = 3
        wk22 = sb("wk22", [128, NW22, 22, 128], BF16); wk22_r = Ring(NW22)
        tmpf = sb("tmpf", [128, 4, 512]); tmpf_r = Ring(4)
        sqb = sb("sqb", [128, 3, 512], BF16); sqb_r = Ring(3)
        rstd = sb("rstd", [128, 2, 512]); rstd_r = Ring(2)
        stgf = sb("stgf", [128, 4, 512]); stgf_r = Ring(4)
        stgb = sb("stgb", [128, 4, 512], BF16); stgb_r = Ring(4)
        ps = st.enter_context(nc.psum_tensor("ps", [128, 8, 512], F32)); ps_r = Ring(8, excl=True)
        ones_b = sb("ones_b", [128, 128], BF16); ones_t = Tl(const=True)
        bd64_b = sb("bd64_b", [128, 128], BF16); bd64_t = Tl(const=True)
        rperm_b = sb("rperm_b", [128, 128], BF16); rperm_t = Tl(const=True)
        cs = sb("cs", [128, 8, 2]); cs_t = Tl(const=True)
        wada = sb("wada", [128, 2, 8, 128]); wada_r = Ring(2)
        mod = {l: sb(f"mod{l}", [128, 72, 2]) for l in layers}; mod_t = Tl(const=True)
        bada = {l: sb(f"bada{l}", [128, 72]) for l in layers}
        vec_t = Tl(const=True)
        epsb = sb("epsb", [128, 1])
        oneb = sb("oneb", [128, 1])
        MS(epsb[:], EPS, [vec_t])
        MS(oneb[:], 1.0, [vec_t])

        DMA('pool', ones_b[:], ones_d, (), [ones_t], 'w')
        DMA('pool', bd64_b[:], bd64_d, (), [bd64_t], 'w')
        DMA('pool', rperm_b[:], rperm_d, (), [rperm_t], 'w')
        DMA('sp', cs[:], cs_d, (), [cs_t], 'l')
        cs_flat = cs[:].rearrange("p a b -> p (a b)")
        ACT(cs_flat, cs_flat, AF.Silu, [cs_t], [cs_t])

        for l in layers:
            DMA('sp', bada[l][:], bada_d[l], (), [mod_t], 'l')
            bk, bt = ps_r.next()
            need = set()
            if l == pre_layer:
                need.update(range(0, 40))
            if l == merge_layer:
                need.update(range(40, 72))
            MS(mod[l][:], 0.0, [mod_t])
            for fc in sorted(need):
                wi, wt = wada_r.next()
                DMA('sp', wada[:, wi], wada_d[l][fc], (), [wt], 'l')
                for kc in range(8):
                    MM(ps[:, bk, 2 * fc:2 * fc + 2], wada[:, wi, kc, :], cs[:, kc, :], kc == 0, kc == 7,
                       [wt, cs_t], [bt])
            lo, hi = min(need), max(need) + 1
            TT(mod[l][:, lo:hi, :], ps[:, bk, 2 * lo:2 * hi].rearrange("p (a b) -> p a b", b=2),
               bada[l][:, lo:hi].unsqueeze(2).to_broadcast([128, hi - lo, 2]), ALU.add, [bt, mod_t], [mod_t])

        def mv(l, k):
            return mod[l][:, 8 * k:8 * k + 8, :]

        vecs = {}

        def mk_AB(name, gain_d, l, kshift, kscale):
            g = sb("g_" + name, [128, 8])
            A = sb("A_" + name, [128, 8, 2])
            B = sb("B_" + name, [128, 8, 2])
            vecs[name + 'A'] = A
            vecs[name + 'B'] = B
            DMA('sp', g[:], gain_d, (), [vec_t], 'l')
            TS(A[:], mv(l, kscale), 1.0, ALU.add, [mod_t, vec_t], [vec_t])
            TT(A[:], A[:], g[:].unsqueeze(2).to_broadcast([128, 8, 2]), ALU.mult, [vec_t], [vec_t])
            CP(B[:], mv(l, kshift), [mod_t, vec_t], [vec_t])

        def mk_gate(name, l, k, mul):
            G = sb("G_" + name, [128, 8, 2])
            vecs[name] = G
            TS(G[:], mv(l, k), float(mul), ALU.mult, [mod_t, vec_t], [vec_t])

        if merge_layer is not None:
            mk_gate('g5', ml, 5, 1.0)
            mk_AB('f2', nf2_d, ml, 6, 7)
            mk_gate('g8', ml, 8, 0.5)
            hgn = sb("hgn", [128, 1])
            DMA('sp', hgn[:], hgn_d, (), [vec_t], 'l')
            dagn = sb("dagn", [128, 1])
            DMA('sp', dagn[:], dag_d, (), [vec_t], 'l')
            TS(dagn[:], dagn[:], float(1.0 - (0.8 - 0.6 * np.exp(-0.3 * ml))), ALU.mult, [vec_t], [vec_t])
        if pre_layer is not None:
            mk_AB('f1', nf1_d, pl, 0, 1)
            mk_gate('g2', pl, 2, 0.5)
            mk_AB('mx', nmix_d, pl, 3, 4)
            oml = sb("oml", [128, 4])
            if pl == 0:
                MS(oml[:], 1.0, [vec_t])
            else:
                lbl = sb("lbl", [128, 4, 2])
                DMA('sp', lbl[:], lbl_d, (), [vec_t], 'l')
                TT(oml[:], lbl[:, :, 0], lbl[:, :, 1], ALU.subtract, [vec_t], [vec_t])
                ACT(oml[:], oml[:], AF.Sigmoid, [vec_t], [vec_t])
        if final:
            fng = sb("fng", [128, 8])
            DMA('sp', fng[:], fnorm_d, (), [vec_t], 'l')

        def load_w8(src):
            wi, wt = wk8_r.next()
            DMA('pool', wk8[:, wi], src, (), [wt], 'w')
            return wi, wt

        def load_w22(src):
            wi, wt = wk22_r.next()
            DMA('pool', wk22[:, wi].rearrange("p a b -> p (a b)").rearrange("p (c d) -> p c d", d=704),
                src.rearrange("p a b -> p (a b)").rearrange("p (c d) -> p c d", d=704), (), [wt], 'w')
            return wi, wt

        def blocks(ts):
            bw = min(512, ts)
            return [(i * bw, bw) for i in range(ts // bw)]

        def sumsq_rstd(srcs, bw, grp_lhsT, grp_t, nfeat, use_ln=True):
            bk, bt = ps_r.next()
            nk = len(srcs)
            for k, (sap, stl) in enumerate(srcs):
                si, s_t = sqb_r.next()
                ACT(sqb[:, si, :bw], sap, AF.Square, [stl], [s_t])
                MM(ps[:, bk, :bw], grp_lhsT, sqb[:, si, :bw], k == 0, k == nk - 1, [s_t, grp_t], [bt])
            ri, r_t = rstd_r.next()
            if use_ln:
                ACT(rstd[:, ri, :bw], ps[:, bk, :bw], AF.Ln, [bt, vec_t], [r_t], scale=1.0 / nfeat, bias=epsb[:, 0:1])
                ACT(rstd[:, ri, :bw], rstd[:, ri, :bw], AF.Exp, [r_t], [r_t], scale=-0.5)
            else:
                ACT(rstd[:, ri, :bw], ps[:, bk, :bw], AF.Sqrt, [bt, vec_t], [r_t], scale=1.0 / nfeat, bias=epsb[:, 0:1])
                RCP(rstd[:, ri, :bw], rstd[:, ri, :bw], [r_t], [r_t])
            return ri, r_t

        def norm_mod(ts, A, B, ci):
            for bi, (t0, bw) in enumerate(blocks(ts)):
                ri, r_t = sumsq_rstd([(hT[:, k, t0:t0 + bw], hT_t[k][bi]) for k in range(8)], bw,
                                     ones_b[:], ones_t, D)
                for k in range(8):
                    ti, t_t = tmpf_r.next()
                    TT(tmpf[:, ti, :bw], hT[:, k, t0:t0 + bw], rstd[:, ri, :bw], ALU.mult, [hT_t[k][bi], r_t], [t_t])
                    ACT(xn[:, k, t0:t0 + bw], tmpf[:, ti, :bw], AF.Identity, [t_t, vec_t], [xn_t[k][bi]],
                        scale=A[:, k, ci:ci + 1], bias=B[:, k, ci:ci + 1])

        def ffn(ts, w1_d, w3_d, w2_d, G, ci):
            blks = blocks(ts)
            for j in range(22):
                w1i, w1t = load_w8(w1_d[j])
                w3i, w3t = load_w8(w3_d[j])
                for bi, (t0, bw) in enumerate(blks):
                    bka, bta = ps_r.next()
                    bkb, btb = ps_r.next()
                    for kc in range(8):
                        MM(ps[:, bka, :bw], wk8[:, w1i, kc, :], xn[:, kc, t0:t0 + bw], kc == 0, kc == 7,
                           [w1t, xn_t[kc][bi]], [bta])
                    for kc in range(8):
                        MM(ps[:, bkb, :bw], wk8[:, w3i, kc, :], xn[:, kc, t0:t0 + bw], kc == 0, kc == 7,
                           [w3t, xn_t[kc][bi]], [btb])
                    ti, t_t = tmpf_r.next()
                    ACT(tmpf[:, ti, :bw], ps[:, bka, :bw], AF.Silu, [bta], [t_t])
                    TT(hid[:, j, t0:t0 + bw], tmpf[:, ti, :bw], ps[:, bkb, :bw], ALU.mult, [t_t, btb], [hid_t[j][bi]])
            for i in range(8):
                w2i, w2t = load_w22(w2_d[i])
                for bi, (t0, bw) in enumerate(blks):
                    bk, bt = ps_r.next()
                    for j in range(22):
                        MM(ps[:, bk, :bw], wk22[:, w2i, j, :], hid[:, j, t0:t0 + bw], j == 0, j == 21,
                           [w2t, hid_t[j][bi]], [bt])
                    STT(hT[:, i, t0:t0 + bw], ps[:, bk, :bw], G[:, i, ci:ci + 1], hT[:, i, t0:t0 + bw],
                        ALU.mult, ALU.add, [bt, hT_t[i][bi], vec_t], [hT_t[i][bi]])

        def store(dram_ap, sbuf_ap, tl):
            DMA('sp', dram_ap, sbuf_ap, [tl], (), 's')

        if merge_layer is not None:
            oda = sb("oda", [128, 4, TS_], BF16); oda_t = [[Tl() for _ in range(2)] for _ in range(4)]
            opl = sb("opl", [128, 2, TS_], BF16); opl_t = Tl()
            ohg = sb("ohg", [128, 2, TS_], BF16); ohg_t = [[Tl() for _ in range(2)] for _ in range(2)]
            hgin = sb("hgin", [128, 2, 3, 512]); hgin_r = Ring(2)
            gts = sb("gts", [128, 2, 3, 512]); gts_r = Ring(2)
        if pre_layer is not None:
            cst = sb("cst", [128, 2, TS_]); cst_t = Tl()

        tiles = [(i * 1024, 1024, 0) for i in range(4)]
        if with_ctx:
            tiles.append((TLAT, TCTX, 1))
        if tiles_override is not None:
            tiles = tiles_override

        for (T0, ts, ci) in tiles:
            blks = blocks(ts)
            for k in range(8):
                DMA('act', hT[:, k, :ts], hT_in[k * 128:(k + 1) * 128, T0:T0 + ts], (), hT_t[k], 'l')
            if merge_layer is not None:
                DMA('act', oda[:, :, :ts], odaT_d[:, T0:T0 + ts].rearrange("(c p) t -> p c t", p=128), (),
                    [t_ for row in oda_t for t_ in row], 'l')
                DMA('act', opl[:, :, :ts], opoolT_d[:, T0:T0 + ts].rearrange("(c p) t -> p c t", p=128), (), [opl_t], 'l')
                for c in range(4):
                    for bi, (t0, bw) in enumerate(blks):
                        oc = oda[:, c, t0:t0 + bw]
                        si, s_t = sqb_r.next()
                        TT(sqb[:, si, :bw], oc, oc, ALU.mult, [oda_t[c][bi]], [s_t])
                        bk, bt = ps_r.next()
                        MM(ps[:, bk, :bw], ones_b[:], sqb[:, si, :bw], True, True, [s_t, ones_t], [bt])
                        ri, r_t = rstd_r.next()
                        ACT(rstd[:, ri, :bw], ps[:, bk, :bw], AF.Ln, [bt, vec_t], [r_t], scale=1.0 / 128, bias=epsb[:, 0:1])
                        ACT(rstd[:, ri, :bw], rstd[:, ri, :bw], AF.Exp, [r_t], [r_t], scale=-0.5)
                        STT(oc, oc, dagn[:, 0:1], rstd[:, ri, :bw], ALU.mult, ALU.mult, [oda_t[c][bi], r_t, vec_t],
                            [oda_t[c][bi]])
                for c2 in range(2):
                    for bi, (t0, bw) in enumerate(blks):
                        hi_, h_t = hgin_r.next()
                        for s_i, src in enumerate((ohfT_d, ohbT_d, sgT_in_d)):
                            DMA('act', hgin[:, hi_, s_i, :bw], src[c2 * 128:(c2 + 1) * 128, T0 + t0:T0 + t0 + bw],
                                (), [h_t], 'l')
                        o0 = hgin[:, hi_, 0, :bw]
                        TT(o0, o0, hgin[:, hi_, 1, :bw], ALU.add, [h_t], [h_t])
                        ri, r_t = sumsq_rstd([(o0, h_t)], bw, bd64_b[:], bd64_t, 64)
                        TT(o0, o0, rstd[:, ri, :bw], ALU.mult, [h_t, r_t], [h_t])
                        STT(ohg[:, c2, t0:t0 + bw], o0, hgn[:, 0:1], hgin[:, hi_, 2, :bw], ALU.mult, ALU.mult,
                            [h_t, vec_t], [ohg_t[c2][bi]])
                gview = gateT_in_d.rearrange("(b c p) t -> c p b t", b=3, p=128)
                for i in range(8):
                    wi, wt = load_w8(wmrg_d[i])
                    for bi, (t0, bw) in enumerate(blks):
                        gi, g_t = gts_r.next()
                        DMA('act', gts[:, gi, :, :bw], gview[i][:, :, T0 + t0:T0 + t0 + bw], (), [g_t], 'l')
                        bka, bta = ps_r.next()
                        bkh, bth = ps_r.next()
                        bkp, btp = ps_r.next()
                        for c in range(4):
                            MM(ps[:, bka, :bw], wk8[:, wi, c, :], oda[:, c, t0:t0 + bw], c == 0, c == 3,
                               [wt, oda_t[c][bi]], [bta])
                        for c in range(2):
                            MM(ps[:, bkh, :bw], wk8[:, wi, 4 + c, :], ohg[:, c, t0:t0 + bw], c == 0, c == 1,
                               [wt, ohg_t[c][bi]], [bth])
                        for c in range(2):
                            MM(ps[:, bkp, :bw], wk8[:, wi, 6 + c, :], opl[:, c, t0:t0 + bw], c == 0, c == 1, [wt, opl_t], [btp])
                        t1, t1t = tmpf_r.next()
                        t2, t2t = tmpf_r.next()
                        TT(tmpf[:, t1, :bw], gts[:, gi, 0, :bw], ps[:, bka, :bw], ALU.mult, [g_t, bta], [t1t])
                        TT(tmpf[:, t2, :bw], gts[:, gi, 1, :bw], ps[:, bkh, :bw], ALU.mult, [g_t, bth], [t2t])
                        TT(tmpf[:, t1, :bw], tmpf[:, t1, :bw], tmpf[:, t2, :bw], ALU.add, [t1t, t2t], [t1t])
                        TT(tmpf[:, t2, :bw], gts[:, gi, 2, :bw], ps[:, bkp, :bw], ALU.mult, [g_t, btp, t2t], [t2t])
                        TT(xn[:, i, t0:t0 + bw], tmpf[:, t1, :bw], tmpf[:, t2, :bw], ALU.add, [t1t, t2t], [xn_t[i][bi]])
                G5 = vecs['g5']
                for i in range(8):
                    wi, wt = load_w8(wout_d[i])
                    for bi, (t0, bw) in enumerate(blks):
                        bk, bt = ps_r.next()
                        for kc in range(8):
                            MM(ps[:, bk, :bw], wk8[:, wi, kc, :], xn[:, kc, t0:t0 + bw], kc == 0, kc == 7,
                               [wt, xn_t[kc][bi]], [bt])
                        STT(hT[:, i, t0:t0 + bw], ps[:, bk, :bw], G5[:, i, ci:ci + 1], hT[:, i, t0:t0 + bw],
                            ALU.mult, ALU.add, [bt, hT_t[i][bi], vec_t], [hT_t[i][bi]])
                norm_mod(ts, vecs['f2A'], vecs['f2B'], ci)
                ffn(ts, f2w1_d, f2w3_d, f2w2_d, vecs['g8'], ci)
            if final:
                for bi, (t0, bw) in enumerate(blks):
                    ri, r_t = sumsq_rstd([(hT[:, k, t0:t0 + bw], hT_t[k][bi]) for k in range(8)], bw,
                                         ones_b[:], ones_t, D)
                    for k in range(8):
                        si, s_t = stgf_r.next()
                        STT(stgf[:, si, :bw], hT[:, k, t0:t0 + bw], fng[:, k:k + 1], rstd[:, ri, :bw],
                            ALU.mult, ALU.mult, [hT_t[k][bi], r_t, vec_t], [s_t])
                        store(outT_d[k * 128:(k + 1) * 128, T0 + t0:T0 + t0 + bw], stgf[:, si, :bw], s_t)
            if pre_layer is not None:
                if dbg >= 1:
                    norm_mod(ts, vecs['f1A'], vecs['f1B'], ci)
                if dbg >= 2:
                    ffn(ts, f1w1_d, f1w3_d, f1w2_d, vecs['g2'], ci)
                for k in range(8):
                    DMA('sp', hT_out[k * 128:(k + 1) * 128, T0:T0 + ts], hT[:, k, :ts], hT_t[k], (), 's')
                if dbg < 3:
                    continue
                norm_mod(ts, vecs['mxA'], vecs['mxB'], ci)
                DMA('act', cst[:, 0, :ts], cosT_d[:, T0:T0 + ts], (), [cst_t], 'l')
                DMA('act', cst[:, 1, :ts], sinT_d[:, T0:T0 + ts], (), [cst_t], 'l')
                for c in range(48):
                    wi, wt = load_w8(win_d[c])
                    for bi, (t0, bw) in enumerate(blks):
                        bk, bt = ps_r.next()
                        for kc in range(8):
                            MM(ps[:, bk, :bw], wk8[:, wi, kc, :], xn[:, kc, t0:t0 + bw], kc == 0, kc == 7,
                               [wt, xn_t[kc][bi]], [bt])
                        zin = ps[:, bk, :bw]
                        tok = slice(T0 + t0, T0 + t0 + bw)
                        r2 = slice((c % 2) * 128, (c % 2 + 1) * 128)
                        if c < 8:
                            dst = (qT_d if c < 4 else kT_d)[(c % 4) * 128:(c % 4 + 1) * 128, tok]
                            si, s_t = sqb_r.next()
                            ACT(sqb[:, si, :bw], zin, AF.Copy, [bt], [s_t])
                            bk2, bt2 = ps_r.next()
                            MM(ps[:, bk2, :bw], rperm_b[:], sqb[:, si, :bw], True, True, [s_t, rperm_t], [bt2])
                            t1, t1t = tmpf_r.next()
                            t2, t2t = tmpf_r.next()
                            TT(tmpf[:, t1, :bw], zin, cst[:, 0, t0:t0 + bw], ALU.mult, [bt, cst_t], [t1t])
                            TT(tmpf[:, t2, :bw], ps[:, bk2, :bw], cst[:, 1, t0:t0 + bw], ALU.mult, [bt2, cst_t], [t2t])
                            oi, o_t = stgb_r.next()
                            TT(stgb[:, oi, :bw], tmpf[:, t1, :bw], tmpf[:, t2, :bw], ALU.add, [t1t, t2t], [o_t])
                            store(dst, stgb[:, oi, :bw], o_t)
                        elif c < 12 or 18 <= c < 20 or 22 <= c < 24:
                            if c < 12:
                                dst = vT_d[(c - 8) * 128:(c - 7) * 128, tok]
                            elif c < 20:
                                dst = hiT_d[r2, tok]
                            else:
                                dst = zpT_d[r2, tok]
                            oi, o_t = stgb_r.next()
                            CP(stgb[:, oi, :bw], zin, [bt], [o_t])
                            store(dst, stgb[:, oi, :bw], o_t)
                        elif c < 14 or 20 <= c < 22:
                            dst = (hqT_d if c < 14 else sgT_d)[r2, tok]
                            oi, o_t = stgf_r.next()
                            ACT(stgf[:, oi, :bw], zin, AF.Silu, [bt], [o_t])
                            store(dst, stgf[:, oi, :bw], o_t)
                        elif c < 18:
                            di = (c - 14) // 2
                            col = c - 14
                            kd = (kkfT_d, kkbT_d)[di][r2, tok]
                            ld = (lffT_d, lfbT_d)[di][r2, tok]
                            oi, o_t = stgf_r.next()
                            ACT(stgf[:, oi, :bw], zin, AF.Sigmoid, [bt], [o_t], scale=-1.0)
                            TS(stgf[:, oi, :bw], stgf[:, oi, :bw], oml[:, col:col + 1], ALU.mult, [o_t, vec_t], [o_t])
                            store(kd, stgf[:, oi, :bw], o_t)
                            o2, o2_t = stgf_r.next()
                            ACT(stgf[:, o2, :bw], stgf[:, oi, :bw], AF.Ln, [o_t, vec_t], [o2_t], scale=-1.0,
                                bias=oneb[:, 0:1])
                            store(ld, stgf[:, o2, :bw], o2_t)
                        else:
                            dst = gateT_d[(c - 24) * 128:(c - 23) * 128, tok]
                            oi, o_t = stgf_r.next()
                            ACT(stgf[:, oi, :bw], zin, AF.Sigmoid, [bt], [o_t])
                            store(dst, stgf[:, oi, :bw], o_t)
        P.emit(nc)
    return nc


def host_consts():
    ones = np.ones((128, 128), np.float32)
    bd64 = np.zeros((128, 128), np.float32)
    bd64[:64, :64] = 1
    bd64[64:, 64:] = 1
    R = np.zeros((128, 128), np.float32)
    for blk in (0, 64):
        for j in range(16):
            R[blk + j, blk + 16 + j] = -1
            R[blk + 16 + j, blk + j] = 1
            R[blk + 32 + j, blk + 48 + j] = -1
            R[blk + 48 + j, blk + 32 + j] = 1
    return ones, bd64, np.ascontiguousarray(R.T)


def rope_tables():
    t = np.arange(SEQ)
    row = (t // 64).astype(np.float32)
    col = (t % 64).astype(np.float32)
    inv = (np.float32(10000.0) ** (-np.arange(0, 32, 2, dtype=np.float32) / np.float32(32))).astype(np.float32)
    ar = (row[:, None] * inv[None]).astype(np.float32)
    ac = (col[:, None] * inv[None]).astype(np.float32)
    cos64 = np.concatenate([np.cos(ar), np.cos(ar), np.cos(ac), np.cos(ac)], 1).astype(np.float32)
    sin64 = np.concatenate([np.sin(ar), np.sin(ar), np.sin(ac), np.sin(ac)], 1).astype(np.float32)
    return np.tile(cos64.T, (2, 1)), np.tile(sin64.T, (2, 1))


def core_bq(c):
    return c // 4, c % 4


class HostW:
    def __init__(self, inp):
        self.inp = inp
        self.ones, self.bd64, self.rperm = host_consts()
        self.cosT, self.sinT = rope_tables()
        self.cache = {}

    def get(self, key, fn):
        if key not in self.cache:
            self.cache[key] = fn()
        return self.cache[key]

    def common(self, c, layers):
        inp = self.inp
        b, q = core_bq(c)
        d = dict(ones=self.ones, bd64=self.bd64, rperm=self.rperm)
        d['cs'] = np.ascontiguousarray(np.stack([vec_p(inp['c'][b]), vec_p(inp['c_ctx'])], axis=2))
        for l in layers:
            d[f'wada{l}'] = self.get(('wada', l), lambda: arr_w(inp['w_ada'][l]))
            d[f'bada{l}'] = self.get(('bada', l), lambda: vec_p(inp['b_ada'][l]))
        return d

    def pre(self, c, l, with_ctx=True):
        inp = self.inp
        b, q = core_bq(c)
        d = {}
        d['nf1'] = vec_p(inp['norm_ffn1'][l])
        d['nmix'] = vec_p(inp['norm_mix'][l])
        d['f1w1'] = self.get(('f1w1', l), lambda: arr_w(inp['ffn1_w1'][l]))
        d['f1w3'] = self.get(('f1w3', l), lambda: arr_w(inp['ffn1_w3'][l]))
        d['f1w2'] = self.get(('f1w2', l), lambda: arr_w(inp['ffn1_w2'][l]))
        d['win'] = self.get(('win', l), lambda: arr_w(inp['w_in'][l]))
        lg = inp['hg_lb_logits']
        lbl = np.zeros((128, 4, 2), np.float32)
        for di in range(2):
            for ch in range(2):
                for dep in range(2):
                    lbl[:, di * 2 + ch, dep] = lg[dep, di, ch * 128:(ch + 1) * 128]
        d['lbl'] = lbl
        cosT = self.cosT[:, q * TLAT:(q + 1) * TLAT]
        sinT = self.sinT[:, q * TLAT:(q + 1) * TLAT]
        if with_ctx:
            cosT = np.concatenate([cosT, np.ones((128, TCTX), np.float32)], 1)
            sinT = np.concatenate([sinT, np.zeros((128, TCTX), np.float32)], 1)
        d['cosT'] = np.ascontiguousarray(cosT)
        d['sinT'] = np.ascontiguousarray(sinT)
        return d

    def mrg(self, c, l):
        inp = self.inp
        d = {}
        d['nf2'] = vec_p(inp['norm_ffn2'][l])
        d['f2w1'] = self.get(('f2w1', l), lambda: arr_w(inp['ffn2_w1'][l]))
        d['f2w3'] = self.get(('f2w3', l), lambda: arr_w(inp['ffn2_w3'][l]))
        d['f2w2'] = self.get(('f2w2', l), lambda: arr_w(inp['ffn2_w2'][l]))
        d['wmrg'] = self.get(('wmrg', l), lambda: arr_w(np.concatenate(
            [inp['w_proj_da'][l], inp['w_proj_hg'][l], inp['w_proj_pool'][l]], 0)))
        d['wout'] = self.get(('wout', l), lambda: arr_w(inp['w_out'][l]))
        d['hgn'] = np.ascontiguousarray(np.tile(inp['hg_norm'][l], 2)[:, None])
        d['dag'] = np.ascontiguousarray(inp['da_subln'][l][:, None])
        return d


def initial_hT(inp, c):
    b, q = core_bq(c)
    xs = inp['x'][b, q * TLAT:(q + 1) * TLAT]
    cx = inp['ctx'][b, q * TCTX:(q + 1) * TCTX]
    return np.ascontiguousarray(np.concatenate([xs, cx], 0).T)


NQC = SEQ + CTX


def build_A(nq_tiles=32, nkb=130, with_ctxq=True):
    import contextlib
    nc = bass.Bass("TRN2", target_bir_lowering=False)
    P = Prog()
    O = Ops(P)
    MM, ACT, TT, TS, STT, CP, RCP, MS, DMA = O.MM, O.ACT, O.TT, O.TS, O.STT, O.CP, O.RCP, O.MS, O.DMA

    def din(name, shape, dt=F32):
        return nc.dram_tensor(name, list(shape), dt, kind="ExternalInput").ap()

    qT_d = din("qT", [128, NQC], BF16)
    kT_d = din("kT", [128, NKEY], BF16)
    v_d = din("v", [128, 130, 128], BF16)
    lamv_d = din("lamv", [128, 4, 64])
    lami_d = din("lami", [128, 2])
    gain_d = din("gain", [128, 1])
    ones_d = din("ones", [128, 128])
    oT_d = nc.dram_tensor("oT", [128, NQC], BF16, kind="ExternalOutput").ap()

    with contextlib.ExitStack() as st:
        def sb(name, shape, dt=F32):
            return st.enter_context(nc.sbuf_tensor("s_" + name, list(shape), dt))

        qT = sb("qT", [128, NQC], BF16)
        kT = sb("kT", [128, NKEY], BF16); kT_t = Tl(const=True)
        v = sb("v", [128, 130, 128], BF16); v_t = Tl(const=True)
        NPB = 12
        pb = sb("pb", [128, NPB, 2, 512], BF16); pb_r = Ring(NPB)
        tq = sb("tq", [128, 4, 2, 512], BF16); tq_r = Ring(4)
        acc = sb("acc", [128, 2, 512]); acc_t = Tl()
        fin = sb("fin", [128, 4, 512]); fin_r = Ring(4)
        ocp = sb("ocp", [128, 2, 2, 512]); oc_r = Ring(2)
        sqb = sb("sqb", [128, 512], BF16); sqb_t = Tl()
        ob = sb("ob", [128, 2, 512], BF16); ob_r = Ring(2)
        ones_f = sb("ones_f", [128, 128]); ones_b = sb("ones_b", [128, 128], BF16); c_t = Tl(const=True)
        lamv = sb("lamv", [128, 4, 64]); lami = sb("lami", [128, 2]); gain = sb("gain", [128, 1])
        lw = sb("lw", [128, 2, 64]); ls = sb("ls", [128, 2]); neglam = sb("neglam", [128, 1]); gsc = sb("gsc", [128, 1])
        epsb = sb("epsb", [128, 1])
        ps = st.enter_context(nc.psum_tensor("ps", [128, 8, 512], F32))
        s_r = Ring(6, excl=True)
        o_t = [Tl(excl=True), Tl(excl=True)]

        MS(epsb[:], EPS, [c_t])
        DMA('sp', ones_f[:], ones_d, (), [c_t], 'l')
        onesb_t = Tl(const=True)
        DMA('pool', ones_b[:], ones_d, (), [onesb_t], 'l')
        DMA('sp', lamv[:], lamv_d, (), [c_t], 'l')
        DMA('sp', lami[:], lami_d, (), [c_t], 'l')
        DMA('sp', gain[:], gain_d, (), [c_t], 'l')
        TT(lw[:], lamv[:, 0:4:2, :], lamv[:, 1:4:2, :], ALU.mult, [c_t], [c_t])
        P.op('dve', 'tensor_reduce', dict(out=ls[:], in_=lw[:], axis=mybir.AxisListType.X, op=ALU.add), [c_t], [c_t])
        ACT(ls[:], ls[:], AF.Exp, [c_t], [c_t])
        TT(neglam[:], ls[:, 0:1], ls[:, 1:2], ALU.subtract, [c_t], [c_t])
        STT(neglam[:], neglam[:], -1.0, lami[:, 0:1], ALU.mult, ALU.subtract, [c_t], [c_t])
        TT(gsc[:], gain[:], lami[:, 1:2], ALU.mult, [c_t], [c_t])

        for i in range(0, NKEY, 2080):
            DMA('sp', kT[:, i:i + 2080], kT_d[:, i:i + 2080], (), [kT_t], 'l')
        for i in range(0, 130, 13):
            DMA('sp', v[:, i:i + 13, :], v_d[:, i:i + 13, :], (), [v_t], 'l')

        qtiles = [(i * 512, 512, 0, nkb) for i in range(nq_tiles)]
        if with_ctxq:
            qtiles.append((SEQ, CTX, 128, 130))
        q_t = [Tl() for _ in qtiles]
        for qi, (q0, qw, kb0, kb1) in enumerate(qtiles):
            DMA('sp', qT[:, q0:q0 + qw], qT_d[:, q0:q0 + qw], (), [q_t[qi]], 'l')

        blocks = [(qi, kb) for qi, (q0, qw, kb0, kb1) in enumerate(qtiles) for kb in range(kb0, kb1)]
        LA = 2
        sbanks = {}

        def emit_qk(bi):
            qi, kb = blocks[bi]
            q0, qw, kb0, kb1 = qtiles[qi]
            banks = [s_r.next(), s_r.next()]
            sbanks[bi] = banks
            for c in range(2):
                bk, bt = banks[c]
                MM(ps[:, bk, :qw], kT[c * 64:(c + 1) * 64, kb * 128:(kb + 1) * 128],
                   qT[c * 64:(c + 1) * 64, q0:q0 + qw], True, True, [kT_t, q_t[qi]], [bt])

        pend = []
        accst = {'first': True, 'prev': None}

        def accumulate(xap, x_t, qw):
            if accst['first']:
                CP(acc[:, :, :qw], xap, [x_t], [acc_t])
                accst['first'] = False
            else:
                TT(acc[:, :, :qw], acc[:, :, :qw], xap, ALU.add, [x_t, acc_t], [acc_t])

        def emit_rest(bi):
            qi, kb = blocks[bi]
            q0, qw, kb0, kb1 = qtiles[qi]
            first = kb == kb0
            last = kb == kb1 - 1
            banks = sbanks.pop(bi)
            (bk0, bt0), (bk1, bt1) = banks
            assert bk1 == bk0 + 1
            pi, p_t = pb_r.next()
            ACT(pb[:, pi, :, :qw], ps[:, bk0:bk0 + 2, :qw], AF.Exp, [bt0, bt1], [p_t], scale=0.125)
            for c in range(2):
                MM(ps[:, 6 + c, :qw], v[:, kb, :], pb[:, pi, c, :qw], first, last, [v_t, p_t], [o_t[c]])
            if first:
                accst['first'] = True
                accst['prev'] = None
            if accst['prev'] is None and not last:
                accst['prev'] = (pi, p_t)
            else:
                if accst['prev'] is None:
                    pend.append((pb[:, pi, :, :qw], p_t))
                else:
                    ppi, pp_t = accst['prev']
                    accst['prev'] = None
                    ti, t_t = tq_r.next()
                    TT(tq[:, ti, :, :qw], pb[:, ppi, :, :qw], pb[:, pi, :, :qw], ALU.add, [pp_t, p_t], [t_t])
                    pend.append((tq[:, ti, :, :qw], t_t))
                if len(pend) == 2:
                    (xa, xa_t), (xb_, xb_t) = pend
                    TT(xa, xa, xb_, ALU.add, [xa_t, xb_t], [xa_t])
                    accumulate(xa, xa_t, qw)
                    del pend[:]
                if last and pend:
                    accumulate(pend[0][0], pend[0][1], qw)
                    del pend[:]
            if not last:
                return
            oci, oc_t = oc_r.next()
            for c in range(2):
                ACT(ocp[:, oci, c, :qw], ps[:, 6 + c, :qw], AF.Copy, [o_t[c]], [oc_t])
            ts_ = []
            for c in range(2):
                bk, bt = s_r.next()
                MM(ps[:, bk, :qw], ones_f[:], acc[:, c, :qw], True, True, [c_t, acc_t], [bt])
                fi, f_t = fin_r.next()
                RCP(fin[:, fi, :qw], ps[:, bk, :qw], [bt], [f_t])
                TT(fin[:, fi, :qw], ocp[:, oci, c, :qw], fin[:, fi, :qw], ALU.mult, [oc_t, f_t], [f_t])
                ts_.append((fi, f_t))
            (f0, f0t), (f1, f1t) = ts_
            oi, ob_t = ob_r.next()
            STT(ob[:, oi, :qw], fin[:, f1, :qw], neglam[:, 0:1], fin[:, f0, :qw], ALU.mult, ALU.add,
                [f0t, f1t, c_t], [ob_t])
            DMA('sp', oT_d[:, q0:q0 + qw], ob[:, oi, :qw], [ob_t], (), 's')

        base = 0
        for qi, (q0, qw, kb0, kb1) in enumerate(qtiles):
            n = kb1 - kb0
            if s_r.i % 2:
                s_r.next()
            for j in range(n + LA):
                if j < n:
                    emit_qk(base + j)
                if j - LA >= 0:
                    emit_rest(base + j - LA)
            base += n
        P.emit(nc)
    return nc


NCH = NKEY // 64


def build_H(groups=None, pool_tiles=128, do_pool=True):
    import contextlib
    nc = bass.Bass("TRN2", target_bir_lowering=False)
    P = Prog()
    O = Ops(P)
    MM, TR, ACT, TT, TS, STT, CP, RCP, MS, DMA = O.MM, O.TR, O.ACT, O.TT, O.TS, O.STT, O.CP, O.RCP, O.MS, O.DMA
    if groups is None:
        groups = [(0, 4)] + [(4 + 8 * i, 8) for i in range(32)]

    def din(name, shape, dt=F32):
        return nc.dram_tensor(name, list(shape), dt, kind="ExternalInput").ap()

    hq_d = [din(f"hq{s}", [64, NKEY]) for s in range(2)]
    kk_d = [din(f"kk{s}", [64, NKEY]) for s in range(2)]
    lf_d = [din(f"lf{s}", [64, NKEY]) for s in range(2)]
    vt_d = [din(f"vt{s}", [64, NCH, 64], BF16) for s in range(2)]
    reset_d = din("reset", [64, 512])
    mask_d = din("mask", [64, 64])
    ident_d = din("ident", [64, 64])
    oT_d = [nc.dram_tensor(f"oT{s}", [64, NKEY], F32, kind="ExternalOutput").ap() for s in range(2)]
    if do_pool:
        zp_d = din("zp", [128, 130, 64], BF16)
        band_d = din("band", [5, 128, 128])
        pw_d = din("pw", [64, 64])
        psc_d = din("psc", [64, 1])
        opT_d = nc.dram_tensor("opT", [64, NKEY], BF16, kind="ExternalOutput").ap()

    with contextlib.ExitStack() as st:
        def sb(name, shape, dt=F32):
            return st.enter_context(nc.sbuf_tensor("s_" + name, list(shape), dt))

        reset = sb("reset", [64, 512]); mask = sb("mask", [64, 64]); c_t = Tl(const=True)
        ident = sb("ident", [64, 64], BF16); cb_t = Tl(const=True)
        DMA('sp', reset[:], reset_d, (), [c_t], 'l')
        DMA('sp', mask[:], mask_d, (), [c_t], 'l')
        DMA('pool', ident[:], ident_d, (), [cb_t], 'l')
        ps = st.enter_context(nc.psum_tensor("ps", [128, 8, 512], F32))
        sc_t = Tl(excl=True)
        kt_t = Tl(excl=True)
        ot_t = [[Tl(excl=True), Tl(excl=True)], [Tl(excl=True), Tl(excl=True)]]
        kv_t = [Tl(excl=True), Tl(excl=True)]
        kt_bf = ps[:, 1, :].bitcast(BF16)

        S = []
        for s in range(2):
            d = {}
            d['gin'] = sb(f"gin{s}", [64, 2, 3, 512]); d['gin_r'] = Ring(2)
            d['a'] = sb(f"a{s}", [64, 2, 512]); d['a_r'] = Ring(2)
            d['e1'] = sb(f"e1{s}", [64, 2, 512]); d['e1_r'] = Ring(2)
            d['e2'] = sb(f"e2{s}", [64, 2, 512]); d['e2_r'] = Ring(2)
            d['qp'] = sb(f"qp{s}", [64, 2, 512], BF16); d['qp_r'] = Ring(2)
            d['kp'] = sb(f"kp{s}", [64, 2, 512], BF16); d['kp_r'] = Ring(2)
            d['sm'] = sb(f"sm{s}", [64, 2, 512], BF16); d['sm_r'] = Ring(2)
            d['ktok'] = sb(f"ktok{s}", [64, 2, 512], BF16); d['ktok_r'] = Ring(2)
            d['sc1'] = sb(f"sc1{s}", [64, 2, 8]); d['sc1_r'] = Ring(2)
            d['ser'] = sb(f"ser{s}", [64, 2, 8]); d['ser_r'] = Ring(2)
            d['v'] = sb(f"v{s}", [64, NCH, 64], BF16); d['v_t'] = Tl(const=True)
            d['state'] = sb(f"state{s}", [64, 2, 64]); d['state_t'] = [Tl(), Tl()]; d['sp'] = 0
            d['sr'] = sb(f"sr{s}", [64, 64], BF16); d['sr_t'] = Tl()
            d['ost'] = sb(f"ost{s}", [64, 2, 512]); d['ost_r'] = Ring(2)
            MS(d['state'][:], 0.0, d['state_t'])
            for t_ in d['sm_r'].t:
                pass
            MS(d['sm'][:], 0.0, d['sm_r'].t)
            for i in range(0, NCH, 52):
                DMA('sp', d['v'][:, i:i + 52, :], vt_d[s][:, i:i + 52, :], (), [d['v_t']], 'l')
            S.append(d)

        def pre(s, g, out):
            d = S[s]
            c0, n = groups[g]
            W = 64 * n
            t0 = 64 * c0
            gi, g_t = d['gin_r'].next()
            for j, src in enumerate((hq_d[s], kk_d[s], lf_d[s])):
                DMA('sp', d['gin'][:, gi, j, :W], src[:, t0:t0 + W], (), [g_t], 'l')
            yield
            ai, a_t = d['a_r'].next()
            a = d['a'][:, ai, :W]
            P.op('dve', 'tensor_tensor_scan', dict(out=a, data0=reset[:, :W], data1=d['gin'][:, gi, 2, :W], initial=0.0,
                                                   op0=ALU.mult, op1=ALU.add), [g_t, c_t], [a_t])
            a3 = a.rearrange("p (c t) -> p c t", t=64)
            yield
            s1i, s1_t = d['sc1_r'].next()
            eri, er_t = d['ser_r'].next()
            ACT(d['sc1'][:, s1i, :n], a3[:, :, 63], AF.Exp, [a_t], [s1_t])
            ACT(d['ser'][:, eri, :n], a3[:, :, 31], AF.Exp, [a_t], [er_t])
            e2i, e2_t = d['e2_r'].next()
            dd = d['e2'][:, e2i, :W]
            TT(dd.rearrange("p (c t) -> p c t", t=64), a3, a3[:, :, 31:32].to_broadcast([64, n, 64]), ALU.subtract,
               [a_t], [e2_t])
            yield
            e1i, e1_t = d['e1_r'].next()
            e1 = d['e1'][:, e1i, :W]
            ACT(e1, dd, AF.Exp, [e2_t], [e1_t])
            ACT(dd, dd, AF.Exp, [e2_t], [e2_t], scale=-1.0)
            yield
            qi, q_t = d['qp_r'].next()
            ki, k_t = d['kp_r'].next()
            TT(d['qp'][:, qi, :W], d['gin'][:, gi, 0, :W], e1, ALU.mult, [g_t, e1_t], [q_t])
            TT(d['kp'][:, ki, :W], d['gin'][:, gi, 1, :W], dd, ALU.mult, [g_t, e2_t], [k_t])
            yield
            for c in range(n):
                MM(ps[0:64, 0, c * 64 + 32:(c + 1) * 64], d['kp'][:, ki, c * 64:(c + 1) * 64],
                   d['qp'][:, qi, c * 64 + 32:(c + 1) * 64], True, True, [k_t, q_t], [sc_t])
                MM(ps[0:32, 0, c * 64:c * 64 + 32], d['kp'][:, ki, c * 64:c * 64 + 32],
                   d['qp'][:, qi, c * 64:c * 64 + 32], True, True, [k_t, q_t], [sc_t])
            for c in range(n):
                TR(kt_bf[0:64, c * 64:(c + 1) * 64], d['kp'][:, ki, c * 64:(c + 1) * 64], ident[:], [k_t, cb_t], [kt_t])
            smi, sm_t = d['sm_r'].next()
            sm3 = d['sm'][:, smi, :W].rearrange("p (c t) -> p c t", t=64)
            sc3 = ps[0:64, 0, :W].rearrange("p (c t) -> p c t", t=64)
            TT(sm3[:, :, 32:64], sc3[:, :, 32:64], mask[:, 32:64].unsqueeze(1).to_broadcast([64, n, 32]), ALU.mult,
               [sc_t, c_t], [sm_t])
            TT(sm3[0:32, :, 0:32], sc3[0:32, :, 0:32], mask[0:32, 0:32].unsqueeze(1).to_broadcast([32, n, 32]),
               ALU.mult, [sc_t, c_t, sm_t], [sm_t])
            kti, kt2_t = d['ktok_r'].next()
            CP(d['ktok'][:, kti, :W], kt_bf[0:64, :W], [kt_t], [kt2_t])
            out.update(dict(par=g % 2, c0=c0, n=n, W=W, t0=t0, qi=qi, q_t=q_t, smi=smi, sm_t=sm_t, kti=kti, kt2_t=kt2_t,
                            s1i=s1i, s1_t=s1_t, eri=eri, er_t=er_t, e1i=e1i, e1_t=e1_t))

        def step(s, pr, c):
            d = S[s]
            ch = pr['c0'] + c
            cs_ = slice(c * 64, (c + 1) * 64)
            po = d['sp']
            pn = 1 - po
            d['sp'] = pn
            st_o, st_n = d['state'][:, po, :], d['state'][:, pn, :]
            so_t, sn_t = d['state_t'][po], d['state_t'][pn]
            ACT(d['sr'][:], st_o, AF.Copy, [so_t, pr['er_t']], [d['sr_t']],
                scale=d['ser'][:, pr['eri'], c:c + 1])
            ob_ = (2 + s) if pr['par'] == 0 else (6 + s)
            MM(ps[0:64, ob_, cs_], d['v'][:, ch, :], d['sm'][:, pr['smi'], cs_], True, False,
               [d['v_t'], pr['sm_t']], [ot_t[s][pr['par']]])
            MM(ps[0:64, ob_, cs_], d['sr'][:], d['qp'][:, pr['qi'], cs_], False, True,
               [d['sr_t'], pr['q_t']], [ot_t[s][pr['par']]])
            MM(ps[0:64, 4 + s, 0:64], d['ktok'][:, pr['kti'], cs_], d['v'][:, ch, :], True, True,
               [pr['kt2_t'], d['v_t']], [kv_t[s]])
            TS(st_n, st_o, d['sc1'][:, pr['s1i'], c:c + 1], ALU.mult, [so_t, pr['s1_t']], [sn_t])
            STT(st_n, ps[0:64, 4 + s, 0:64], d['e1'][:, pr['e1i'], c * 64 + 63:c * 64 + 64], st_n,
                ALU.mult, ALU.add, [kv_t[s], pr['e1_t'], sn_t], [sn_t])

        def fin(s, pr):
            d = S[s]
            W = pr['W']
            oi, o_t = d['ost_r'].next()
            ob_ = (2 + s) if pr['par'] == 0 else (6 + s)
            ACT(d['ost'][:, oi, :W], ps[0:64, ob_, :W], AF.Copy, [ot_t[s][pr['par']]], [o_t])
            DMA('pool', oT_d[s][:, pr['t0']:pr['t0'] + W], d['ost'][:, oi, :W], [o_t], (), 's')

        def drain(gens):
            for gg in gens:
                for _ in gg:
                    pass

        prs = [{}, {}]
        drain([pre(0, 0, prs[0]), pre(1, 0, prs[1])])
        for g in range(len(groups)):
            nxt, gens = None, []
            if g + 1 < len(groups):
                nxt = [{}, {}]
                gens = [pre(0, g + 1, nxt[0]), pre(1, g + 1, nxt[1])]
            for c in range(groups[g][1]):
                step(0, prs[0], c)
                step(1, prs[1], c)
                for gg in gens:
                    next(gg, None)
            drain(gens)
            fin(0, prs[0])
            fin(1, prs[1])
            prs = nxt

        if do_pool:
            zp = sb("zp", [128, 130, 64], BF16); zp_t = Tl(const=True)
            band = sb("band", [128, 5, 128], BF16); pw = sb("pw", [64, 64], BF16); pc_t = Tl(const=True)
            psc = sb("psc", [64, 1]); pcs_t = Tl(const=True)
            mxb = sb("mxb", [64, 2, 512], BF16); mxb_r = Ring(2)
            pob = sb("pob", [64, 2, 512], BF16); pob_r = Ring(2)
            for i in range(0, 130, 26):
                DMA('sp', zp[:, i:i + 26, :], zp_d[:, i:i + 26, :], (), [zp_t], 'l')
            for i in range(5):
                DMA('pool', band[:, i, :], band_d[i], (), [pc_t], 'l')
            DMA('pool', pw[:], pw_d, (), [pc_t], 'l')
            DMA('sp', psc[:], psc_d, (), [pcs_t], 'l')
            mx_t = ot_t[0][1]
            py_t = ot_t[1][1]
            seqs = [(0, pool_tiles, 256 // 1)] if False else None
            plan = []
            for (tb, nt, tok0) in ((0, pool_tiles, CTX), (128, 2, 0)):
                for g0 in range(0, nt, 4):
                    plan.append((tb, nt, tok0, g0, min(4, nt - g0)))
            for (tb, nt, tok0, g0, gn) in plan:
                for j in range(gn):
                    ti = g0 + j
                    terms = []
                    if ti == 0:
                        terms.append((ti, 1))
                    elif ti == nt - 1:
                        terms.append((ti, 2))
                    else:
                        terms.append((ti, 0))
                    if ti > 0:
                        terms.append((ti - 1, 3))
                    if ti < nt - 1:
                        terms.append((ti + 1, 4))
                    for k, (src, bi) in enumerate(terms):
                        MM(ps[0:64, 6, j * 128:(j + 1) * 128], zp[:, tb + src, :], band[:, bi, :], k == 0,
                           k == len(terms) - 1, [zp_t, pc_t], [mx_t])
                mi, m_t = mxb_r.next()
                CP(mxb[:, mi, :gn * 128], ps[0:64, 6, :gn * 128], [mx_t], [m_t])
                MM(ps[0:64, 7, :gn * 128], pw[:], mxb[:, mi, :gn * 128], True, True, [pc_t, m_t], [py_t])
                pi, p_t = pob_r.next()
                ACT(pob[:, pi, :gn * 128], ps[0:64, 7, :gn * 128], AF.Copy, [py_t, pcs_t], [p_t], scale=psc[:, 0:1])
                DMA('pool', opT_d[:, tok0 + g0 * 128:tok0 + (g0 + gn) * 128], pob[:, pi, :gn * 128], [p_t], (), 's')
        P.emit(nc)
    return nc


def h_consts():
    reset = np.ones((64, 512), np.float32)
    reset[:, ::64] = 0.0
    s = np.arange(64)
    mask = (s[:, None] <= s[None, :]).astype(np.float32)
    return dict(reset=reset, mask=mask, ident=np.eye(64, dtype=np.float32))


def zp_tiles(z_lat, z_ctx):
    a = z_lat.reshape(-1, 128, 64).transpose(1, 0, 2)
    b = z_ctx.reshape(-1, 128, 64).transpose(1, 0, 2)
    return np.ascontiguousarray(np.concatenate([a, b], 1))


def band_mats(w):
    h = w // 2
    n = 128 * 3
    t = np.arange(n)
    full = np.zeros((n, n), np.float64)
    for tt in range(n):
        lo, hi = tt - h, tt + h
        for ss in range(max(lo, 0), min(hi, n)):
            full[ss, tt] = 1.0 / w
    Bc = full[128:256, 128:256] - np.eye(128)
    Bp = full[0:128, 128:256]
    Bn = full[256:384, 128:256]
    first = np.zeros((128, 128))
    last = np.zeros((128, 128))
    for tt in range(128):
        lo, hi = max(tt - h, 0), tt + h
        cnt = hi - lo
        for ss in range(lo, min(hi, 128)):
            first[ss, tt] = 1.0 / cnt
        lo2, hi2 = tt - h, min(tt + h, 128)
        cnt2 = hi2 - lo2
        for ss in range(max(lo2, 0), hi2):
            last[ss, tt] = 1.0 / cnt2
    first -= np.eye(128)
    last -= np.eye(128)
    return np.ascontiguousarray(np.stack([Bc, first, last, Bp, Bn]).astype(np.float32))


_PROGS = {}


def _prog(key, fn):
    if key not in _PROGS:
        _PROGS[key] = fn()
    return _PROGS[key]


def _run(nc, maps):
    res = run_bass_kernel_spmd(nc, maps, core_ids=list(range(NCORE)))
    return [{k: np.asarray(v) for k, v in r.items()} for r in res.results]


def _gather_tok(rs, b, key, rows):
    lat = np.concatenate([rs[b * 4 + q][key][rows, :TLAT] for q in range(4)], 1)
    ctx = np.concatenate([rs[b * 4 + q][key][rows, TLAT:] for q in range(4)], 1)
    return lat, ctx


def _mixer_inputs(inp, rs, l):
    lam_init = 0.8 - 0.6 * float(np.exp(-0.3 * l))
    lamv = np.stack([inp['da_lambda_q1'][l], inp['da_lambda_k1'][l], inp['da_lambda_q2'][l], inp['da_lambda_k2'][l]])
    hc = h_consts()
    ones = np.ones((128, 128), np.float32)
    mapsA, mapsH = [], []
    for c in range(NCORE):
        b, h = core_bq(c)
        r128 = slice(h * 128, (h + 1) * 128)
        r64 = slice(h * 64, (h + 1) * 64)
        ql, qc = _gather_tok(rs, b, 'qT', r128)
        kl, kc = _gather_tok(rs, b, 'kT', r128)
        vl, vc = _gather_tok(rs, b, 'vT', r128)
        vtok = np.concatenate([vl, vc], 1).T
        dA = dict(qT=np.ascontiguousarray(np.concatenate([ql, qc], 1)),
                  kT=np.ascontiguousarray(np.concatenate([kl, kc], 1)),
                  v=np.ascontiguousarray(vtok.reshape(130, 128, 128).transpose(1, 0, 2)),
                  lamv=np.ascontiguousarray(np.broadcast_to(lamv[None], (128, 4, 64))).astype(np.float32),
                  lami=np.ascontiguousarray(np.broadcast_to(
                      np.array([lam_init, 1.0 - lam_init], np.float32)[None], (128, 2))),
                  gain=np.ascontiguousarray(inp['da_subln'][l][:, None]), ones=ones)
        mapsA.append(dA)
        dH = dict(hc)

        def scan_order(key, flip):
            lat, ctx = _gather_tok(rs, b, key, r64)
            if flip:
                lat, ctx = lat[:, ::-1], ctx[:, ::-1]
            return np.ascontiguousarray(np.concatenate([ctx, lat], 1))

        for s, (kkey, lkey) in enumerate((('kkfT', 'lffT'), ('kkbT', 'lfbT'))):
            dH[f'hq{s}'] = scan_order('hqT', s == 1)
            dH[f'kk{s}'] = scan_order(kkey, s == 1)
            dH[f'lf{s}'] = scan_order(lkey, s == 1)
            vi = scan_order('hiT', s == 1).T
            dH[f'vt{s}'] = np.ascontiguousarray(vi.reshape(NCH, 64, 64).transpose(1, 0, 2))
        zl, zc = _gather_tok(rs, b, 'zpT', r64)
        dH['zp'] = zp_tiles(np.ascontiguousarray(zl.T), np.ascontiguousarray(zc.T))
        dH['band'] = band_mats(2 ** (h + 1))
        dH['pw'] = np.ascontiguousarray(inp['pool_w'][l][h])
        dH['psc'] = np.ascontiguousarray(inp['pool_scale'][l][r64][:, None])
        mapsH.append(dH)
    return mapsA, mapsH


def _merge_inputs(rs, rA, rH, with_ctx):
    out = []
    for c in range(NCORE):
        b, q = core_bq(c)
        lat = slice(q * TLAT, (q + 1) * TLAT)
        cx = slice(q * TCTX, (q + 1) * TCTX)

        def cat(lat_part, ctx_part):
            return np.ascontiguousarray(np.concatenate([lat_part, ctx_part], 1) if with_ctx else lat_part)

        oda = [cat(rA[b * 4 + h]['oT'][:, lat], rA[b * 4 + h]['oT'][:, SEQ + q * TCTX:SEQ + (q + 1) * TCTX]) for h in range(4)]
        ohf, ohb, opl = [], [], []
        for h in range(4):
            r = rH[b * 4 + h]
            f = r['oT0']
            ohf.append(cat(f[:, CTX:][:, lat], f[:, :CTX][:, cx]))
            bw = r['oT1']
            ohb.append(cat(bw[:, CTX:][:, ::-1][:, lat], bw[:, :CTX][:, ::-1][:, cx]))
            p = r['opT']
            opl.append(cat(p[:, CTX:][:, lat], p[:, :CTX][:, cx]))
        n = NT if with_ctx else TLAT
        d = dict(odaT=np.concatenate(oda, 0), ohfT=np.concatenate(ohf, 0), ohbT=np.concatenate(ohb, 0),
                 opoolT=np.concatenate(opl, 0),
                 sgT_in=np.ascontiguousarray(rs[c]['sgT'][:, :n]),
                 gateT_in=np.ascontiguousarray(rs[c]['gateT'][:, :n]),
                 hT_in=np.ascontiguousarray(rs[c]['hT_out'][:, :n]))
        out.append(d)
    return out


def kernel(**inputs):
    return _forward(inputs)


def _forward(inputs, dbg=None):
    inp = {k: np.asarray(v) for k, v in inputs.items()}
    dbg = dbg or (lambda name, val: None)
    H = HostW(inp)
    ncT1 = _prog('T1', lambda: build_T(None, 0, False, True))
    maps = []
    for c in range(NCORE):
        d = H.common(c, [0])
        d.update(H.pre(c, 0))
        d['hT_in'] = initial_hT(inp, c)
        maps.append(d)
    r1 = _run(ncT1, maps)
    dbg('r1', r1)
    ncA = _prog('A', build_A)
    ncH = _prog('H', build_H)
    mA, mH = _mixer_inputs(inp, r1, 0)
    rA = _run(ncA, mA)
    dbg('rA0', rA)
    rH = _run(ncH, mH)
    dbg('rH0', rH)
    del mA, mH
    ncT2 = _prog('T2', lambda: build_T(0, 1, False, True))
    mm = _merge_inputs(r1, rA, rH, True)
    maps = []
    for c in range(NCORE):
        d = H.common(c, [0, 1])
        d.update(H.mrg(c, 0))
        d.update(H.pre(c, 1))
        d.update(mm[c])
        maps.append(d)
    del r1, rA, rH
    r2 = _run(ncT2, maps)
    dbg('r2', r2)
    mA, mH = _mixer_inputs(inp, r2, 1)
    rA = _run(ncA, mA)
    dbg('rA1', rA)
    rH = _run(ncH, mH)
    dbg('rH1', rH)
    del mA, mH
    ncT3 = _prog('T3', lambda: build_T(1, None, True, False))
    mm = _merge_inputs(r2, rA, rH, False)
    maps = []
    for c in range(NCORE):
        d = H.common(c, [1])
        d.update(H.mrg(c, 1))
        d.update(mm[c])
        d['fnorm'] = vec_p(inp['final_norm'])
        maps.append(d)
    r3 = _run(ncT3, maps)
    out = np.empty((BATCH, SEQ, D), np.float32)
    for c in range(NCORE):
        b, q = core_bq(c)
        out[b, q * TLAT:(q + 1) * TLAT] = r3[c]['outT'].T
    return out
```

```python
import numpy as np
import ml_dtypes
import concourse.bass as bass
import concourse.mybir as mybir
from concourse.bass_utils import run_bass_kernel_spmd

F32 = mybir.dt.float32
BF16 = mybir.dt.bfloat16
AF = mybir.ActivationFunctionType
ALU = mybir.AluOpType
NPBF = ml_dtypes.bfloat16

D = 1024
SEQ = 16384
BATCH = 2
CTX = 256
DFF = 2816
DIN = 6144
NCORE = 8
TLAT = 4096
TCTX = 64
NT = TLAT + TCTX
NKEY = SEQ + CTX
EPS = 1e-6

ENGS = ('pe', 'act', 'dve', 'pool', 'sp')


class Tl:
    __slots__ = ('w', 'r', 'const', 'lsem', 'ssem', 'excl')

    def __init__(self, const=False, excl=False):
        self.w = None
        self.r = []
        self.const = const
        self.excl = excl
        self.lsem = None
        self.ssem = None


class Op:
    __slots__ = ('eng', 'fn', 'deps', 'sig', 'val', 'dsem')


class Prog:
    def __init__(self):
        self.ops = {e: [] for e in ENGS}
        self.dcnt = {}
        self.deng = {}

    def op(self, eng, meth, kw, reads=(), writes=(), dsem=None):
        o = Op()
        o.eng = eng
        o.fn = (meth, kw)
        o.sig = False
        o.val = 0
        o.dsem = dsem
        wr = {}
        rd = {}
        for t in reads:
            if t.w is not None:
                wr[id(t.w)] = t.w
            if t.excl:
                for r in t.r:
                    if r.eng != eng:
                        wr[id(r)] = r
        for t in writes:
            if t.w is not None:
                wr[id(t.w)] = t.w
            for r in t.r:
                rd[id(r)] = r
        deps = []
        for d in wr.values():
            if d.dsem is None and dsem is None and d.eng == eng and eng == 'pe':
                continue
            deps.append(d)
        for d in rd.values():
            if id(d) in wr:
                continue
            if d.dsem is None and dsem is None and d.eng == eng and eng == 'pe':
                continue
            deps.append(d)
        for d in deps:
            if d.dsem is None:
                d.sig = True
        o.deps = deps
        for t in reads:
            if not t.const:
                t.r.append(o)
        for t in writes:
            t.w = o
            t.r = []
        if dsem is not None:
            if len(writes) > 0:
                t = writes[0]
                if t.lsem is None:
                    t.lsem = 'l%d' % len(self.dcnt)
                    self.dcnt[t.lsem] = 0
                dsem = t.lsem
            else:
                t = reads[0]
                if t.ssem is None:
                    t.ssem = 's%d' % len(self.dcnt)
                    self.dcnt[t.ssem] = 0
                dsem = t.ssem
            o.dsem = dsem
            self.dcnt[dsem] = self.dcnt[dsem] + 16
            o.val = self.dcnt[dsem]
        self.ops[eng].append(o)
        return o

    def emit(self, nc):
        for e in ENGS:
            c = 0
            for o in self.ops[e]:
                if o.dsem is None and o.sig:
                    c += 1
                    o.val = c
        import contextlib
        with contextlib.ExitStack() as st:
            esem = {e: st.enter_context(nc.semaphore('es_' + e)) for e in ENGS}
            dsem = {k: st.enter_context(nc.semaphore('ds_' + k)) for k in self.dcnt}
            block = st.enter_context(nc.Block())
            prog = self

            def run(eng_name, e):
                waited = {}
                for o in prog.ops[eng_name]:
                    for d in o.deps:
                        if d.dsem is not None:
                            key = ('d', d.dsem)
                            sem = dsem[d.dsem]
                        else:
                            key = ('e', d.eng)
                            sem = esem[d.eng]
                        if waited.get(key, 0) >= d.val:
                            continue
                        waited[key] = d.val
                        e.wait_ge(sem, d.val)
                    ins = getattr(e, o.fn[0])(**o.fn[1])
                    if o.dsem is not None:
                        ins.then_inc(dsem[o.dsem], 16)
                    elif o.sig:
                        ins.then_inc(esem[eng_name], 1)
                if eng_name == 'sp':
                    for k, v in prog.dcnt.items():
                        e.wait_ge(dsem[k], v)

            @block.tensor
            def _(e):
                run('pe', e)

            @block.scalar
            def _(e):
                run('act', e)

            @block.vector
            def _(e):
                run('dve', e)

            @block.gpsimd
            def _(e):
                run('pool', e)

            @block.sync
            def _(e):
                run('sp', e)


class Ring:
    def __init__(self, n, excl=False):
        self.n = n
        self.i = 0
        self.t = [Tl(excl=excl) for _ in range(n)]

    def next(self):
        k = self.i % self.n
        self.i += 1
        return k, self.t[k]


def arr_w(W):
    K, N = W.shape
    return np.ascontiguousarray(W.reshape(K // 128, 128, N // 128, 128).transpose(2, 1, 0, 3))


def vec_p(v):
    return np.ascontiguousarray(v.reshape(-1, 128).T)


class Ops:
    def __init__(self, P):
        self.P = P

    def MM(self, out, lhsT, rhs, start, stop, reads, writes):
        self.P.op('pe', 'matmul', dict(out=out, lhsT=lhsT, rhs=rhs, start=start, stop=stop), reads, writes)

    def TR(self, out, in_, ident, reads, writes):
        self.P.op('pe', 'transpose', dict(out=out, in_=in_, identity=ident), reads, writes)

    def ACT(self, out, in_, func, reads, writes, **kw):
        self.P.op('act', 'activation', dict(out=out, in_=in_, func=func, **kw), reads, writes)

    def TT(self, out, in0, in1, op, reads, writes, eng='dve'):
        self.P.op(eng, 'tensor_tensor', dict(out=out, in0=in0, in1=in1, op=op), reads, writes)

    def TS(self, out, in0, s1, op0, reads, writes, s2=None, op1=None, eng='dve'):
        kw = dict(out=out, in0=in0, scalar1=s1, scalar2=s2, op0=op0)
        if op1 is not None:
            kw['op1'] = op1
        self.P.op(eng, 'tensor_scalar', kw, reads, writes)

    def STT(self, out, in0, scalar, in1, op0, op1, reads, writes):
        self.P.op('dve', 'scalar_tensor_tensor', dict(out=out, in0=in0, scalar=scalar, in1=in1, op0=op0, op1=op1),
                  reads, writes)

    def CP(self, out, in_, reads, writes, eng='dve'):
        self.P.op(eng, 'tensor_copy', dict(out=out, in_=in_), reads, writes)

    def RCP(self, out, in_, reads, writes, scratch=None):
        if scratch is None:
            self.P.op('dve', 'reciprocal', dict(out=out, in_=in_), reads, writes)
        else:
            self.P.op('dve', 'reciprocal_approx_accurate', dict(out=out, in_=in_, scratch=scratch), reads, writes)

    def MS(self, ap, val, writes, eng='dve'):
        self.P.op(eng, 'memset', dict(ap=ap, constant=val), (), writes)

    def DMA(self, eng, out, in_, reads, writes, dsem):
        self.P.op(eng, 'dma_start', dict(out=out, in_=in_), reads, writes, dsem)


def build_T(merge_layer, pre_layer, final, with_ctx, tiles_override=None, dbg=99):
    import contextlib
    nc = bass.Bass("TRN2", target_bir_lowering=False)
    P = Prog()
    O = Ops(P)
    MM, ACT, TT, TS, STT, CP, RCP, MS, DMA = O.MM, O.ACT, O.TT, O.TS, O.STT, O.CP, O.RCP, O.MS, O.DMA
    ntok = NT if with_ctx else TLAT

    def din(name, shape, dt=F32):
        return nc.dram_tensor(name, list(shape), dt, kind="ExternalInput").ap()

    def dout(name, shape, dt=F32):
        return nc.dram_tensor(name, list(shape), dt, kind="ExternalOutput").ap()

    hT_in = din("hT_in", [D, ntok])
    cs_d = din("cs", [128, 8, 2])
    ones_d = din("ones", [128, 128])
    bd64_d = din("bd64", [128, 128])
    rperm_d = din("rperm", [128, 128])
    layers = sorted(set(x for x in (merge_layer, pre_layer) if x is not None))
    wada_d = {l: din(f"wada{l}", [72, 128, 8, 128]) for l in layers}
    bada_d = {l: din(f"bada{l}", [128, 72]) for l in layers}
    if merge_layer is not None:
        ml = merge_layer
        nf2_d = din("nf2", [128, 8])
        f2w1_d = din("f2w1", [22, 128, 8, 128])
        f2w3_d = din("f2w3", [22, 128, 8, 128])
        f2w2_d = din("f2w2", [8, 128, 22, 128])
        wmrg_d = din("wmrg", [8, 128, 8, 128])
        wout_d = din("wout", [8, 128, 8, 128])
        hgn_d = din("hgn", [128, 1])
        dag_d = din("dag", [128, 1])
        odaT_d = din("odaT", [512, ntok], BF16)
        ohfT_d = din("ohfT", [256, ntok])
        ohbT_d = din("ohbT", [256, ntok])
        sgT_in_d = din("sgT_in", [256, ntok])
        opoolT_d = din("opoolT", [256, ntok], BF16)
        gateT_in_d = din("gateT_in", [3072, ntok])
    if pre_layer is not None:
        pl = pre_layer
        nf1_d = din("nf1", [128, 8])
        nmix_d = din("nmix", [128, 8])
        f1w1_d = din("f1w1", [22, 128, 8, 128])
        f1w3_d = din("f1w3", [22, 128, 8, 128])
        f1w2_d = din("f1w2", [8, 128, 22, 128])
        win_d = din("win", [48, 128, 8, 128])
        lbl_d = din("lbl", [128, 4, 2])
        cosT_d = din("cosT", [128, ntok])
        sinT_d = din("sinT", [128, ntok])
        hT_out = dout("hT_out", [D, ntok])
        qT_d = dout("qT", [512, ntok], BF16)
        kT_d = dout("kT", [512, ntok], BF16)
        vT_d = dout("vT", [512, ntok], BF16)
        hqT_d = dout("hqT", [256, ntok])
        kkfT_d = dout("kkfT", [256, ntok])
        kkbT_d = dout("kkbT", [256, ntok])
        lffT_d = dout("lffT", [256, ntok])
        lfbT_d = dout("lfbT", [256, ntok])
        hiT_d = dout("hiT", [256, ntok], BF16)
        sgT_d = dout("sgT", [256, ntok])
        zpT_d = dout("zpT", [256, ntok], BF16)
        gateT_d = dout("gateT", [3072, ntok])
    if final:
        fnorm_d = din("fnorm", [128, 8])
        outT_d = dout("outT", [D, TLAT])

    with contextlib.ExitStack() as st:
        def sb(name, shape, dt=F32):
            return st.enter_context(nc.sbuf_tensor("s_" + name, list(shape), dt))

        TS_ = 1024 + (TCTX if with_ctx else 0)
        hT = sb("hT", [128, 8, TS_]); hT_t = [[Tl() for _ in range(3)] for _ in range(8)]
        xn = sb("xn", [128, 8, TS_], BF16); xn_t = [[Tl() for _ in range(3)] for _ in range(8)]
        hid = sb("hid", [128, 22, TS_], BF16); hid_t = [[Tl() for _ in range(3)] for _ in range(22)]
        NW8 = 6 if (merge_layer is not None and pre_layer is not None) else 8
        wk8 = sb("wk8", [128, NW8, 8, 128], BF16); wk8_r = Ring(NW8)
        NW22 = 2 if (merge_layer is not None and pre_layer is not None) else 3
        wk22 = sb("wk22", [128, NW22, 22, 128], BF16); wk22_r = Ring(NW22)
        NTS = 3 if (merge_layer is not None and pre_layer is not None) else 4
        tmpf = sb("tmpf", [128, NTS, 512]); tmpf_r = Ring(NTS)
        sqb = sb("sqb", [128, 3, 512], BF16); sqb_r = Ring(3)
        rstd = sb("rstd", [128, 2, 512]); rstd_r = Ring(2)
        stgf = sb("stgf", [128, NTS, 512]); stgf_r = Ring(NTS)
        stgb = sb("stgb", [128, NTS, 512], BF16); stgb_r = Ring(NTS)
        ps = st.enter_context(nc.psum_tensor("ps", [128, 8, 512], F32)); ps_r = Ring(8, excl=True)
        ones_b = sb("ones_b", [128, 128], BF16); ones_t = Tl(const=True)
        bd64_b = sb("bd64_b", [128, 128], BF16); bd64_t = Tl(const=True)
        rperm_b = sb("rperm_b", [128, 128], BF16); rperm_t = Tl(const=True)
        cs = sb("cs", [128, 8, 2]); cs_t = Tl(const=True)
        wada = sb("wada", [128, 2, 8, 128]); wada_r = Ring(2)
        mod = {l: sb(f"mod{l}", [128, 72, 2]) for l in layers}; mod_t = Tl(const=True)
        bada = {l: sb(f"bada{l}", [128, 72]) for l in layers}
        vec_t = Tl(const=True)
        epsb = sb("epsb", [128, 1])
        oneb = sb("oneb", [128, 1])
        MS(epsb[:], EPS, [vec_t])
        MS(oneb[:], 1.0, [vec_t])

        DMA('pool', ones_b[:], ones_d, (), [ones_t], 'w')
        DMA('pool', bd64_b[:], bd64_d, (), [bd64_t], 'w')
        DMA('pool', rperm_b[:], rperm_d, (), [rperm_t], 'w')
        DMA('sp', cs[:], cs_d, (), [cs_t], 'l')
        cs_flat = cs[:].rearrange("p a b -> p (a b)")
        ACT(cs_flat, cs_flat, AF.Silu, [cs_t], [cs_t])

        for l in layers:
            DMA('sp', bada[l][:], bada_d[l], (), [mod_t], 'l')
            bk, bt = ps_r.next()
            need = set()
            if l == pre_layer:
                need.update(range(0, 40))
            if l == merge_layer:
                need.update(range(40, 72))
            MS(mod[l][:], 0.0, [mod_t])
            for fc in sorted(need):
                wi, wt = wada_r.next()
                DMA('sp', wada[:, wi], wada_d[l][fc], (), [wt], 'l')
                for kc in range(8):
                    MM(ps[:, bk, 2 * fc:2 * fc + 2], wada[:, wi, kc, :], cs[:, kc, :], kc == 0, kc == 7,
                       [wt, cs_t], [bt])
            lo, hi = min(need), max(need) + 1
            TT(mod[l][:, lo:hi, :], ps[:, bk, 2 * lo:2 * hi].rearrange("p (a b) -> p a b", b=2),
               bada[l][:, lo:hi].unsqueeze(2).to_broadcast([128, hi - lo, 2]), ALU.add, [bt, mod_t], [mod_t])

        def mv(l, k):
            return mod[l][:, 8 * k:8 * k + 8, :]

        vecs = {}

        def mk_AB(name, gain_d, l, kshift, kscale):
            g = sb("g_" + name, [128, 8])
            A = sb("A_" + name, [128, 8, 2])
            B = sb("B_" + name, [128, 8, 2])
            vecs[name + 'A'] = A
            vecs[name + 'B'] = B
            DMA('sp', g[:], gain_d, (), [vec_t], 'l')
            TS(A[:], mv(l, kscale), 1.0, ALU.add, [mod_t, vec_t], [vec_t])
            TT(A[:], A[:], g[:].unsqueeze(2).to_broadcast([128, 8, 2]), ALU.mult, [vec_t], [vec_t])
            CP(B[:], mv(l, kshift), [mod_t, vec_t], [vec_t])

        def mk_gate(name, l, k, mul):
            G = sb("G_" + name, [128, 8, 2])
            vecs[name] = G
            TS(G[:], mv(l, k), float(mul), ALU.mult, [mod_t, vec_t], [vec_t])

        if merge_layer is not None:
            mk_gate('g5', ml, 5, 1.0)
            mk_AB('f2', nf2_d, ml, 6, 7)
            mk_gate('g8', ml, 8, 0.5)
            hgn = sb("hgn", [128, 1])
            DMA('sp', hgn[:], hgn_d, (), [vec_t], 'l')
            dagn = sb("dagn", [128, 1])
            DMA('sp', dagn[:], dag_d, (), [vec_t], 'l')
            TS(dagn[:], dagn[:], float(1.0 - (0.8 - 0.6 * np.exp(-0.3 * ml))), ALU.mult, [vec_t], [vec_t])
        if pre_layer is not None:
            mk_AB('f1', nf1_d, pl, 0, 1)
            mk_gate('g2', pl, 2, 0.5)
            mk_AB('mx', nmix_d, pl, 3, 4)
            oml = sb("oml", [128, 4])
            if pl == 0:
                MS(oml[:], 1.0, [vec_t])
            else:
                lbl = sb("lbl", [128, 4, 2])
                DMA('sp', lbl[:], lbl_d, (), [vec_t], 'l')
                TT(oml[:], lbl[:, :, 0], lbl[:, :, 1], ALU.subtract, [vec_t], [vec_t])
                ACT(oml[:], oml[:], AF.Sigmoid, [vec_t], [vec_t])
        if final:
            fng = sb("fng", [128, 8])
            DMA('sp', fng[:], fnorm_d, (), [vec_t], 'l')

        def load_w8(src):
            wi, wt = wk8_r.next()
            DMA('pool', wk8[:, wi], src, (), [wt], 'w')
            return wi, wt

        def load_w22(src):
            wi, wt = wk22_r.next()
            DMA('pool', wk22[:, wi].rearrange("p a b -> p (a b)").rearrange("p (c d) -> p c d", d=704),
                src.rearrange("p a b -> p (a b)").rearrange("p (c d) -> p c d", d=704), (), [wt], 'w')
            return wi, wt

        def blocks(ts):
            b = [(0, 512, 0), (512, 512, 0)]
            if ts > 1024:
                b.append((1024, ts - 1024, 1))
            return b

        def sumsq_rstd(srcs, bw, grp_lhsT, grp_t, nfeat, use_ln=True):
            bk, bt = ps_r.next()
            nk = len(srcs)
            for k, (sap, stl) in enumerate(srcs):
                si, s_t = sqb_r.next()
                ACT(sqb[:, si, :bw], sap, AF.Square, [stl], [s_t])
                MM(ps[:, bk, :bw], grp_lhsT, sqb[:, si, :bw], k == 0, k == nk - 1, [s_t, grp_t], [bt])
            ri, r_t = rstd_r.next()
            if use_ln:
                ACT(rstd[:, ri, :bw], ps[:, bk, :bw], AF.Ln, [bt, vec_t], [r_t], scale=1.0 / nfeat, bias=epsb[:, 0:1])
                ACT(rstd[:, ri, :bw], rstd[:, ri, :bw], AF.Exp, [r_t], [r_t], scale=-0.5)
            else:
                ACT(rstd[:, ri, :bw], ps[:, bk, :bw], AF.Sqrt, [bt, vec_t], [r_t], scale=1.0 / nfeat, bias=epsb[:, 0:1])
                RCP(rstd[:, ri, :bw], rstd[:, ri, :bw], [r_t], [r_t])
            return ri, r_t

        def norm_mod(ts, A, B):
            for bi, (t0, bw, ci) in enumerate(blocks(ts)):
                ri, r_t = sumsq_rstd([(hT[:, k, t0:t0 + bw], hT_t[k][bi]) for k in range(8)], bw,
                                     ones_b[:], ones_t, D)
                for k in range(8):
                    ti, t_t = tmpf_r.next()
                    TT(tmpf[:, ti, :bw], hT[:, k, t0:t0 + bw], rstd[:, ri, :bw], ALU.mult, [hT_t[k][bi], r_t], [t_t])
                    ACT(xn[:, k, t0:t0 + bw], tmpf[:, ti, :bw], AF.Identity, [t_t, vec_t], [xn_t[k][bi]],
                        scale=A[:, k, ci:ci + 1], bias=B[:, k, ci:ci + 1])

        def ffn(ts, w1_d, w3_d, w2_d, G):
            blks = blocks(ts)
            for j in range(22):
                w1i, w1t = load_w8(w1_d[j])
                w3i, w3t = load_w8(w3_d[j])
                for bi, (t0, bw, ci) in enumerate(blks):
                    bka, bta = ps_r.next()
                    bkb, btb = ps_r.next()
                    for kc in range(8):
                        MM(ps[:, bka, :bw], wk8[:, w1i, kc, :], xn[:, kc, t0:t0 + bw], kc == 0, kc == 7,
                           [w1t, xn_t[kc][bi]], [bta])
                    for kc in range(8):
                        MM(ps[:, bkb, :bw], wk8[:, w3i, kc, :], xn[:, kc, t0:t0 + bw], kc == 0, kc == 7,
                           [w3t, xn_t[kc][bi]], [btb])
                    ti, t_t = tmpf_r.next()
                    ACT(tmpf[:, ti, :bw], ps[:, bka, :bw], AF.Silu, [bta], [t_t])
                    TT(hid[:, j, t0:t0 + bw], tmpf[:, ti, :bw], ps[:, bkb, :bw], ALU.mult, [t_t, btb], [hid_t[j][bi]])
            for i in range(8):
                w2i, w2t = load_w22(w2_d[i])
                for bi, (t0, bw, ci) in enumerate(blks):
                    bk, bt = ps_r.next()
                    for j in range(22):
                        MM(ps[:, bk, :bw], wk22[:, w2i, j, :], hid[:, j, t0:t0 + bw], j == 0, j == 21,
                           [w2t, hid_t[j][bi]], [bt])
                    STT(hT[:, i, t0:t0 + bw], ps[:, bk, :bw], G[:, i, ci:ci + 1], hT[:, i, t0:t0 + bw],
                        ALU.mult, ALU.add, [bt, hT_t[i][bi], vec_t], [hT_t[i][bi]])

        def store(dram_ap, sbuf_ap, tl):
            DMA('sp', dram_ap, sbuf_ap, [tl], (), 's')

        if merge_layer is not None:
            oda = sb("oda", [128, 4, TS_], BF16); oda_t = [[Tl() for _ in range(3)] for _ in range(4)]
            opl = sb("opl", [128, 2, TS_], BF16); opl_t = Tl()
            ohg = sb("ohg", [128, 2, TS_], BF16); ohg_t = [[Tl() for _ in range(3)] for _ in range(2)]
            hgin = sb("hgin", [128, 2, 3, 512]); hgin_r = Ring(2)
            gts = sb("gts", [128, 2, 3, 512]); gts_r = Ring(2)
        if pre_layer is not None:
            cst = sb("cst", [128, 2, TS_]); cst_t = Tl()

        tiles = [(i * 1024, 1024) for i in range(4)]
        if with_ctx:
            tiles[3] = (3072, 1024 + TCTX)
        assert tiles_override is None

        for (T0, ts) in tiles:
            blks = blocks(ts)
            for k in range(8):
                DMA('act', hT[:, k, :ts], hT_in[k * 128:(k + 1) * 128, T0:T0 + ts], (), hT_t[k], 'l')
            if merge_layer is not None:
                DMA('act', oda[:, :, :ts], odaT_d[:, T0:T0 + ts].rearrange("(c p) t -> p c t", p=128), (),
                    [t_ for row in oda_t for t_ in row], 'l')
                DMA('act', opl[:, :, :ts], opoolT_d[:, T0:T0 + ts].rearrange("(c p) t -> p c t", p=128), (), [opl_t], 'l')
                for c in range(4):
                    for bi, (t0, bw, ci) in enumerate(blks):
                        oc = oda[:, c, t0:t0 + bw]
                        si, s_t = sqb_r.next()
                        TT(sqb[:, si, :bw], oc, oc, ALU.mult, [oda_t[c][bi]], [s_t])
                        bk, bt = ps_r.next()
                        MM(ps[:, bk, :bw], ones_b[:], sqb[:, si, :bw], True, True, [s_t, ones_t], [bt])
                        ri, r_t = rstd_r.next()
                        ACT(rstd[:, ri, :bw], ps[:, bk, :bw], AF.Ln, [bt, vec_t], [r_t], scale=1.0 / 128, bias=epsb[:, 0:1])
                        ACT(rstd[:, ri, :bw], rstd[:, ri, :bw], AF.Exp, [r_t], [r_t], scale=-0.5)
                        STT(oc, oc, dagn[:, 0:1], rstd[:, ri, :bw], ALU.mult, ALU.mult, [oda_t[c][bi], r_t, vec_t],
                            [oda_t[c][bi]])
                for c2 in range(2):
                    for bi, (t0, bw, ci) in enumerate(blks):
                        hi_, h_t = hgin_r.next()
                        for s_i, src in enumerate((ohfT_d, ohbT_d, sgT_in_d)):
                            DMA('act', hgin[:, hi_, s_i, :bw], src[c2 * 128:(c2 + 1) * 128, T0 + t0:T0 + t0 + bw],
                                (), [h_t], 'l')
                        o0 = hgin[:, hi_, 0, :bw]
                        TT(o0, o0, hgin[:, hi_, 1, :bw], ALU.add, [h_t], [h_t])
                        ri, r_t = sumsq_rstd([(o0, h_t)], bw, bd64_b[:], bd64_t, 64)
                        TT(o0, o0, rstd[:, ri, :bw], ALU.mult, [h_t, r_t], [h_t])
                        STT(ohg[:, c2, t0:t0 + bw], o0, hgn[:, 0:1], hgin[:, hi_, 2, :bw], ALU.mult, ALU.mult,
                            [h_t, vec_t], [ohg_t[c2][bi]])
                gview = gateT_in_d.rearrange("(b c p) t -> c p b t", b=3, p=128)
                for i in range(8):
                    wi, wt = load_w8(wmrg_d[i])
                    for bi, (t0, bw, ci) in enumerate(blks):
                        gi, g_t = gts_r.next()
                        DMA('act', gts[:, gi, :, :bw], gview[i][:, :, T0 + t0:T0 + t0 + bw], (), [g_t], 'l')
                        bka, bta = ps_r.next()
                        bkh, bth = ps_r.next()
                        bkp, btp = ps_r.next()
                        for c in range(4):
                            MM(ps[:, bka, :bw], wk8[:, wi, c, :], oda[:, c, t0:t0 + bw], c == 0, c == 3,
                               [wt, oda_t[c][bi]], [bta])
                        for c in range(2):
                            MM(ps[:, bkh, :bw], wk8[:, wi, 4 + c, :], ohg[:, c, t0:t0 + bw], c == 0, c == 1,
                               [wt, ohg_t[c][bi]], [bth])
                        for c in range(2):
                            MM(ps[:, bkp, :bw], wk8[:, wi, 6 + c, :], opl[:, c, t0:t0 + bw], c == 0, c == 1, [wt, opl_t], [btp])
                        t1, t1t = tmpf_r.next()
                        t2, t2t = tmpf_r.next()
                        TT(tmpf[:, t1, :bw], gts[:, gi, 0, :bw], ps[:, bka, :bw], ALU.mult, [g_t, bta], [t1t])
                        TT(tmpf[:, t2, :bw], gts[:, gi, 1, :bw], ps[:, bkh, :bw], ALU.mult, [g_t, bth], [t2t])
                        TT(tmpf[:, t1, :bw], tmpf[:, t1, :bw], tmpf[:, t2, :bw], ALU.add, [t1t, t2t], [t1t])
                        TT(tmpf[:, t2, :bw], gts[:, gi, 2, :bw], ps[:, bkp, :bw], ALU.mult, [g_t, btp, t2t], [t2t])
                        TT(xn[:, i, t0:t0 + bw], tmpf[:, t1, :bw], tmpf[:, t2, :bw], ALU.add, [t1t, t2t], [xn_t[i][bi]])
                G5 = vecs['g5']
                for i in range(8):
                    wi, wt = load_w8(wout_d[i])
                    for bi, (t0, bw, ci) in enumerate(blks):
                        bk, bt = ps_r.next()
                        for kc in range(8):
                            MM(ps[:, bk, :bw], wk8[:, wi, kc, :], xn[:, kc, t0:t0 + bw], kc == 0, kc == 7,
                               [wt, xn_t[kc][bi]], [bt])
                        STT(hT[:, i, t0:t0 + bw], ps[:, bk, :bw], G5[:, i, ci:ci + 1], hT[:, i, t0:t0 + bw],
                            ALU.mult, ALU.add, [bt, hT_t[i][bi], vec_t], [hT_t[i][bi]])
                norm_mod(ts, vecs['f2A'], vecs['f2B'])
                ffn(ts, f2w1_d, f2w3_d, f2w2_d, vecs['g8'])
            if final:
                for bi, (t0, bw, ci) in enumerate(blks):
                    ri, r_t = sumsq_rstd([(hT[:, k, t0:t0 + bw], hT_t[k][bi]) for k in range(8)], bw,
                                         ones_b[:], ones_t, D)
                    for k in range(8):
                        si, s_t = stgf_r.next()
                        STT(stgf[:, si, :bw], hT[:, k, t0:t0 + bw], fng[:, k:k + 1], rstd[:, ri, :bw],
                            ALU.mult, ALU.mult, [hT_t[k][bi], r_t, vec_t], [s_t])
                        store(outT_d[k * 128:(k + 1) * 128, T0 + t0:T0 + t0 + bw], stgf[:, si, :bw], s_t)
            if pre_layer is not None:
                if dbg >= 1:
                    norm_mod(ts, vecs['f1A'], vecs['f1B'])
                if dbg >= 2:
                    ffn(ts, f1w1_d, f1w3_d, f1w2_d, vecs['g2'])
                for k in range(8):
                    DMA('sp', hT_out[k * 128:(k + 1) * 128, T0:T0 + ts], hT[:, k, :ts], hT_t[k], (), 's')
                if dbg < 3:
                    continue
                norm_mod(ts, vecs['mxA'], vecs['mxB'])
                DMA('act', cst[:, 0, :ts], cosT_d[:, T0:T0 + ts], (), [cst_t], 'l')
                DMA('act', cst[:, 1, :ts], sinT_d[:, T0:T0 + ts], (), [cst_t], 'l')
                for c in range(48):
                    wi, wt = load_w8(win_d[c])
                    for bi, (t0, bw, ci) in enumerate(blks):
                        bk, bt = ps_r.next()
                        for kc in range(8):
                            MM(ps[:, bk, :bw], wk8[:, wi, kc, :], xn[:, kc, t0:t0 + bw], kc == 0, kc == 7,
                               [wt, xn_t[kc][bi]], [bt])
                        zin = ps[:, bk, :bw]
                        tok = slice(T0 + t0, T0 + t0 + bw)
                        r2 = slice((c % 2) * 128, (c % 2 + 1) * 128)
                        if c < 8:
                            dst = (qT_d if c < 4 else kT_d)[(c % 4) * 128:(c % 4 + 1) * 128, tok]
                            si, s_t = sqb_r.next()
                            ACT(sqb[:, si, :bw], zin, AF.Copy, [bt], [s_t])
                            bk2, bt2 = ps_r.next()
                            MM(ps[:, bk2, :bw], rperm_b[:], sqb[:, si, :bw], True, True, [s_t, rperm_t], [bt2])
                            t1, t1t = tmpf_r.next()
                            t2, t2t = tmpf_r.next()
                            TT(tmpf[:, t1, :bw], zin, cst[:, 0, t0:t0 + bw], ALU.mult, [bt, cst_t], [t1t])
                            TT(tmpf[:, t2, :bw], ps[:, bk2, :bw], cst[:, 1, t0:t0 + bw], ALU.mult, [bt2, cst_t], [t2t])
                            oi, o_t = stgb_r.next()
                            TT(stgb[:, oi, :bw], tmpf[:, t1, :bw], tmpf[:, t2, :bw], ALU.add, [t1t, t2t], [o_t])
                            store(dst, stgb[:, oi, :bw], o_t)
                        elif c < 12 or 18 <= c < 20 or 22 <= c < 24:
                            if c < 12:
                                dst = vT_d[(c - 8) * 128:(c - 7) * 128, tok]
                            elif c < 20:
                                dst = hiT_d[r2, tok]
                            else:
                                dst = zpT_d[r2, tok]
                            oi, o_t = stgb_r.next()
                            CP(stgb[:, oi, :bw], zin, [bt], [o_t])
                            store(dst, stgb[:, oi, :bw], o_t)
                        elif c < 14 or 20 <= c < 22:
                            dst = (hqT_d if c < 14 else sgT_d)[r2, tok]
                            oi, o_t = stgf_r.next()
                            ACT(stgf[:, oi, :bw], zin, AF.Silu, [bt], [o_t])
                            store(dst, stgf[:, oi, :bw], o_t)
                        elif c < 18:
                            di = (c - 14) // 2
                            col = c - 14
                            kd = (kkfT_d, kkbT_d)[di][r2, tok]
                            ld = (lffT_d, lfbT_d)[di][r2, tok]
                            oi, o_t = stgf_r.next()
                            ACT(stgf[:, oi, :bw], zin, AF.Sigmoid, [bt], [o_t], scale=-1.0)
                            TS(stgf[:, oi, :bw], stgf[:, oi, :bw], oml[:, col:col + 1], ALU.mult, [o_t, vec_t], [o_t])
                            store(kd, stgf[:, oi, :bw], o_t)
                            o2, o2_t = stgf_r.next()
                            ACT(stgf[:, o2, :bw], stgf[:, oi, :bw], AF.Ln, [o_t, vec_t], [o2_t], scale=-1.0,
                                bias=oneb[:, 0:1])
                            store(ld, stgf[:, o2, :bw], o2_t)
                        else:
                            dst = gateT_d[(c - 24) * 128:(c - 23) * 128, tok]
                            oi, o_t = stgf_r.next()
                            ACT(stgf[:, oi, :bw], zin, AF.Sigmoid, [bt], [o_t])
                            store(dst, stgf[:, oi, :bw], o_t)
        P.emit(nc)
    return nc


def host_consts():
    ones = np.ones((128, 128), np.float32)
    bd64 = np.zeros((128, 128), np.float32)
    bd64[:64, :64] = 1
    bd64[64:, 64:] = 1
    R = np.zeros((128, 128), np.float32)
    for blk in (0, 64):
        for j in range(16):
            R[blk + j, blk + 16 + j] = -1
            R[blk + 16 + j, blk + j] = 1
            R[blk + 32 + j, blk + 48 + j] = -1
            R[blk + 48 + j, blk + 32 + j] = 1
    return ones, bd64, np.ascontiguousarray(R.T)


def rope_tables():
    t = np.arange(SEQ)
    row = (t // 64).astype(np.float32)
    col = (t % 64).astype(np.float32)
    inv = (np.float32(10000.0) ** (-np.arange(0, 32, 2, dtype=np.float32) / np.float32(32))).astype(np.float32)
    ar = (row[:, None] * inv[None]).astype(np.float32)
    ac = (col[:, None] * inv[None]).astype(np.float32)
    cos64 = np.concatenate([np.cos(ar), np.cos(ar), np.cos(ac), np.cos(ac)], 1).astype(np.float32)
    sin64 = np.concatenate([np.sin(ar), np.sin(ar), np.sin(ac), np.sin(ac)], 1).astype(np.float32)
    return np.tile(cos64.T, (2, 1)), np.tile(sin64.T, (2, 1))


def core_bq(c):
    return c // 4, c % 4


class HostW:
    def __init__(self, inp):
        self.inp = inp
        self.ones, self.bd64, self.rperm = host_consts()
        self.cosT, self.sinT = rope_tables()
        self.cache = {}

    def get(self, key, fn):
        if key not in self.cache:
            self.cache[key] = fn()
        return self.cache[key]

    def common(self, c, layers):
        inp = self.inp
        b, q = core_bq(c)
        d = dict(ones=self.ones, bd64=self.bd64, rperm=self.rperm)
        d['cs'] = np.ascontiguousarray(np.stack([vec_p(inp['c'][b]), vec_p(inp['c_ctx'])], axis=2))
        for l in layers:
            d[f'wada{l}'] = self.get(('wada', l), lambda: arr_w(inp['w_ada'][l]))
            d[f'bada{l}'] = self.get(('bada', l), lambda: vec_p(inp['b_ada'][l]))
        return d

    def pre(self, c, l, with_ctx=True):
        inp = self.inp
        b, q = core_bq(c)
        d = {}
        d['nf1'] = vec_p(inp['norm_ffn1'][l])
        d['nmix'] = vec_p(inp['norm_mix'][l])
        d['f1w1'] = self.get(('f1w1', l), lambda: arr_w(inp['ffn1_w1'][l]))
        d['f1w3'] = self.get(('f1w3', l), lambda: arr_w(inp['ffn1_w3'][l]))
        d['f1w2'] = self.get(('f1w2', l), lambda: arr_w(inp['ffn1_w2'][l]))
        d['win'] = self.get(('win', l), lambda: arr_w(inp['w_in'][l]))
        lg = inp['hg_lb_logits']
        lbl = np.zeros((128, 4, 2), np.float32)
        for di in range(2):
            for ch in range(2):
                for dep in range(2):
                    lbl[:, di * 2 + ch, dep] = lg[dep, di, ch * 128:(ch + 1) * 128]
        d['lbl'] = lbl
        cosT = self.cosT[:, q * TLAT:(q + 1) * TLAT]
        sinT = self.sinT[:, q * TLAT:(q + 1) * TLAT]
        if with_ctx:
            cosT = np.concatenate([cosT, np.ones((128, TCTX), np.float32)], 1)
            sinT = np.concatenate([sinT, np.zeros((128, TCTX), np.float32)], 1)
        d['cosT'] = np.ascontiguousarray(cosT)
        d['sinT'] = np.ascontiguousarray(sinT)
        return d

    def mrg(self, c, l):
        inp = self.inp
        d = {}
        d['nf2'] = vec_p(inp['norm_ffn2'][l])
        d['f2w1'] = self.get(('f2w1', l), lambda: arr_w(inp['ffn2_w1'][l]))
        d['f2w3'] = self.get(('f2w3', l), lambda: arr_w(inp['ffn2_w3'][l]))
        d['f2w2'] = self.get(('f2w2', l), lambda: arr_w(inp['ffn2_w2'][l]))
        d['wmrg'] = self.get(('wmrg', l), lambda: arr_w(np.concatenate(
            [inp['w_proj_da'][l], inp['w_proj_hg'][l], inp['w_proj_pool'][l]], 0)))
        d['wout'] = self.get(('wout', l), lambda: arr_w(inp['w_out'][l]))
        d['hgn'] = np.ascontiguousarray(np.tile(inp['hg_norm'][l], 2)[:, None])
        d['dag'] = np.ascontiguousarray(inp['da_subln'][l][:, None])
        return d


def initial_hT(inp, c):
    b, q = core_bq(c)
    xs = inp['x'][b, q * TLAT:(q + 1) * TLAT]
    cx = inp['ctx'][b, q * TCTX:(q + 1) * TCTX]
    return np.ascontiguousarray(np.concatenate([xs, cx], 0).T)


NQC = SEQ + CTX


def build_A(nq_tiles=32, nkb=130, with_ctxq=True):
    import contextlib
    nc = bass.Bass("TRN2", target_bir_lowering=False)
    P = Prog()
    O = Ops(P)
    MM, ACT, TT, TS, STT, CP, RCP, MS, DMA = O.MM, O.ACT, O.TT, O.TS, O.STT, O.CP, O.RCP, O.MS, O.DMA

    def din(name, shape, dt=F32):
        return nc.dram_tensor(name, list(shape), dt, kind="ExternalInput").ap()

    qT_d = din("qT", [128, NQC], BF16)
    kT_d = din("kT", [128, NKEY], BF16)
    v_d = din("v", [128, 130, 128], BF16)
    lamv_d = din("lamv", [128, 4, 64])
    lami_d = din("lami", [128, 2])
    gain_d = din("gain", [128, 1])
    ones_d = din("ones", [128, 128])
    oT_d = nc.dram_tensor("oT", [128, NQC], BF16, kind="ExternalOutput").ap()

    with contextlib.ExitStack() as st:
        def sb(name, shape, dt=F32):
            return st.enter_context(nc.sbuf_tensor("s_" + name, list(shape), dt))

        qT = sb("qT", [128, NQC], BF16)
        kT = sb("kT", [128, NKEY], BF16); kT_t = Tl(const=True)
        v = sb("v", [128, 130, 128], BF16); v_t = Tl(const=True)
        NPB = 12
        pb = sb("pb", [128, NPB, 2, 512], BF16); pb_r = Ring(NPB)
        tq = sb("tq", [128, 4, 2, 512], BF16); tq_r = Ring(4)
        acc = sb("acc", [128, 2, 512]); acc_t = Tl()
        fin = sb("fin", [128, 4, 512]); fin_r = Ring(4)
        ocp = sb("ocp", [128, 2, 2, 512]); oc_r = Ring(2)
        sqb = sb("sqb", [128, 512], BF16); sqb_t = Tl()
        ob = sb("ob", [128, 2, 512], BF16); ob_r = Ring(2)
        ones_f = sb("ones_f", [128, 128]); ones_b = sb("ones_b", [128, 128], BF16); c_t = Tl(const=True)
        lamv = sb("lamv", [128, 4, 64]); lami = sb("lami", [128, 2]); gain = sb("gain", [128, 1])
        lw = sb("lw", [128, 2, 64]); ls = sb("ls", [128, 2]); neglam = sb("neglam", [128, 1]); gsc = sb("gsc", [128, 1])
        epsb = sb("epsb", [128, 1])
        ps = st.enter_context(nc.psum_tensor("ps", [128, 8, 512], F32))
        s_r = Ring(6, excl=True)
        o_t = [Tl(excl=True), Tl(excl=True)]

        MS(epsb[:], EPS, [c_t])
        DMA('sp', ones_f[:], ones_d, (), [c_t], 'l')
        onesb_t = Tl(const=True)
        DMA('pool', ones_b[:], ones_d, (), [onesb_t], 'l')
        DMA('sp', lamv[:], lamv_d, (), [c_t], 'l')
        DMA('sp', lami[:], lami_d, (), [c_t], 'l')
        DMA('sp', gain[:], gain_d, (), [c_t], 'l')
        TT(lw[:], lamv[:, 0:4:2, :], lamv[:, 1:4:2, :], ALU.mult, [c_t], [c_t])
        P.op('dve', 'tensor_reduce', dict(out=ls[:], in_=lw[:], axis=mybir.AxisListType.X, op=ALU.add), [c_t], [c_t])
        ACT(ls[:], ls[:], AF.Exp, [c_t], [c_t])
        TT(neglam[:], ls[:, 0:1], ls[:, 1:2], ALU.subtract, [c_t], [c_t])
        STT(neglam[:], neglam[:], -1.0, lami[:, 0:1], ALU.mult, ALU.subtract, [c_t], [c_t])
        TT(gsc[:], gain[:], lami[:, 1:2], ALU.mult, [c_t], [c_t])

        for i in range(0, NKEY, 2080):
            DMA('sp', kT[:, i:i + 2080], kT_d[:, i:i + 2080], (), [kT_t], 'l')
        for i in range(0, 130, 13):
            DMA('sp', v[:, i:i + 13, :], v_d[:, i:i + 13, :], (), [v_t], 'l')

        qtiles = [(i * 512, 512, 0, nkb) for i in range(nq_tiles)]
        if with_ctxq:
            qtiles.append((SEQ, CTX, 128, 130))
        q_t = [Tl() for _ in qtiles]
        for qi, (q0, qw, kb0, kb1) in enumerate(qtiles):
            DMA('sp', qT[:, q0:q0 + qw], qT_d[:, q0:q0 + qw], (), [q_t[qi]], 'l')

        blocks = [(qi, kb) for qi, (q0, qw, kb0, kb1) in enumerate(qtiles) for kb in range(kb0, kb1)]
        LA = 2
        sbanks = {}

        def emit_qk(bi):
            qi, kb = blocks[bi]
            q0, qw, kb0, kb1 = qtiles[qi]
            banks = [s_r.next(), s_r.next()]
            sbanks[bi] = banks
            for c in range(2):
                bk, bt = banks[c]
                MM(ps[:, bk, :qw], kT[c * 64:(c + 1) * 64, kb * 128:(kb + 1) * 128],
                   qT[c * 64:(c + 1) * 64, q0:q0 + qw], True, True, [kT_t, q_t[qi]], [bt])

        pend = []
        accst = {'first': True, 'prev': None}

        def accumulate(xap, x_t, qw):
            if accst['first']:
                CP(acc[:, :, :qw], xap, [x_t], [acc_t])
                accst['first'] = False
            else:
                TT(acc[:, :, :qw], acc[:, :, :qw], xap, ALU.add, [x_t, acc_t], [acc_t])

        def emit_rest(bi):
            qi, kb = blocks[bi]
            q0, qw, kb0, kb1 = qtiles[qi]
            first = kb == kb0
            last = kb == kb1 - 1
            banks = sbanks.pop(bi)
            (bk0, bt0), (bk1, bt1) = banks
            assert bk1 == bk0 + 1
            pi, p_t = pb_r.next()
            ACT(pb[:, pi, :, :qw], ps[:, bk0:bk0 + 2, :qw], AF.Exp, [bt0, bt1], [p_t], scale=0.125)
            for c in range(2):
                MM(ps[:, 6 + c, :qw], v[:, kb, :], pb[:, pi, c, :qw], first, last, [v_t, p_t], [o_t[c]])
            if first:
                accst['first'] = True
                accst['prev'] = None
            if accst['prev'] is None and not last:
                accst['prev'] = (pi, p_t)
            else:
                if accst['prev'] is None:
                    pend.append((pb[:, pi, :, :qw], p_t))
                else:
                    ppi, pp_t = accst['prev']
                    accst['prev'] = None
                    ti, t_t = tq_r.next()
                    TT(tq[:, ti, :, :qw], pb[:, ppi, :, :qw], pb[:, pi, :, :qw], ALU.add, [pp_t, p_t], [t_t])
                    pend.append((tq[:, ti, :, :qw], t_t))
                if len(pend) == 2:
                    (xa, xa_t), (xb_, xb_t) = pend
                    TT(xa, xa, xb_, ALU.add, [xa_t, xb_t], [xa_t])
                    accumulate(xa, xa_t, qw)
                    del pend[:]
                if last and pend:
                    accumulate(pend[0][0], pend[0][1], qw)
                    del pend[:]
            if not last:
                return
            oci, oc_t = oc_r.next()
            for c in range(2):
                ACT(ocp[:, oci, c, :qw], ps[:, 6 + c, :qw], AF.Copy, [o_t[c]], [oc_t])
            ts_ = []
            for c in range(2):
                bk, bt = s_r.next()
                MM(ps[:, bk, :qw], ones_f[:], acc[:, c, :qw], True, True, [c_t, acc_t], [bt])
                fi, f_t = fin_r.next()
                RCP(fin[:, fi, :qw], ps[:, bk, :qw], [bt], [f_t])
                TT(fin[:, fi, :qw], ocp[:, oci, c, :qw], fin[:, fi, :qw], ALU.mult, [oc_t, f_t], [f_t])
                ts_.append((fi, f_t))
            (f0, f0t), (f1, f1t) = ts_
            oi, ob_t = ob_r.next()
            STT(ob[:, oi, :qw], fin[:, f1, :qw], neglam[:, 0:1], fin[:, f0, :qw], ALU.mult, ALU.add,
                [f0t, f1t, c_t], [ob_t])
            DMA('sp', oT_d[:, q0:q0 + qw], ob[:, oi, :qw], [ob_t], (), 's')

        base = 0
        for qi, (q0, qw, kb0, kb1) in enumerate(qtiles):
            n = kb1 - kb0
            if s_r.i % 2:
                s_r.next()
            for j in range(n + LA):
                if j < n:
                    emit_qk(base + j)
                if j - LA >= 0:
                    emit_rest(base + j - LA)
            base += n
        P.emit(nc)
    return nc


NCH = NKEY // 64


def build_H(groups=None, pool_tiles=128, do_pool=True):
    import contextlib
    nc = bass.Bass("TRN2", target_bir_lowering=False)
    P = Prog()
    O = Ops(P)
    MM, TR, ACT, TT, TS, STT, CP, RCP, MS, DMA = O.MM, O.TR, O.ACT, O.TT, O.TS, O.STT, O.CP, O.RCP, O.MS, O.DMA
    if groups is None:
        groups = [(0, 4)] + [(4 + 8 * i, 8) for i in range(32)]

    def din(name, shape, dt=F32):
        return nc.dram_tensor(name, list(shape), dt, kind="ExternalInput").ap()

    hq_d = [din(f"hq{s}", [64, NKEY]) for s in range(2)]
    kk_d = [din(f"kk{s}", [64, NKEY]) for s in range(2)]
    lf_d = [din(f"lf{s}", [64, NKEY]) for s in range(2)]
    vt_d = [din(f"vt{s}", [64, NCH, 64], BF16) for s in range(2)]
    reset_d = din("reset", [64, 512])
    mask_d = din("mask", [64, 64])
    ident_d = din("ident", [64, 64])
    oT_d = [nc.dram_tensor(f"oT{s}", [64, NKEY], F32, kind="ExternalOutput").ap() for s in range(2)]
    if do_pool:
        zp_d = din("zp", [128, 130, 64], BF16)
        band_d = din("band", [5, 128, 128])
        pw_d = din("pw", [64, 64])
        psc_d = din("psc", [64, 1])
        opT_d = nc.dram_tensor("opT", [64, NKEY], BF16, kind="ExternalOutput").ap()

    with contextlib.ExitStack() as st:
        def sb(name, shape, dt=F32):
            return st.enter_context(nc.sbuf_tensor("s_" + name, list(shape), dt))

        reset = sb("reset", [64, 512]); mask = sb("mask", [64, 64]); c_t = Tl(const=True)
        ident = sb("ident", [64, 64], BF16); cb_t = Tl(const=True)
        DMA('sp', reset[:], reset_d, (), [c_t], 'l')
        DMA('sp', mask[:], mask_d, (), [c_t], 'l')
        DMA('pool', ident[:], ident_d, (), [cb_t], 'l')
        ps = st.enter_context(nc.psum_tensor("ps", [128, 8, 512], F32))
        sc_t = Tl(excl=True)
        kt_t = Tl(excl=True)
        ot_t = [[Tl(excl=True), Tl(excl=True)], [Tl(excl=True), Tl(excl=True)]]
        kv_t = [Tl(excl=True), Tl(excl=True)]
        kt_bf = ps[:, 1, :].bitcast(BF16)

        S = []
        for s in range(2):
            d = {}
            d['gin'] = sb(f"gin{s}", [64, 2, 3, 512]); d['gin_r'] = Ring(2)
            d['a'] = sb(f"a{s}", [64, 2, 512]); d['a_r'] = Ring(2)
            d['e1'] = sb(f"e1{s}", [64, 2, 512]); d['e1_r'] = Ring(2)
            d['e2'] = sb(f"e2{s}", [64, 2, 512]); d['e2_r'] = Ring(2)
            d['qp'] = sb(f"qp{s}", [64, 2, 512], BF16); d['qp_r'] = Ring(2)
            d['kp'] = sb(f"kp{s}", [64, 2, 512], BF16); d['kp_r'] = Ring(2)
            d['sm'] = sb(f"sm{s}", [64, 2, 512], BF16); d['sm_r'] = Ring(2)
            d['ktok'] = sb(f"ktok{s}", [64, 2, 512], BF16); d['ktok_r'] = Ring(2)
            d['sc1'] = sb(f"sc1{s}", [64, 2, 8]); d['sc1_r'] = Ring(2)
            d['ser'] = sb(f"ser{s}", [64, 2, 8]); d['ser_r'] = Ring(2)
            d['v'] = sb(f"v{s}", [64, NCH, 64], BF16); d['v_t'] = Tl(const=True)
            d['state'] = sb(f"state{s}", [64, 2, 64]); d['state_t'] = [Tl(), Tl()]; d['sp'] = 0
            d['sr'] = sb(f"sr{s}", [64, 64], BF16); d['sr_t'] = Tl()
            d['ost'] = sb(f"ost{s}", [64, 2, 512]); d['ost_r'] = Ring(2)
            MS(d['state'][:], 0.0, d['state_t'])
            for t_ in d['sm_r'].t:
                pass
            MS(d['sm'][:], 0.0, d['sm_r'].t)
            for i in range(0, NCH, 52):
                DMA('sp', d['v'][:, i:i + 52, :], vt_d[s][:, i:i + 52, :], (), [d['v_t']], 'l')
            S.append(d)

        def pre(s, g, out):
            d = S[s]
            c0, n = groups[g]
            W = 64 * n
            t0 = 64 * c0
            gi, g_t = d['gin_r'].next()
            for j, src in enumerate((hq_d[s], kk_d[s], lf_d[s])):
                DMA('sp', d['gin'][:, gi, j, :W], src[:, t0:t0 + W], (), [g_t], 'l')
            yield
            ai, a_t = d['a_r'].next()
            a = d['a'][:, ai, :W]
            P.op('dve', 'tensor_tensor_scan', dict(out=a, data0=reset[:, :W], data1=d['gin'][:, gi, 2, :W], initial=0.0,
                                                   op0=ALU.mult, op1=ALU.add), [g_t, c_t], [a_t])
            a3 = a.rearrange("p (c t) -> p c t", t=64)
            yield
            s1i, s1_t = d['sc1_r'].next()
            eri, er_t = d['ser_r'].next()
            ACT(d['sc1'][:, s1i, :n], a3[:, :, 63], AF.Exp, [a_t], [s1_t])
            ACT(d['ser'][:, eri, :n], a3[:, :, 31], AF.Exp, [a_t], [er_t])
            e2i, e2_t = d['e2_r'].next()
            dd = d['e2'][:, e2i, :W]
            TT(dd.rearrange("p (c t) -> p c t", t=64), a3, a3[:, :, 31:32].to_broadcast([64, n, 64]), ALU.subtract,
               [a_t], [e2_t])
            yield
            e1i, e1_t = d['e1_r'].next()
            e1 = d['e1'][:, e1i, :W]
            ACT(e1, dd, AF.Exp, [e2_t], [e1_t])
            ACT(dd, dd, AF.Exp, [e2_t], [e2_t], scale=-1.0)
            yield
            qi, q_t = d['qp_r'].next()
            ki, k_t = d['kp_r'].next()
            TT(d['qp'][:, qi, :W], d['gin'][:, gi, 0, :W], e1, ALU.mult, [g_t, e1_t], [q_t])
            TT(d['kp'][:, ki, :W], d['gin'][:, gi, 1, :W], dd, ALU.mult, [g_t, e2_t], [k_t])
            yield
            for c in range(n):
                MM(ps[0:64, 0, c * 64 + 32:(c + 1) * 64], d['kp'][:, ki, c * 64:(c + 1) * 64],
                   d['qp'][:, qi, c * 64 + 32:(c + 1) * 64], True, True, [k_t, q_t], [sc_t])
                MM(ps[0:32, 0, c * 64:c * 64 + 32], d['kp'][:, ki, c * 64:c * 64 + 32],
                   d['qp'][:, qi, c * 64:c * 64 + 32], True, True, [k_t, q_t], [sc_t])
            for c in range(n):
                TR(kt_bf[0:64, c * 64:(c + 1) * 64], d['kp'][:, ki, c * 64:(c + 1) * 64], ident[:], [k_t, cb_t], [kt_t])
            smi, sm_t = d['sm_r'].next()
            sm3 = d['sm'][:, smi, :W].rearrange("p (c t) -> p c t", t=64)
            sc3 = ps[0:64, 0, :W].rearrange("p (c t) -> p c t", t=64)
            TT(sm3[:, :, 32:64], sc3[:, :, 32:64], mask[:, 32:64].unsqueeze(1).to_broadcast([64, n, 32]), ALU.mult,
               [sc_t, c_t], [sm_t])
            TT(sm3[0:32, :, 0:32], sc3[0:32, :, 0:32], mask[0:32, 0:32].unsqueeze(1).to_broadcast([32, n, 32]),
               ALU.mult, [sc_t, c_t, sm_t], [sm_t])
            kti, kt2_t = d['ktok_r'].next()
            CP(d['ktok'][:, kti, :W], kt_bf[0:64, :W], [kt_t], [kt2_t])
            out.update(dict(par=g % 2, c0=c0, n=n, W=W, t0=t0, qi=qi, q_t=q_t, smi=smi, sm_t=sm_t, kti=kti, kt2_t=kt2_t,
                            s1i=s1i, s1_t=s1_t, eri=eri, er_t=er_t, e1i=e1i, e1_t=e1_t))

        def step(s, pr, c):
            d = S[s]
            ch = pr['c0'] + c
            cs_ = slice(c * 64, (c + 1) * 64)
            po = d['sp']
            pn = 1 - po
            d['sp'] = pn
            st_o, st_n = d['state'][:, po, :], d['state'][:, pn, :]
            so_t, sn_t = d['state_t'][po], d['state_t'][pn]
            ACT(d['sr'][:], st_o, AF.Copy, [so_t, pr['er_t']], [d['sr_t']],
                scale=d['ser'][:, pr['eri'], c:c + 1])
            ob_ = (2 + s) if pr['par'] == 0 else (6 + s)
            MM(ps[0:64, ob_, cs_], d['v'][:, ch, :], d['sm'][:, pr['smi'], cs_], True, False,
               [d['v_t'], pr['sm_t']], [ot_t[s][pr['par']]])
            MM(ps[0:64, ob_, cs_], d['sr'][:], d['qp'][:, pr['qi'], cs_], False, True,
               [d['sr_t'], pr['q_t']], [ot_t[s][pr['par']]])
            MM(ps[0:64, 4 + s, 0:64], d['ktok'][:, pr['kti'], cs_], d['v'][:, ch, :], True, True,
               [pr['kt2_t'], d['v_t']], [kv_t[s]])
            TS(st_n, st_o, d['sc1'][:, pr['s1i'], c:c + 1], ALU.mult, [so_t, pr['s1_t']], [sn_t])
            STT(st_n, ps[0:64, 4 + s, 0:64], d['e1'][:, pr['e1i'], c * 64 + 63:c * 64 + 64], st_n,
                ALU.mult, ALU.add, [kv_t[s], pr['e1_t'], sn_t], [sn_t])

        def fin(s, pr):
            d = S[s]
            W = pr['W']
            oi, o_t = d['ost_r'].next()
            ob_ = (2 + s) if pr['par'] == 0 else (6 + s)
            ACT(d['ost'][:, oi, :W], ps[0:64, ob_, :W], AF.Copy, [ot_t[s][pr['par']]], [o_t])
            DMA('pool', oT_d[s][:, pr['t0']:pr['t0'] + W], d['ost'][:, oi, :W], [o_t], (), 's')

        def drain(gens):
            for gg in gens:
                for _ in gg:
                    pass

        prs = [{}, {}]
        drain([pre(0, 0, prs[0]), pre(1, 0, prs[1])])
        for g in range(len(groups)):
            nxt, gens = None, []
            if g + 1 < len(groups):
                nxt = [{}, {}]
                gens = [pre(0, g + 1, nxt[0]), pre(1, g + 1, nxt[1])]
            for c in range(groups[g][1]):
                step(0, prs[0], c)
                step(1, prs[1], c)
                for gg in gens:
                    next(gg, None)
            drain(gens)
            fin(0, prs[0])
            fin(1, prs[1])
            prs = nxt

        if do_pool:
            zp = sb("zp", [128, 130, 64], BF16); zp_t = Tl(const=True)
            band = sb("band", [128, 5, 128], BF16); pw = sb("pw", [64, 64], BF16); pc_t = Tl(const=True)
            psc = sb("psc", [64, 1]); pcs_t = Tl(const=True)
            mxb = sb("mxb", [64, 2, 512], BF16); mxb_r = Ring(2)
            pob = sb("pob", [64, 2, 512], BF16); pob_r = Ring(2)
            for i in range(0, 130, 26):
                DMA('sp', zp[:, i:i + 26, :], zp_d[:, i:i + 26, :], (), [zp_t], 'l')
            for i in range(5):
                DMA('pool', band[:, i, :], band_d[i], (), [pc_t], 'l')
            DMA('pool', pw[:], pw_d, (), [pc_t], 'l')
            DMA('sp', psc[:], psc_d, (), [pcs_t], 'l')
            mx_t = ot_t[0][1]
            py_t = ot_t[1][1]
            seqs = [(0, pool_tiles, 256 // 1)] if False else None
            plan = []
            for (tb, nt, tok0) in ((0, pool_tiles, CTX), (128, 2, 0)):
                for g0 in range(0, nt, 4):
                    plan.append((tb, nt, tok0, g0, min(4, nt - g0)))
            for (tb, nt, tok0, g0, gn) in plan:
                for j in range(gn):
                    ti = g0 + j
                    terms = []
                    if ti == 0:
                        terms.append((ti, 1))
                    elif ti == nt - 1:
                        terms.append((ti, 2))
                    else:
                        terms.append((ti, 0))
                    if ti > 0:
                        terms.append((ti - 1, 3))
                    if ti < nt - 1:
                        terms.append((ti + 1, 4))
                    for k, (src, bi) in enumerate(terms):
                        MM(ps[0:64, 6, j * 128:(j + 1) * 128], zp[:, tb + src, :], band[:, bi, :], k == 0,
                           k == len(terms) - 1, [zp_t, pc_t], [mx_t])
                mi, m_t = mxb_r.next()
                CP(mxb[:, mi, :gn * 128], ps[0:64, 6, :gn * 128], [mx_t], [m_t])
                MM(ps[0:64, 7, :gn * 128], pw[:], mxb[:, mi, :gn * 128], True, True, [pc_t, m_t], [py_t])
                pi, p_t = pob_r.next()
                ACT(pob[:, pi, :gn * 128], ps[0:64, 7, :gn * 128], AF.Copy, [py_t, pcs_t], [p_t], scale=psc[:, 0:1])
                DMA('pool', opT_d[:, tok0 + g0 * 128:tok0 + (g0 + gn) * 128], pob[:, pi, :gn * 128], [p_t], (), 's')
        P.emit(nc)
    return nc


def h_consts():
    reset = np.ones((64, 512), np.float32)
    reset[:, ::64] = 0.0
    s = np.arange(64)
    mask = (s[:, None] <= s[None, :]).astype(np.float32)
    return dict(reset=reset, mask=mask, ident=np.eye(64, dtype=np.float32))


def zp_tiles(z_lat, z_ctx):
    a = z_lat.reshape(-1, 128, 64).transpose(1, 0, 2)
    b = z_ctx.reshape(-1, 128, 64).transpose(1, 0, 2)
    return np.ascontiguousarray(np.concatenate([a, b], 1))


def band_mats(w):
    h = w // 2
    n = 128 * 3
    t = np.arange(n)
    full = np.zeros((n, n), np.float64)
    for tt in range(n):
        lo, hi = tt - h, tt + h
        for ss in range(max(lo, 0), min(hi, n)):
            full[ss, tt] = 1.0 / w
    Bc = full[128:256, 128:256] - np.eye(128)
    Bp = full[0:128, 128:256]
    Bn = full[256:384, 128:256]
    first = np.zeros((128, 128))
    last = np.zeros((128, 128))
    for tt in range(128):
        lo, hi = max(tt - h, 0), tt + h
        cnt = hi - lo
        for ss in range(lo, min(hi, 128)):
            first[ss, tt] = 1.0 / cnt
        lo2, hi2 = tt - h, min(tt + h, 128)
        cnt2 = hi2 - lo2
        for ss in range(max(lo2, 0), hi2):
            last[ss, tt] = 1.0 / cnt2
    first -= np.eye(128)
    last -= np.eye(128)
    return np.ascontiguousarray(np.stack([Bc, first, last, Bp, Bn]).astype(np.float32))


_PROGS = {}


def _prog(key, fn):
    if key not in _PROGS:
        _PROGS[key] = fn()
    return _PROGS[key]


def _run(nc, maps):
    res = run_bass_kernel_spmd(nc, maps, core_ids=list(range(NCORE)))
    return [{k: np.asarray(v) for k, v in r.items()} for r in res.results]


def _gather_tok(rs, b, key, rows):
    lat = np.concatenate([rs[b * 4 + q][key][rows, :TLAT] for q in range(4)], 1)
    ctx = np.concatenate([rs[b * 4 + q][key][rows, TLAT:] for q in range(4)], 1)
    return lat, ctx


def _mixer_inputs(inp, rs, l):
    lam_init = 0.8 - 0.6 * float(np.exp(-0.3 * l))
    lamv = np.stack([inp['da_lambda_q1'][l], inp['da_lambda_k1'][l], inp['da_lambda_q2'][l], inp['da_lambda_k2'][l]])
    hc = h_consts()
    ones = np.ones((128, 128), np.float32)
    mapsA, mapsH = [], []
    for c in range(NCORE):
        b, h = core_bq(c)
        r128 = slice(h * 128, (h + 1) * 128)
        r64 = slice(h * 64, (h + 1) * 64)
        ql, qc = _gather_tok(rs, b, 'qT', r128)
        kl, kc = _gather_tok(rs, b, 'kT', r128)
        vl, vc = _gather_tok(rs, b, 'vT', r128)
        vtok = np.concatenate([vl, vc], 1).T
        dA = dict(qT=np.ascontiguousarray(np.concatenate([ql, qc], 1)),
                  kT=np.ascontiguousarray(np.concatenate([kl, kc], 1)),
                  v=np.ascontiguousarray(vtok.reshape(130, 128, 128).transpose(1, 0, 2)),
                  lamv=np.ascontiguousarray(np.broadcast_to(lamv[None], (128, 4, 64))).astype(np.float32),
                  lami=np.ascontiguousarray(np.broadcast_to(
                      np.array([lam_init, 1.0 - lam_init], np.float32)[None], (128, 2))),
                  gain=np.ascontiguousarray(inp['da_subln'][l][:, None]), ones=ones)
        mapsA.append(dA)
        dH = dict(hc)

        def scan_order(key, flip):
            lat, ctx = _gather_tok(rs, b, key, r64)
            if flip:
                lat, ctx = lat[:, ::-1], ctx[:, ::-1]
            return np.ascontiguousarray(np.concatenate([ctx, lat], 1))

        for s, (kkey, lkey) in enumerate((('kkfT', 'lffT'), ('kkbT', 'lfbT'))):
            dH[f'hq{s}'] = scan_order('hqT', s == 1)
            dH[f'kk{s}'] = scan_order(kkey, s == 1)
            dH[f'lf{s}'] = scan_order(lkey, s == 1)
            vi = scan_order('hiT', s == 1).T
            dH[f'vt{s}'] = np.ascontiguousarray(vi.reshape(NCH, 64, 64).transpose(1, 0, 2))
        zl, zc = _gather_tok(rs, b, 'zpT', r64)
        dH['zp'] = zp_tiles(np.ascontiguousarray(zl.T), np.ascontiguousarray(zc.T))
        dH['band'] = band_mats(2 ** (h + 1))
        dH['pw'] = np.ascontiguousarray(inp['pool_w'][l][h])
        dH['psc'] = np.ascontiguousarray(inp['pool_scale'][l][r64][:, None])
        mapsH.append(dH)
    return mapsA, mapsH


def _merge_inputs(rs, rA, rH, with_ctx):
    out = []
    for c in range(NCORE):
        b, q = core_bq(c)
        lat = slice(q * TLAT, (q + 1) * TLAT)
        cx = slice(q * TCTX, (q + 1) * TCTX)

        def cat(lat_part, ctx_part):
            return np.ascontiguousarray(np.concatenate([lat_part, ctx_part], 1) if with_ctx else lat_part)

        oda = [cat(rA[b * 4 + h]['oT'][:, lat], rA[b * 4 + h]['oT'][:, SEQ + q * TCTX:SEQ + (q + 1) * TCTX]) for h in range(4)]
        ohf, ohb, opl = [], [], []
        for h in range(4):
            r = rH[b * 4 + h]
            f = r['oT0']
            ohf.append(cat(f[:, CTX:][:, lat], f[:, :CTX][:, cx]))
            bw = r['oT1']
            ohb.append(cat(bw[:, CTX:][:, ::-1][:, lat], bw[:, :CTX][:, ::-1][:, cx]))
            p = r['opT']
            opl.append(cat(p[:, CTX:][:, lat], p[:, :CTX][:, cx]))
        n = NT if with_ctx else TLAT
        d = dict(odaT=np.concatenate(oda, 0), ohfT=np.concatenate(ohf, 0), ohbT=np.concatenate(ohb, 0),
                 opoolT=np.concatenate(opl, 0),
                 sgT_in=np.ascontiguousarray(rs[c]['sgT'][:, :n]),
                 gateT_in=np.ascontiguousarray(rs[c]['gateT'][:, :n]),
                 hT_in=np.ascontiguousarray(rs[c]['hT_out'][:, :n]))
        out.append(d)
    return out


def kernel(**inputs):
    return _forward(inputs)


def _forward(inputs, dbg=None):
    inp = {k: np.asarray(v) for k, v in inputs.items()}
    dbg = dbg or (lambda name, val: None)
    H = HostW(inp)
    ncT1 = _prog('T1', lambda: build_T(None, 0, False, True))
    maps = []
    for c in range(NCORE):
        d = H.common(c, [0])
        d.update(H.pre(c, 0))
        d['hT_in'] = initial_hT(inp, c)
        maps.append(d)
    r1 = _run(ncT1, maps)
    dbg('r1', r1)
    ncA = _prog('A', build_A)
    ncH = _prog('H', build_H)
    mA, mH = _mixer_inputs(inp, r1, 0)
    rA = _run(ncA, mA)
    dbg('rA0', rA)
    rH = _run(ncH, mH)
    dbg('rH0', rH)
    del mA, mH
    ncT2 = _prog('T2', lambda: build_T(0, 1, False, True))
    mm = _merge_inputs(r1, rA, rH, True)
    maps = []
    for c in range(NCORE):
        d = H.common(c, [0, 1])
        d.update(H.mrg(c, 0))
        d.update(H.pre(c, 1))
        d.update(mm[c])
        maps.append(d)
    del r1, rA, rH
    r2 = _run(ncT2, maps)
    dbg('r2', r2)
    mA, mH = _mixer_inputs(inp, r2, 1)
    rA = _run(ncA, mA)
    dbg('rA1', rA)
    rH = _run(ncH, mH)
    dbg('rH1', rH)
    del mA, mH
    ncT3 = _prog('T3', lambda: build_T(1, None, True, False))
    mm = _merge_inputs(r2, rA, rH, False)
    maps = []
    for c in range(NCORE):
        d = H.common(c, [1])
        d.update(H.mrg(c, 1))
        d.update(mm[c])
        d['fnorm'] = vec_p(inp['final_norm'])
        maps.append(d)
    r3 = _run(ncT3, maps)
    out = np.empty((BATCH, SEQ, D), np.float32)
    for c in range(NCORE):
        b, q = core_bq(c)
        out[b, q * TLAT:(q + 1) * TLAT] = r3[c]['outT'].T
    return out
```

```python
import numpy as np
import ml_dtypes
import concourse.bass as bass
import concourse.mybir as mybir
from concourse.bass_utils import run_bass_kernel_spmd

F32 = mybir.dt.float32
BF16 = mybir.dt.bfloat16
AF = mybir.ActivationFunctionType
ALU = mybir.AluOpType
NPBF = ml_dtypes.bfloat16

D = 1024
SEQ = 16384
BATCH = 2
CTX = 256
DFF = 2816
DIN = 6144
NCORE = 8
TLAT = 4096
TCTX = 64
NT = TLAT + TCTX
NKEY = SEQ + CTX
EPS = 1e-6

ENGS = ('pe', 'act', 'dve', 'pool', 'sp')


class Tl:
    __slots__ = ('w', 'r', 'const', 'lsem', 'ssem', 'excl')

    def __init__(self, const=False, excl=False):
        self.w = None
        self.r = []
        self.const = const
        self.excl = excl
        self.lsem = None
        self.ssem = None


class Op:
    __slots__ = ('eng', 'fn', 'deps', 'sig', 'val', 'dsem')


class Prog:
    def __init__(self):
        self.ops = {e: [] for e in ENGS}
        self.dcnt = {}
        self.deng = {}

    def op(self, eng, meth, kw, reads=(), writes=(), dsem=None):
        o = Op()
        o.eng = eng
        o.fn = (meth, kw)
        o.sig = False
        o.val = 0
        o.dsem = dsem
        wr = {}
        rd = {}
        for t in reads:
            if t.w is not None:
                wr[id(t.w)] = t.w
            if t.excl:
                for r in t.r:
                    if r.eng != eng:
                        wr[id(r)] = r
        for t in writes:
            if t.w is not None:
                wr[id(t.w)] = t.w
            for r in t.r:
                rd[id(r)] = r
        deps = []
        for d in wr.values():
            if d.dsem is None and dsem is None and d.eng == eng and eng == 'pe':
                continue
            deps.append(d)
        for d in rd.values():
            if id(d) in wr:
                continue
            if d.dsem is None and dsem is None and d.eng == eng and eng == 'pe':
                continue
            deps.append(d)
        for d in deps:
            if d.dsem is None:
                d.sig = True
        o.deps = deps
        for t in reads:
            if not t.const:
                t.r.append(o)
        for t in writes:
            t.w = o
            t.r = []
        if dsem is not None:
            if len(writes) > 0:
                t = writes[0]
                if t.lsem is None:
                    t.lsem = 'l%d' % len(self.dcnt)
                    self.dcnt[t.lsem] = 0
                dsem = t.lsem
            else:
                t = reads[0]
                if t.ssem is None:
                    t.ssem = 's%d' % len(self.dcnt)
                    self.dcnt[t.ssem] = 0
                dsem = t.ssem
            o.dsem = dsem
            self.dcnt[dsem] = self.dcnt[dsem] + 16
            o.val = self.dcnt[dsem]
        self.ops[eng].append(o)
        return o

    def emit(self, nc):
        for e in ENGS:
            c = 0
            for o in self.ops[e]:
                if o.dsem is None and o.sig:
                    c += 1
                    o.val = c
        import contextlib
        with contextlib.ExitStack() as st:
            esem = {e: st.enter_context(nc.semaphore('es_' + e)) for e in ENGS}
            dsem = {k: st.enter_context(nc.semaphore('ds_' + k)) for k in self.dcnt}
            block = st.enter_context(nc.Block())
            prog = self

            def run(eng_name, e):
                waited = {}
                for o in prog.ops[eng_name]:
                    for d in o.deps:
                        if d.dsem is not None:
                            key = ('d', d.dsem)
                            sem = dsem[d.dsem]
                        else:
                            key = ('e', d.eng)
                            sem = esem[d.eng]
                        if waited.get(key, 0) >= d.val:
                            continue
                        waited[key] = d.val
                        e.wait_ge(sem, d.val)
                    ins = getattr(e, o.fn[0])(**o.fn[1])
                    if o.dsem is not None:
                        ins.then_inc(dsem[o.dsem], 16)
                    elif o.sig:
                        ins.then_inc(esem[eng_name], 1)
                if eng_name == 'sp':
                    for k, v in prog.dcnt.items():
                        e.wait_ge(dsem[k], v)

            @block.tensor
            def _(e):
                run('pe', e)

            @block.scalar
            def _(e):
                run('act', e)

            @block.vector
            def _(e):
                run('dve', e)

            @block.gpsimd
            def _(e):
                run('pool', e)

            @block.sync
            def _(e):
                run('sp', e)


class Ring:
    def __init__(self, n, excl=False):
        self.n = n
        self.i = 0
        self.t = [Tl(excl=excl) for _ in range(n)]

    def next(self):
        k = self.i % self.n
        self.i += 1
        return k, self.t[k]


def arr_w(W):
    K, N = W.shape
    return np.ascontiguousarray(W.reshape(K // 128, 128, N // 128, 128).transpose(2, 1, 0, 3))


def vec_p(v):
    return np.ascontiguousarray(v.reshape(-1, 128).T)


class Ops:
    def __init__(self, P):
        self.P = P

    def MM(self, out, lhsT, rhs, start, stop, reads, writes):
        self.P.op('pe', 'matmul', dict(out=out, lhsT=lhsT, rhs=rhs, start=start, stop=stop), reads, writes)

    def TR(self, out, in_, ident, reads, writes):
        self.P.op('pe', 'transpose', dict(out=out, in_=in_, identity=ident), reads, writes)

    def ACT(self, out, in_, func, reads, writes, **kw):
        self.P.op('act', 'activation', dict(out=out, in_=in_, func=func, **kw), reads, writes)

    def TT(self, out, in0, in1, op, reads, writes, eng='dve'):
        self.P.op(eng, 'tensor_tensor', dict(out=out, in0=in0, in1=in1, op=op), reads, writes)

    def TS(self, out, in0, s1, op0, reads, writes, s2=None, op1=None, eng='dve'):
        kw = dict(out=out, in0=in0, scalar1=s1, scalar2=s2, op0=op0)
        if op1 is not None:
            kw['op1'] = op1
        self.P.op(eng, 'tensor_scalar', kw, reads, writes)

    def STT(self, out, in0, scalar, in1, op0, op1, reads, writes):
        self.P.op('dve', 'scalar_tensor_tensor', dict(out=out, in0=in0, scalar=scalar, in1=in1, op0=op0, op1=op1),
                  reads, writes)

    def CP(self, out, in_, reads, writes, eng='dve'):
        self.P.op(eng, 'tensor_copy', dict(out=out, in_=in_), reads, writes)

    def RCP(self, out, in_, reads, writes, scratch=None):
        if scratch is None:
            self.P.op('dve', 'reciprocal', dict(out=out, in_=in_), reads, writes)
        else:
            self.P.op('dve', 'reciprocal_approx_accurate', dict(out=out, in_=in_, scratch=scratch), reads, writes)

    def MS(self, ap, val, writes, eng='dve'):
        self.P.op(eng, 'memset', dict(ap=ap, constant=val), (), writes)

    def DMA(self, eng, out, in_, reads, writes, dsem):
        self.P.op(eng, 'dma_start', dict(out=out, in_=in_), reads, writes, dsem)


def build_T(merge_layer, pre_layer, final, with_ctx, tiles_override=None, dbg=99):
    import contextlib
    nc = bass.Bass("TRN2", target_bir_lowering=False)
    P = Prog()
    O = Ops(P)
    MM, ACT, TT, TS, STT, CP, RCP, MS, DMA = O.MM, O.ACT, O.TT, O.TS, O.STT, O.CP, O.RCP, O.MS, O.DMA
    ntok = NT if with_ctx else TLAT

    def din(name, shape, dt=F32):
        return nc.dram_tensor(name, list(shape), dt, kind="ExternalInput").ap()

    def dout(name, shape, dt=F32):
        return nc.dram_tensor(name, list(shape), dt, kind="ExternalOutput").ap()

    hT_in = din("hT_in", [D, ntok])
    cs_d = din("cs", [128, 8, 2])
    ones_d = din("ones", [128, 128])
    bd64_d = din("bd64", [128, 128])
    rperm_d = din("rperm", [128, 128])
    layers = sorted(set(x for x in (merge_layer, pre_layer) if x is not None))
    wada_d = {l: din(f"wada{l}", [72, 128, 8, 128]) for l in layers}
    bada_d = {l: din(f"bada{l}", [128, 72]) for l in layers}
    if merge_layer is not None:
        ml = merge_layer
        nf2_d = din("nf2", [128, 8])
        f2w1_d = din("f2w1", [22, 128, 8, 128])
        f2w3_d = din("f2w3", [22, 128, 8, 128])
        f2w2_d = din("f2w2", [8, 128, 22, 128])
        wmrg_d = din("wmrg", [8, 128, 8, 128])
        wout_d = din("wout", [8, 128, 8, 128])
        hgn_d = din("hgn", [128, 1])
        dag_d = din("dag", [128, 1])
        odaT_d = din("odaT", [512, ntok], BF16)
        ohfT_d = din("ohfT", [256, ntok])
        ohbT_d = din("ohbT", [256, ntok])
        sgT_in_d = din("sgT_in", [256, ntok])
        opoolT_d = din("opoolT", [256, ntok], BF16)
        gateT_in_d = din("gateT_in", [3072, ntok])
    if pre_layer is not None:
        pl = pre_layer
        nf1_d = din("nf1", [128, 8])
        nmix_d = din("nmix", [128, 8])
        f1w1_d = din("f1w1", [22, 128, 8, 128])
        f1w3_d = din("f1w3", [22, 128, 8, 128])
        f1w2_d = din("f1w2", [8, 128, 22, 128])
        win_d = din("win", [48, 128, 8, 128])
        lbl_d = din("lbl", [128, 4, 2])
        cosT_d = din("cosT", [128, ntok])
        sinT_d = din("sinT", [128, ntok])
        hT_out = dout("hT_out", [D, ntok])
        qT_d = dout("qT", [512, ntok], BF16)
        kT_d = dout("kT", [512, ntok], BF16)
        vT_d = dout("vT", [512, ntok], BF16)
        hqT_d = dout("hqT", [256, ntok])
        kkfT_d = dout("kkfT", [256, ntok])
        kkbT_d = dout("kkbT", [256, ntok])
        lffT_d = dout("lffT", [256, ntok])
        lfbT_d = dout("lfbT", [256, ntok])
        hiT_d = dout("hiT", [256, ntok], BF16)
        sgT_d = dout("sgT", [256, ntok])
        zpT_d = dout("zpT", [256, ntok], BF16)
        gateT_d = dout("gateT", [3072, ntok])
    if final:
        fnorm_d = din("fnorm", [128, 8])
        outT_d = dout("outT", [D, TLAT])

    with contextlib.ExitStack() as st:
        def sb(name, shape, dt=F32):
            return st.enter_context(nc.sbuf_tensor("s_" + name, list(shape), dt))

        TS_ = 1024 + (TCTX if with_ctx else 0)
        hT = sb("hT", [128, 8, TS_]); hT_t = [[Tl() for _ in range(3)] for _ in range(8)]
        xn = sb("xn", [128, 8, TS_], BF16); xn_t = [[Tl() for _ in range(3)] for _ in range(8)]
        hid = sb("hid", [128, 22, TS_], BF16); hid_t = [[Tl() for _ in range(3)] for _ in range(22)]
        NW8 = 6 if (merge_layer is not None and pre_layer is not None) else 8
        wk8 = sb("wk8", [128, NW8, 8, 128], BF16); wk8_r = Ring(NW8)
        NW22 = 2 if (merge_layer is not None and pre_layer is not None) else 3
        wk22 = sb("wk22", [128, NW22, 22, 128], BF16); wk22_r = Ring(NW22)
        NTS = 3 if (merge_layer is not None and pre_layer is not None) else 4
        tmpf = sb("tmpf", [128, NTS, 512]); tmpf_r = Ring(NTS)
        sqb = sb("sqb", [128, 3, 512], BF16); sqb_r = Ring(3)
        rstd = sb("rstd", [128, 2, 512]); rstd_r = Ring(2)
        stgf = sb("stgf", [128, NTS, 512]); stgf_r = Ring(NTS)
        stgb = sb("stgb", [128, NTS, 512], BF16); stgb_r = Ring(NTS)
        ps = st.enter_context(nc.psum_tensor("ps", [128, 8, 512], F32)); ps_r = Ring(8, excl=True)
        ones_b = sb("ones_b", [128, 128], BF16); ones_t = Tl(const=True)
        bd64_b = sb("bd64_b", [128, 128], BF16); bd64_t = Tl(const=True)
        rperm_b = sb("rperm_b", [128, 128], BF16); rperm_t = Tl(const=True)
        cs = sb("cs", [128, 8, 2]); cs_t = Tl(const=True)
        NWA = 4 if merge_layer is None else (3 if pre_layer is None else 2)
        wada = sb("wada", [128, NWA, 8, 128]); wada_r = Ring(NWA)
        mod = {l: sb(f"mod{l}", [128, 72, 2]) for l in layers}; mod_t = Tl(const=True)
        bada = {l: sb(f"bada{l}", [128, 72]) for l in layers}
        vec_t = Tl(const=True)
        epsb = sb("epsb", [128, 1])
        oneb = sb("oneb", [128, 1])
        MS(epsb[:], EPS, [vec_t])
        MS(oneb[:], 1.0, [vec_t])

        DMA('pool', ones_b[:], ones_d, (), [ones_t], 'w')
        DMA('pool', bd64_b[:], bd64_d, (), [bd64_t], 'w')
        DMA('pool', rperm_b[:], rperm_d, (), [rperm_t], 'w')
        DMA('sp', cs[:], cs_d, (), [cs_t], 'l')
        cs_flat = cs[:].rearrange("p a b -> p (a b)")
        ACT(cs_flat, cs_flat, AF.Silu, [cs_t], [cs_t])

        for l in layers:
            DMA('sp', bada[l][:], bada_d[l], (), [mod_t], 'l')
            bk, bt = ps_r.next()
            need = set()
            if l == pre_layer:
                need.update(range(0, 40))
            if l == merge_layer:
                need.update(range(40, 72))
            MS(mod[l][:], 0.0, [mod_t])
            for fc in sorted(need):
                wi, wt = wada_r.next()
                DMA('sp', wada[:, wi], wada_d[l][fc], (), [wt], 'l')
                for kc in range(8):
                    MM(ps[:, bk, 2 * fc:2 * fc + 2], wada[:, wi, kc, :], cs[:, kc, :], kc == 0, kc == 7,
                       [wt, cs_t], [bt])
            lo, hi = min(need), max(need) + 1
            TT(mod[l][:, lo:hi, :], ps[:, bk, 2 * lo:2 * hi].rearrange("p (a b) -> p a b", b=2),
               bada[l][:, lo:hi].unsqueeze(2).to_broadcast([128, hi - lo, 2]), ALU.add, [bt, mod_t], [mod_t])

        def mv(l, k):
            return mod[l][:, 8 * k:8 * k + 8, :]

        vecs = {}

        def mk_AB(name, gain_d, l, kshift, kscale):
            g = sb("g_" + name, [128, 8])
            A = sb("A_" + name, [128, 8, 2])
            B = sb("B_" + name, [128, 8, 2])
            vecs[name + 'A'] = A
            vecs[name + 'B'] = B
            DMA('sp', g[:], gain_d, (), [vec_t], 'l')
            TS(A[:], mv(l, kscale), 1.0, ALU.add, [mod_t, vec_t], [vec_t])
            TT(A[:], A[:], g[:].unsqueeze(2).to_broadcast([128, 8, 2]), ALU.mult, [vec_t], [vec_t])
            CP(B[:], mv(l, kshift), [mod_t, vec_t], [vec_t])

        def mk_gate(name, l, k, mul):
            G = sb("G_" + name, [128, 8, 2])
            vecs[name] = G
            TS(G[:], mv(l, k), float(mul), ALU.mult, [mod_t, vec_t], [vec_t])

        if merge_layer is not None:
            mk_gate('g5', ml, 5, 1.0)
            mk_AB('f2', nf2_d, ml, 6, 7)
            mk_gate('g8', ml, 8, 0.5)
            hgn = sb("hgn", [128, 1])
            DMA('sp', hgn[:], hgn_d, (), [vec_t], 'l')
            dagn = sb("dagn", [128, 1])
            DMA('sp', dagn[:], dag_d, (), [vec_t], 'l')
            TS(dagn[:], dagn[:], float(1.0 - (0.8 - 0.6 * np.exp(-0.3 * ml))), ALU.mult, [vec_t], [vec_t])
        if pre_layer is not None:
            mk_AB('f1', nf1_d, pl, 0, 1)
            mk_gate('g2', pl, 2, 0.5)
            mk_AB('mx', nmix_d, pl, 3, 4)
            oml = sb("oml", [128, 4])
            if pl == 0:
                MS(oml[:], 1.0, [vec_t])
            else:
                lbl = sb("lbl", [128, 4, 2])
                DMA('sp', lbl[:], lbl_d, (), [vec_t], 'l')
                TT(oml[:], lbl[:, :, 0], lbl[:, :, 1], ALU.subtract, [vec_t], [vec_t])
                ACT(oml[:], oml[:], AF.Sigmoid, [vec_t], [vec_t])
        if final:
            fng = sb("fng", [128, 8])
            DMA('sp', fng[:], fnorm_d, (), [vec_t], 'l')

        def load_w8(src):
            wi, wt = wk8_r.next()
            DMA('pool', wk8[:, wi], src, (), [wt], 'w')
            return wi, wt

        def load_w22(src):
            wi, wt = wk22_r.next()
            DMA('pool', wk22[:, wi].rearrange("p a b -> p (a b)").rearrange("p (c d) -> p c d", d=704),
                src.rearrange("p a b -> p (a b)").rearrange("p (c d) -> p c d", d=704), (), [wt], 'w')
            return wi, wt

        def blocks(ts):
            b = [(0, 512, 0), (512, 512, 0)]
            if ts > 1024:
                b.append((1024, ts - 1024, 1))
            return b

        def sumsq_rstd(srcs, bw, grp_lhsT, grp_t, nfeat, use_ln=True):
            bk, bt = ps_r.next()
            nk = len(srcs)
            for k, (sap, stl) in enumerate(srcs):
                si, s_t = sqb_r.next()
                ACT(sqb[:, si, :bw], sap, AF.Square, [stl], [s_t])
                MM(ps[:, bk, :bw], grp_lhsT, sqb[:, si, :bw], k == 0, k == nk - 1, [s_t, grp_t], [bt])
            ri, r_t = rstd_r.next()
            if use_ln:
                ACT(rstd[:, ri, :bw], ps[:, bk, :bw], AF.Ln, [bt, vec_t], [r_t], scale=1.0 / nfeat, bias=epsb[:, 0:1])
                ACT(rstd[:, ri, :bw], rstd[:, ri, :bw], AF.Exp, [r_t], [r_t], scale=-0.5)
            else:
                ACT(rstd[:, ri, :bw], ps[:, bk, :bw], AF.Sqrt, [bt, vec_t], [r_t], scale=1.0 / nfeat, bias=epsb[:, 0:1])
                RCP(rstd[:, ri, :bw], rstd[:, ri, :bw], [r_t], [r_t])
            return ri, r_t

        def norm_mod(ts, A, B):
            for bi, (t0, bw, ci) in enumerate(blocks(ts)):
                ri, r_t = sumsq_rstd([(hT[:, k, t0:t0 + bw], hT_t[k][bi]) for k in range(8)], bw,
                                     ones_b[:], ones_t, D)
                for k in range(8):
                    ti, t_t = tmpf_r.next()
                    TT(tmpf[:, ti, :bw], hT[:, k, t0:t0 + bw], rstd[:, ri, :bw], ALU.mult, [hT_t[k][bi], r_t], [t_t])
                    ACT(xn[:, k, t0:t0 + bw], tmpf[:, ti, :bw], AF.Identity, [t_t, vec_t], [xn_t[k][bi]],
                        scale=A[:, k, ci:ci + 1], bias=B[:, k, ci:ci + 1])

        def ffn(ts, w1_d, w3_d, w2_d, G):
            blks = blocks(ts)
            for j in range(22):
                w1i, w1t = load_w8(w1_d[j])
                w3i, w3t = load_w8(w3_d[j])
                for bi, (t0, bw, ci) in enumerate(blks):
                    bka, bta = ps_r.next()
                    bkb, btb = ps_r.next()
                    for kc in range(8):
                        MM(ps[:, bka, :bw], wk8[:, w1i, kc, :], xn[:, kc, t0:t0 + bw], kc == 0, kc == 7,
                           [w1t, xn_t[kc][bi]], [bta])
                    for kc in range(8):
                        MM(ps[:, bkb, :bw], wk8[:, w3i, kc, :], xn[:, kc, t0:t0 + bw], kc == 0, kc == 7,
                           [w3t, xn_t[kc][bi]], [btb])
                    ti, t_t = tmpf_r.next()
                    ACT(tmpf[:, ti, :bw], ps[:, bka, :bw], AF.Silu, [bta], [t_t])
                    TT(hid[:, j, t0:t0 + bw], tmpf[:, ti, :bw], ps[:, bkb, :bw], ALU.mult, [t_t, btb], [hid_t[j][bi]])
            for i in range(8):
                w2i, w2t = load_w22(w2_d[i])
                for bi, (t0, bw, ci) in enumerate(blks):
                    bk, bt = ps_r.next()
                    for j in range(22):
                        MM(ps[:, bk, :bw], wk22[:, w2i, j, :], hid[:, j, t0:t0 + bw], j == 0, j == 21,
                           [w2t, hid_t[j][bi]], [bt])
                    STT(hT[:, i, t0:t0 + bw], ps[:, bk, :bw], G[:, i, ci:ci + 1], hT[:, i, t0:t0 + bw],
                        ALU.mult, ALU.add, [bt, hT_t[i][bi], vec_t], [hT_t[i][bi]])

        def store(dram_ap, sbuf_ap, tl):
            DMA('sp', dram_ap, sbuf_ap, [tl], (), 's')

        if merge_layer is not None:
            oda = sb("oda", [128, 4, TS_], BF16); oda_t = [[Tl() for _ in range(3)] for _ in range(4)]
            opl = sb("opl", [128, 2, TS_], BF16); opl_t = Tl()
            ohg = sb("ohg", [128, 2, TS_], BF16); ohg_t = [[Tl() for _ in range(3)] for _ in range(2)]
            hgin = sb("hgin", [128, 2, 3, 512]); hgin_r = Ring(2)
            gts = sb("gts", [128, 2, 3, 512]); gts_r = Ring(2)
        if pre_layer is not None:
            cst = sb("cst", [128, 2, TS_]); cst_t = Tl()

        tiles = [(i * 1024, 1024) for i in range(4)]
        if with_ctx:
            tiles[3] = (3072, 1024 + TCTX)
        assert tiles_override is None

        for (T0, ts) in tiles:
            blks = blocks(ts)
            for k in range(8):
                DMA('act', hT[:, k, :ts], hT_in[k * 128:(k + 1) * 128, T0:T0 + ts], (), hT_t[k], 'l')
            if merge_layer is not None:
                DMA('act', oda[:, :, :ts], odaT_d[:, T0:T0 + ts].rearrange("(c p) t -> p c t", p=128), (),
                    [t_ for row in oda_t for t_ in row], 'l')
                DMA('act', opl[:, :, :ts], opoolT_d[:, T0:T0 + ts].rearrange("(c p) t -> p c t", p=128), (), [opl_t], 'l')
                for c in range(4):
                    for bi, (t0, bw, ci) in enumerate(blks):
                        oc = oda[:, c, t0:t0 + bw]
                        si, s_t = sqb_r.next()
                        TT(sqb[:, si, :bw], oc, oc, ALU.mult, [oda_t[c][bi]], [s_t])
                        bk, bt = ps_r.next()
                        MM(ps[:, bk, :bw], ones_b[:], sqb[:, si, :bw], True, True, [s_t, ones_t], [bt])
                        ri, r_t = rstd_r.next()
                        ACT(rstd[:, ri, :bw], ps[:, bk, :bw], AF.Ln, [bt, vec_t], [r_t], scale=1.0 / 128, bias=epsb[:, 0:1])
                        ACT(rstd[:, ri, :bw], rstd[:, ri, :bw], AF.Exp, [r_t], [r_t], scale=-0.5)
                        STT(oc, oc, dagn[:, 0:1], rstd[:, ri, :bw], ALU.mult, ALU.mult, [oda_t[c][bi], r_t, vec_t],
                            [oda_t[c][bi]])
                for c2 in range(2):
                    for bi, (t0, bw, ci) in enumerate(blks):
                        hi_, h_t = hgin_r.next()
                        for s_i, src in enumerate((ohfT_d, ohbT_d, sgT_in_d)):
                            DMA('act', hgin[:, hi_, s_i, :bw], src[c2 * 128:(c2 + 1) * 128, T0 + t0:T0 + t0 + bw],
                                (), [h_t], 'l')
                        o0 = hgin[:, hi_, 0, :bw]
                        TT(o0, o0, hgin[:, hi_, 1, :bw], ALU.add, [h_t], [h_t])
                        ri, r_t = sumsq_rstd([(o0, h_t)], bw, bd64_b[:], bd64_t, 64)
                        TT(o0, o0, rstd[:, ri, :bw], ALU.mult, [h_t, r_t], [h_t])
                        STT(ohg[:, c2, t0:t0 + bw], o0, hgn[:, 0:1], hgin[:, hi_, 2, :bw], ALU.mult, ALU.mult,
                            [h_t, vec_t], [ohg_t[c2][bi]])
                gview = gateT_in_d.rearrange("(b c p) t -> c p b t", b=3, p=128)
                for i in range(8):
                    wi, wt = load_w8(wmrg_d[i])
                    for bi, (t0, bw, ci) in enumerate(blks):
                        gi, g_t = gts_r.next()
                        DMA('act', gts[:, gi, :, :bw], gview[i][:, :, T0 + t0:T0 + t0 + bw], (), [g_t], 'l')
                        bka, bta = ps_r.next()
                        bkh, bth = ps_r.next()
                        bkp, btp = ps_r.next()
                        for c in range(4):
                            MM(ps[:, bka, :bw], wk8[:, wi, c, :], oda[:, c, t0:t0 + bw], c == 0, c == 3,
                               [wt, oda_t[c][bi]], [bta])
                        for c in range(2):
                            MM(ps[:, bkh, :bw], wk8[:, wi, 4 + c, :], ohg[:, c, t0:t0 + bw], c == 0, c == 1,
                               [wt, ohg_t[c][bi]], [bth])
                        for c in range(2):
                            MM(ps[:, bkp, :bw], wk8[:, wi, 6 + c, :], opl[:, c, t0:t0 + bw], c == 0, c == 1, [wt, opl_t], [btp])
                        t1, t1t = tmpf_r.next()
                        t2, t2t = tmpf_r.next()
                        TT(tmpf[:, t1, :bw], gts[:, gi, 0, :bw], ps[:, bka, :bw], ALU.mult, [g_t, bta], [t1t])
                        TT(tmpf[:, t2, :bw], gts[:, gi, 1, :bw], ps[:, bkh, :bw], ALU.mult, [g_t, bth], [t2t])
                        TT(tmpf[:, t1, :bw], tmpf[:, t1, :bw], tmpf[:, t2, :bw], ALU.add, [t1t, t2t], [t1t])
                        TT(tmpf[:, t2, :bw], gts[:, gi, 2, :bw], ps[:, bkp, :bw], ALU.mult, [g_t, btp, t2t], [t2t])
                        TT(xn[:, i, t0:t0 + bw], tmpf[:, t1, :bw], tmpf[:, t2, :bw], ALU.add, [t1t, t2t], [xn_t[i][bi]])
                G5 = vecs['g5']
                for i in range(8):
                    wi, wt = load_w8(wout_d[i])
                    for bi, (t0, bw, ci) in enumerate(blks):
                        bk, bt = ps_r.next()
                        for kc in range(8):
                            MM(ps[:, bk, :bw], wk8[:, wi, kc, :], xn[:, kc, t0:t0 + bw], kc == 0, kc == 7,
                               [wt, xn_t[kc][bi]], [bt])
                        STT(hT[:, i, t0:t0 + bw], ps[:, bk, :bw], G5[:, i, ci:ci + 1], hT[:, i, t0:t0 + bw],
                            ALU.mult, ALU.add, [bt, hT_t[i][bi], vec_t], [hT_t[i][bi]])
                norm_mod(ts, vecs['f2A'], vecs['f2B'])
                ffn(ts, f2w1_d, f2w3_d, f2w2_d, vecs['g8'])
            if final:
                for bi, (t0, bw, ci) in enumerate(blks):
                    ri, r_t = sumsq_rstd([(hT[:, k, t0:t0 + bw], hT_t[k][bi]) for k in range(8)], bw,
                                         ones_b[:], ones_t, D)
                    for k in range(8):
                        si, s_t = stgf_r.next()
                        STT(stgf[:, si, :bw], hT[:, k, t0:t0 + bw], fng[:, k:k + 1], rstd[:, ri, :bw],
                            ALU.mult, ALU.mult, [hT_t[k][bi], r_t, vec_t], [s_t])
                        store(outT_d[k * 128:(k + 1) * 128, T0 + t0:T0 + t0 + bw], stgf[:, si, :bw], s_t)
            if pre_layer is not None:
                if dbg >= 1:
                    norm_mod(ts, vecs['f1A'], vecs['f1B'])
                if dbg >= 2:
                    ffn(ts, f1w1_d, f1w3_d, f1w2_d, vecs['g2'])
                for k in range(8):
                    DMA('sp', hT_out[k * 128:(k + 1) * 128, T0:T0 + ts], hT[:, k, :ts], hT_t[k], (), 's')
                if dbg < 3:
                    continue
                norm_mod(ts, vecs['mxA'], vecs['mxB'])
                DMA('act', cst[:, 0, :ts], cosT_d[:, T0:T0 + ts], (), [cst_t], 'l')
                DMA('act', cst[:, 1, :ts], sinT_d[:, T0:T0 + ts], (), [cst_t], 'l')
                for c in range(48):
                    wi, wt = load_w8(win_d[c])
                    for bi, (t0, bw, ci) in enumerate(blks):
                        bk, bt = ps_r.next()
                        for kc in range(8):
                            MM(ps[:, bk, :bw], wk8[:, wi, kc, :], xn[:, kc, t0:t0 + bw], kc == 0, kc == 7,
                               [wt, xn_t[kc][bi]], [bt])
                        zin = ps[:, bk, :bw]
                        tok = slice(T0 + t0, T0 + t0 + bw)
                        r2 = slice((c % 2) * 128, (c % 2 + 1) * 128)
                        if c < 8:
                            dst = (qT_d if c < 4 else kT_d)[(c % 4) * 128:(c % 4 + 1) * 128, tok]
                            si, s_t = sqb_r.next()
                            ACT(sqb[:, si, :bw], zin, AF.Copy, [bt], [s_t])
                            bk2, bt2 = ps_r.next()
                            MM(ps[:, bk2, :bw], rperm_b[:], sqb[:, si, :bw], True, True, [s_t, rperm_t], [bt2])
                            t1, t1t = tmpf_r.next()
                            t2, t2t = tmpf_r.next()
                            TT(tmpf[:, t1, :bw], zin, cst[:, 0, t0:t0 + bw], ALU.mult, [bt, cst_t], [t1t])
                            TT(tmpf[:, t2, :bw], ps[:, bk2, :bw], cst[:, 1, t0:t0 + bw], ALU.mult, [bt2, cst_t], [t2t])
                            oi, o_t = stgb_r.next()
                            TT(stgb[:, oi, :bw], tmpf[:, t1, :bw], tmpf[:, t2, :bw], ALU.add, [t1t, t2t], [o_t])
                            store(dst, stgb[:, oi, :bw], o_t)
                        elif c < 12 or 18 <= c < 20 or 22 <= c < 24:
                            if c < 12:
                                dst = vT_d[(c - 8) * 128:(c - 7) * 128, tok]
                            elif c < 20:
                                dst = hiT_d[r2, tok]
                            else:
                                dst = zpT_d[r2, tok]
                            oi, o_t = stgb_r.next()
                            CP(stgb[:, oi, :bw], zin, [bt], [o_t])
                            store(dst, stgb[:, oi, :bw], o_t)
                        elif c < 14 or 20 <= c < 22:
                            dst = (hqT_d if c < 14 else sgT_d)[r2, tok]
                            oi, o_t = stgf_r.next()
                            ACT(stgf[:, oi, :bw], zin, AF.Silu, [bt], [o_t])
                            store(dst, stgf[:, oi, :bw], o_t)
                        elif c < 18:
                            di = (c - 14) // 2
                            col = c - 14
                            kd = (kkfT_d, kkbT_d)[di][r2, tok]
                            ld = (lffT_d, lfbT_d)[di][r2, tok]
                            oi, o_t = stgf_r.next()
                            ACT(stgf[:, oi, :bw], zin, AF.Sigmoid, [bt], [o_t], scale=-1.0)
                            TS(stgf[:, oi, :bw], stgf[:, oi, :bw], oml[:, col:col + 1], ALU.mult, [o_t, vec_t], [o_t])
                            store(kd, stgf[:, oi, :bw], o_t)
                            o2, o2_t = stgf_r.next()
                            ACT(stgf[:, o2, :bw], stgf[:, oi, :bw], AF.Ln, [o_t, vec_t], [o2_t], scale=-1.0,
                                bias=oneb[:, 0:1])
                            store(ld, stgf[:, o2, :bw], o2_t)
                        else:
                            dst = gateT_d[(c - 24) * 128:(c - 23) * 128, tok]
                            oi, o_t = stgf_r.next()
                            ACT(stgf[:, oi, :bw], zin, AF.Sigmoid, [bt], [o_t])
                            store(dst, stgf[:, oi, :bw], o_t)
        P.emit(nc)
    return nc


def host_consts():
    ones = np.ones((128, 128), np.float32)
    bd64 = np.zeros((128, 128), np.float32)
    bd64[:64, :64] = 1
    bd64[64:, 64:] = 1
    R = np.zeros((128, 128), np.float32)
    for blk in (0, 64):
        for j in range(16):
            R[blk + j, blk + 16 + j] = -1
            R[blk + 16 + j, blk + j] = 1
            R[blk + 32 + j, blk + 48 + j] = -1
            R[blk + 48 + j, blk + 32 + j] = 1
    return ones, bd64, np.ascontiguousarray(R.T)


def rope_tables():
    t = np.arange(SEQ)
    row = (t // 64).astype(np.float32)
    col = (t % 64).astype(np.float32)
    inv = (np.float32(10000.0) ** (-np.arange(0, 32, 2, dtype=np.float32) / np.float32(32))).astype(np.float32)
    ar = (row[:, None] * inv[None]).astype(np.float32)
    ac = (col[:, None] * inv[None]).astype(np.float32)
    cos64 = np.concatenate([np.cos(ar), np.cos(ar), np.cos(ac), np.cos(ac)], 1).astype(np.float32)
    sin64 = np.concatenate([np.sin(ar), np.sin(ar), np.sin(ac), np.sin(ac)], 1).astype(np.float32)
    return np.tile(cos64.T, (2, 1)), np.tile(sin64.T, (2, 1))


def core_bq(c):
    return c // 4, c % 4


class HostW:
    def __init__(self, inp):
        self.inp = inp
        self.ones, self.bd64, self.rperm = host_consts()
        self.cosT, self.sinT = rope_tables()
        self.cache = {}

    def get(self, key, fn):
        if key not in self.cache:
            self.cache[key] = fn()
        return self.cache[key]

    def common(self, c, layers):
        inp = self.inp
        b, q = core_bq(c)
        d = dict(ones=self.ones, bd64=self.bd64, rperm=self.rperm)
        d['cs'] = np.ascontiguousarray(np.stack([vec_p(inp['c'][b]), vec_p(inp['c_ctx'])], axis=2))
        for l in layers:
            d[f'wada{l}'] = self.get(('wada', l), lambda: arr_w(inp['w_ada'][l]))
            d[f'bada{l}'] = self.get(('bada', l), lambda: vec_p(inp['b_ada'][l]))
        return d

    def pre(self, c, l, with_ctx=True):
        inp = self.inp
        b, q = core_bq(c)
        d = {}
        d['nf1'] = vec_p(inp['norm_ffn1'][l])
        d['nmix'] = vec_p(inp['norm_mix'][l])
        d['f1w1'] = self.get(('f1w1', l), lambda: arr_w(inp['ffn1_w1'][l]))
        d['f1w3'] = self.get(('f1w3', l), lambda: arr_w(inp['ffn1_w3'][l]))
        d['f1w2'] = self.get(('f1w2', l), lambda: arr_w(inp['ffn1_w2'][l]))
        d['win'] = self.get(('win', l), lambda: arr_w(inp['w_in'][l]))
        lg = inp['hg_lb_logits']
        lbl = np.zeros((128, 4, 2), np.float32)
        for di in range(2):
            for ch in range(2):
                for dep in range(2):
                    lbl[:, di * 2 + ch, dep] = lg[dep, di, ch * 128:(ch + 1) * 128]
        d['lbl'] = lbl
        cosT = self.cosT[:, q * TLAT:(q + 1) * TLAT]
        sinT = self.sinT[:, q * TLAT:(q + 1) * TLAT]
        if with_ctx:
            cosT = np.concatenate([cosT, np.ones((128, TCTX), np.float32)], 1)
            sinT = np.concatenate([sinT, np.zeros((128, TCTX), np.float32)], 1)
        d['cosT'] = np.ascontiguousarray(cosT)
        d['sinT'] = np.ascontiguousarray(sinT)
        return d

    def mrg(self, c, l):
        inp = self.inp
        d = {}
        d['nf2'] = vec_p(inp['norm_ffn2'][l])
        d['f2w1'] = self.get(('f2w1', l), lambda: arr_w(inp['ffn2_w1'][l]))
        d['f2w3'] = self.get(('f2w3', l), lambda: arr_w(inp['ffn2_w3'][l]))
        d['f2w2'] = self.get(('f2w2', l), lambda: arr_w(inp['ffn2_w2'][l]))
        d['wmrg'] = self.get(('wmrg', l), lambda: arr_w(np.concatenate(
            [inp['w_proj_da'][l], inp['w_proj_hg'][l], inp['w_proj_pool'][l]], 0)))
        d['wout'] = self.get(('wout', l), lambda: arr_w(inp['w_out'][l]))
        d['hgn'] = np.ascontiguousarray(np.tile(inp['hg_norm'][l], 2)[:, None])
        d['dag'] = np.ascontiguousarray(inp['da_subln'][l][:, None])
        return d


def initial_hT(inp, c):
    b, q = core_bq(c)
    xs = inp['x'][b, q * TLAT:(q + 1) * TLAT]
    cx = inp['ctx'][b, q * TCTX:(q + 1) * TCTX]
    return np.ascontiguousarray(np.concatenate([xs, cx], 0).T)


NQC = SEQ + CTX


def build_A(nq_tiles=32, nkb=130, with_ctxq=True):
    import contextlib
    nc = bass.Bass("TRN2", target_bir_lowering=False)
    P = Prog()
    O = Ops(P)
    MM, ACT, TT, TS, STT, CP, RCP, MS, DMA = O.MM, O.ACT, O.TT, O.TS, O.STT, O.CP, O.RCP, O.MS, O.DMA

    def din(name, shape, dt=F32):
        return nc.dram_tensor(name, list(shape), dt, kind="ExternalInput").ap()

    qT_d = din("qT", [128, NQC], BF16)
    kT_d = din("kT", [128, NKEY], BF16)
    v_d = din("v", [128, 130, 128], BF16)
    lamv_d = din("lamv", [128, 4, 64])
    lami_d = din("lami", [128, 2])
    gain_d = din("gain", [128, 1])
    ones_d = din("ones", [128, 128])
    oT_d = nc.dram_tensor("oT", [128, NQC], BF16, kind="ExternalOutput").ap()

    with contextlib.ExitStack() as st:
        def sb(name, shape, dt=F32):
            return st.enter_context(nc.sbuf_tensor("s_" + name, list(shape), dt))

        qT = sb("qT", [128, NQC], BF16)
        kT = sb("kT", [128, NKEY], BF16); kT_t = Tl(const=True)
        v = sb("v", [128, 130, 128], BF16); v_t = Tl(const=True)
        NPB = 12
        pb = sb("pb", [128, NPB, 2, 512], BF16); pb_r = Ring(NPB)
        tq = sb("tq", [128, 4, 2, 512], BF16); tq_r = Ring(4)
        acc = sb("acc", [128, 2, 512]); acc_t = Tl()
        fin = sb("fin", [128, 4, 512]); fin_r = Ring(4)
        ocp = sb("ocp", [128, 2, 2, 512]); oc_r = Ring(2)
        sqb = sb("sqb", [128, 512], BF16); sqb_t = Tl()
        ob = sb("ob", [128, 2, 512], BF16); ob_r = Ring(2)
        ones_f = sb("ones_f", [128, 128]); ones_b = sb("ones_b", [128, 128], BF16); c_t = Tl(const=True)
        lamv = sb("lamv", [128, 4, 64]); lami = sb("lami", [128, 2]); gain = sb("gain", [128, 1])
        lw = sb("lw", [128, 2, 64]); ls = sb("ls", [128, 2]); neglam = sb("neglam", [128, 1]); gsc = sb("gsc", [128, 1])
        epsb = sb("epsb", [128, 1])
        ps = st.enter_context(nc.psum_tensor("ps", [128, 8, 512], F32))
        s_r = Ring(6, excl=True)
        o_t = [Tl(excl=True), Tl(excl=True)]

        MS(epsb[:], EPS, [c_t])
        DMA('sp', ones_f[:], ones_d, (), [c_t], 'l')
        onesb_t = Tl(const=True)
        DMA('pool', ones_b[:], ones_d, (), [onesb_t], 'l')
        DMA('sp', lamv[:], lamv_d, (), [c_t], 'l')
        DMA('sp', lami[:], lami_d, (), [c_t], 'l')
        DMA('sp', gain[:], gain_d, (), [c_t], 'l')
        TT(lw[:], lamv[:, 0:4:2, :], lamv[:, 1:4:2, :], ALU.mult, [c_t], [c_t])
        P.op('dve', 'tensor_reduce', dict(out=ls[:], in_=lw[:], axis=mybir.AxisListType.X, op=ALU.add), [c_t], [c_t])
        ACT(ls[:], ls[:], AF.Exp, [c_t], [c_t])
        TT(neglam[:], ls[:, 0:1], ls[:, 1:2], ALU.subtract, [c_t], [c_t])
        STT(neglam[:], neglam[:], -1.0, lami[:, 0:1], ALU.mult, ALU.subtract, [c_t], [c_t])
        TT(gsc[:], gain[:], lami[:, 1:2], ALU.mult, [c_t], [c_t])

        for i in range(0, NKEY, 2080):
            DMA('sp', kT[:, i:i + 2080], kT_d[:, i:i + 2080], (), [kT_t], 'l')
        for i in range(0, 130, 13):
            DMA('sp', v[:, i:i + 13, :], v_d[:, i:i + 13, :], (), [v_t], 'l')

        qtiles = [(i * 512, 512, 0, nkb) for i in range(nq_tiles)]
        if with_ctxq:
            qtiles.append((SEQ, CTX, 128, 130))
        q_t = [Tl() for _ in qtiles]
        for qi, (q0, qw, kb0, kb1) in enumerate(qtiles):
            DMA('sp', qT[:, q0:q0 + qw], qT_d[:, q0:q0 + qw], (), [q_t[qi]], 'l')

        blocks = [(qi, kb) for qi, (q0, qw, kb0, kb1) in enumerate(qtiles) for kb in range(kb0, kb1)]
        LA = 2
        sbanks = {}

        def emit_qk(bi):
            qi, kb = blocks[bi]
            q0, qw, kb0, kb1 = qtiles[qi]
            banks = [s_r.next(), s_r.next()]
            sbanks[bi] = banks
            for c in range(2):
                bk, bt = banks[c]
                MM(ps[:, bk, :qw], kT[c * 64:(c + 1) * 64, kb * 128:(kb + 1) * 128],
                   qT[c * 64:(c + 1) * 64, q0:q0 + qw], True, True, [kT_t, q_t[qi]], [bt])

        pend = []
        accst = {'first': True, 'prev': None}

        def accumulate(xap, x_t, qw):
            if accst['first']:
                CP(acc[:, :, :qw], xap, [x_t], [acc_t])
                accst['first'] = False
            else:
                TT(acc[:, :, :qw], acc[:, :, :qw], xap, ALU.add, [x_t, acc_t], [acc_t])

        def emit_rest(bi):
            qi, kb = blocks[bi]
            q0, qw, kb0, kb1 = qtiles[qi]
            first = kb == kb0
            last = kb == kb1 - 1
            banks = sbanks.pop(bi)
            (bk0, bt0), (bk1, bt1) = banks
            assert bk1 == bk0 + 1
            pi, p_t = pb_r.next()
            ACT(pb[:, pi, :, :qw], ps[:, bk0:bk0 + 2, :qw], AF.Exp, [bt0, bt1], [p_t], scale=0.125)
            for c in range(2):
                MM(ps[:, 6 + c, :qw], v[:, kb, :], pb[:, pi, c, :qw], first, last, [v_t, p_t], [o_t[c]])
            if first:
                accst['first'] = True
                accst['prev'] = None
            if accst['prev'] is None and not last:
                accst['prev'] = (pi, p_t)
            else:
                if accst['prev'] is None:
                    pend.append((pb[:, pi, :, :qw], p_t))
                else:
                    ppi, pp_t = accst['prev']
                    accst['prev'] = None
                    ti, t_t = tq_r.next()
                    TT(tq[:, ti, :, :qw], pb[:, ppi, :, :qw], pb[:, pi, :, :qw], ALU.add, [pp_t, p_t], [t_t])
                    pend.append((tq[:, ti, :, :qw], t_t))
                if len(pend) == 2:
                    (xa, xa_t), (xb_, xb_t) = pend
                    TT(xa, xa, xb_, ALU.add, [xa_t, xb_t], [xa_t])
                    accumulate(xa, xa_t, qw)
                    del pend[:]
                if last and pend:
                    accumulate(pend[0][0], pend[0][1], qw)
                    del pend[:]
            if not last:
                return
            oci, oc_t = oc_r.next()
            for c in range(2):
                ACT(ocp[:, oci, c, :qw], ps[:, 6 + c, :qw], AF.Copy, [o_t[c]], [oc_t])
            ts_ = []
            for c in range(2):
                bk, bt = s_r.next()
                MM(ps[:, bk, :qw], ones_f[:], acc[:, c, :qw], True, True, [c_t, acc_t], [bt])
                fi, f_t = fin_r.next()
                RCP(fin[:, fi, :qw], ps[:, bk, :qw], [bt], [f_t])
                TT(fin[:, fi, :qw], ocp[:, oci, c, :qw], fin[:, fi, :qw], ALU.mult, [oc_t, f_t], [f_t])
                ts_.append((fi, f_t))
            (f0, f0t), (f1, f1t) = ts_
            oi, ob_t = ob_r.next()
            STT(ob[:, oi, :qw], fin[:, f1, :qw], neglam[:, 0:1], fin[:, f0, :qw], ALU.mult, ALU.add,
                [f0t, f1t, c_t], [ob_t])
            DMA('sp', oT_d[:, q0:q0 + qw], ob[:, oi, :qw], [ob_t], (), 's')

        base = 0
        for qi, (q0, qw, kb0, kb1) in enumerate(qtiles):
            n = kb1 - kb0
            if s_r.i % 2:
                s_r.next()
            for j in range(n + LA):
                if j < n:
                    emit_qk(base + j)
                if j - LA >= 0:
                    emit_rest(base + j - LA)
            base += n
        P.emit(nc)
    return nc


NCH = NKEY // 64


def build_H(groups=None, pool_tiles=128, do_pool=True):
    import contextlib
    nc = bass.Bass("TRN2", target_bir_lowering=False)
    P = Prog()
    O = Ops(P)
    MM, TR, ACT, TT, TS, STT, CP, RCP, MS, DMA = O.MM, O.TR, O.ACT, O.TT, O.TS, O.STT, O.CP, O.RCP, O.MS, O.DMA
    if groups is None:
        groups = [(0, 4)] + [(4 + 8 * i, 8) for i in range(32)]

    def din(name, shape, dt=F32):
        return nc.dram_tensor(name, list(shape), dt, kind="ExternalInput").ap()

    hq_d = [din(f"hq{s}", [64, NKEY]) for s in range(2)]
    kk_d = [din(f"kk{s}", [64, NKEY]) for s in range(2)]
    lf_d = [din(f"lf{s}", [64, NKEY]) for s in range(2)]
    vt_d = [din(f"vt{s}", [64, NCH, 64], BF16) for s in range(2)]
    reset_d = din("reset", [64, 512])
    mask_d = din("mask", [64, 64])
    ident_d = din("ident", [64, 64])
    oT_d = [nc.dram_tensor(f"oT{s}", [64, NKEY], F32, kind="ExternalOutput").ap() for s in range(2)]
    if do_pool:
        zp_d = din("zp", [128, 130, 64], BF16)
        band_d = din("band", [5, 128, 128])
        pw_d = din("pw", [64, 64])
        psc_d = din("psc", [64, 1])
        opT_d = nc.dram_tensor("opT", [64, NKEY], BF16, kind="ExternalOutput").ap()

    with contextlib.ExitStack() as st:
        def sb(name, shape, dt=F32):
            return st.enter_context(nc.sbuf_tensor("s_" + name, list(shape), dt))

        reset = sb("reset", [64, 512]); mask = sb("mask", [64, 64]); c_t = Tl(const=True)
        ident = sb("ident", [64, 64], BF16); cb_t = Tl(const=True)
        DMA('sp', reset[:], reset_d, (), [c_t], 'l')
        DMA('sp', mask[:], mask_d, (), [c_t], 'l')
        DMA('pool', ident[:], ident_d, (), [cb_t], 'l')
        ps = st.enter_context(nc.psum_tensor("ps", [128, 8, 512], F32))
        sc_t = Tl(excl=True)
        kt_t = Tl(excl=True)
        ot_t = [[Tl(excl=True), Tl(excl=True)], [Tl(excl=True), Tl(excl=True)]]
        kv_t = [Tl(excl=True), Tl(excl=True)]
        kt_bf = ps[:, 1, :].bitcast(BF16)

        S = []
        for s in range(2):
            d = {}
            d['gin'] = sb(f"gin{s}", [64, 2, 3, 512]); d['gin_r'] = Ring(2)
            d['a'] = sb(f"a{s}", [64, 2, 512]); d['a_r'] = Ring(2)
            d['e1'] = sb(f"e1{s}", [64, 2, 512]); d['e1_r'] = Ring(2)
            d['e2'] = sb(f"e2{s}", [64, 2, 512]); d['e2_r'] = Ring(2)
            d['qp'] = sb(f"qp{s}", [64, 2, 512], BF16); d['qp_r'] = Ring(2)
            d['kp'] = sb(f"kp{s}", [64, 2, 512], BF16); d['kp_r'] = Ring(2)
            d['sm'] = sb(f"sm{s}", [64, 2, 512], BF16); d['sm_r'] = Ring(2)
            d['ktok'] = sb(f"ktok{s}", [64, 2, 512], BF16); d['ktok_r'] = Ring(2)
            d['sc1'] = sb(f"sc1{s}", [64, 2, 8]); d['sc1_r'] = Ring(2)
            d['ser'] = sb(f"ser{s}", [64, 2, 8]); d['ser_r'] = Ring(2)
            d['v'] = sb(f"v{s}", [64, NCH, 64], BF16); d['v_t'] = Tl(const=True)
            d['state'] = sb(f"state{s}", [64, 2, 64]); d['state_t'] = [Tl(), Tl()]; d['sp'] = 0
            d['sr'] = sb(f"sr{s}", [64, 64], BF16); d['sr_t'] = Tl()
            d['ost'] = sb(f"ost{s}", [64, 2, 512]); d['ost_r'] = Ring(2)
            MS(d['state'][:], 0.0, d['state_t'])
            for t_ in d['sm_r'].t:
                pass
            MS(d['sm'][:], 0.0, d['sm_r'].t)
            for i in range(0, NCH, 52):
                DMA('sp', d['v'][:, i:i + 52, :], vt_d[s][:, i:i + 52, :], (), [d['v_t']], 'l')
            S.append(d)

        def pre(s, g, out):
            d = S[s]
            c0, n = groups[g]
            W = 64 * n
            t0 = 64 * c0
            gi, g_t = d['gin_r'].next()
            for j, src in enumerate((hq_d[s], kk_d[s], lf_d[s])):
                DMA('sp', d['gin'][:, gi, j, :W], src[:, t0:t0 + W], (), [g_t], 'l')
            yield
            ai, a_t = d['a_r'].next()
            a = d['a'][:, ai, :W]
            P.op('dve', 'tensor_tensor_scan', dict(out=a, data0=reset[:, :W], data1=d['gin'][:, gi, 2, :W], initial=0.0,
                                                   op0=ALU.mult, op1=ALU.add), [g_t, c_t], [a_t])
            a3 = a.rearrange("p (c t) -> p c t", t=64)
            yield
            s1i, s1_t = d['sc1_r'].next()
            eri, er_t = d['ser_r'].next()
            ACT(d['sc1'][:, s1i, :n], a3[:, :, 63], AF.Exp, [a_t], [s1_t])
            ACT(d['ser'][:, eri, :n], a3[:, :, 31], AF.Exp, [a_t], [er_t])
            e2i, e2_t = d['e2_r'].next()
            dd = d['e2'][:, e2i, :W]
            TT(dd.rearrange("p (c t) -> p c t", t=64), a3, a3[:, :, 31:32].to_broadcast([64, n, 64]), ALU.subtract,
               [a_t], [e2_t])
            yield
            e1i, e1_t = d['e1_r'].next()
            e1 = d['e1'][:, e1i, :W]
            ACT(e1, dd, AF.Exp, [e2_t], [e1_t])
            ACT(dd, dd, AF.Exp, [e2_t], [e2_t], scale=-1.0)
            yield
            qi, q_t = d['qp_r'].next()
            ki, k_t = d['kp_r'].next()
            TT(d['qp'][:, qi, :W], d['gin'][:, gi, 0, :W], e1, ALU.mult, [g_t, e1_t], [q_t])
            TT(d['kp'][:, ki, :W], d['gin'][:, gi, 1, :W], dd, ALU.mult, [g_t, e2_t], [k_t])
            yield
            for c in range(n):
                MM(ps[0:64, 0, c * 64 + 32:(c + 1) * 64], d['kp'][:, ki, c * 64:(c + 1) * 64],
                   d['qp'][:, qi, c * 64 + 32:(c + 1) * 64], True, True, [k_t, q_t], [sc_t])
                MM(ps[0:32, 0, c * 64:c * 64 + 32], d['kp'][:, ki, c * 64:c * 64 + 32],
                   d['qp'][:, qi, c * 64:c * 64 + 32], True, True, [k_t, q_t], [sc_t])
            for c in range(n):
                TR(kt_bf[0:64, c * 64:(c + 1) * 64], d['kp'][:, ki, c * 64:(c + 1) * 64], ident[:], [k_t, cb_t], [kt_t])
            smi, sm_t = d['sm_r'].next()
            sm3 = d['sm'][:, smi, :W].rearrange("p (c t) -> p c t", t=64)
            sc3 = ps[0:64, 0, :W].rearrange("p (c t) -> p c t", t=64)
            TT(sm3[:, :, 32:64], sc3[:, :, 32:64], mask[:, 32:64].unsqueeze(1).to_broadcast([64, n, 32]), ALU.mult,
               [sc_t, c_t], [sm_t])
            TT(sm3[0:32, :, 0:32], sc3[0:32, :, 0:32], mask[0:32, 0:32].unsqueeze(1).to_broadcast([32, n, 32]),
               ALU.mult, [sc_t, c_t, sm_t], [sm_t])
            kti, kt2_t = d['ktok_r'].next()
            CP(d['ktok'][:, kti, :W], kt_bf[0:64, :W], [kt_t], [kt2_t])
            out.update(dict(par=g % 2, c0=c0, n=n, W=W, t0=t0, qi=qi, q_t=q_t, smi=smi, sm_t=sm_t, kti=kti, kt2_t=kt2_t,
                            s1i=s1i, s1_t=s1_t, eri=eri, er_t=er_t, e1i=e1i, e1_t=e1_t))

        def step(s, pr, c):
            d = S[s]
            ch = pr['c0'] + c
            cs_ = slice(c * 64, (c + 1) * 64)
            po = d['sp']
            pn = 1 - po
            d['sp'] = pn
            st_o, st_n = d['state'][:, po, :], d['state'][:, pn, :]
            so_t, sn_t = d['state_t'][po], d['state_t'][pn]
            ACT(d['sr'][:], st_o, AF.Copy, [so_t, pr['er_t']], [d['sr_t']],
                scale=d['ser'][:, pr['eri'], c:c + 1])
            ob_ = (2 + s) if pr['par'] == 0 else (6 + s)
            MM(ps[0:64, ob_, cs_], d['v'][:, ch, :], d['sm'][:, pr['smi'], cs_], True, False,
               [d['v_t'], pr['sm_t']], [ot_t[s][pr['par']]])
            MM(ps[0:64, ob_, cs_], d['sr'][:], d['qp'][:, pr['qi'], cs_], False, True,
               [d['sr_t'], pr['q_t']], [ot_t[s][pr['par']]])
            MM(ps[0:64, 4 + s, 0:64], d['ktok'][:, pr['kti'], cs_], d['v'][:, ch, :], True, True,
               [pr['kt2_t'], d['v_t']], [kv_t[s]])
            TS(st_n, st_o, d['sc1'][:, pr['s1i'], c:c + 1], ALU.mult, [so_t, pr['s1_t']], [sn_t])
            STT(st_n, ps[0:64, 4 + s, 0:64], d['e1'][:, pr['e1i'], c * 64 + 63:c * 64 + 64], st_n,
                ALU.mult, ALU.add, [kv_t[s], pr['e1_t'], sn_t], [sn_t])

        def fin(s, pr):
            d = S[s]
            W = pr['W']
            oi, o_t = d['ost_r'].next()
            ob_ = (2 + s) if pr['par'] == 0 else (6 + s)
            ACT(d['ost'][:, oi, :W], ps[0:64, ob_, :W], AF.Copy, [ot_t[s][pr['par']]], [o_t])
            DMA('pool', oT_d[s][:, pr['t0']:pr['t0'] + W], d['ost'][:, oi, :W], [o_t], (), 's')

        def drain(gens):
            for gg in gens:
                for _ in gg:
                    pass

        prs = [{}, {}]
        drain([pre(0, 0, prs[0]), pre(1, 0, prs[1])])
        for g in range(len(groups)):
            nxt, gens = None, []
            if g + 1 < len(groups):
                nxt = [{}, {}]
                gens = [pre(0, g + 1, nxt[0]), pre(1, g + 1, nxt[1])]
            for c in range(groups[g][1]):
                step(0, prs[0], c)
                step(1, prs[1], c)
                for gg in gens:
                    next(gg, None)
            drain(gens)
            fin(0, prs[0])
            fin(1, prs[1])
            prs = nxt

        if do_pool:
            zp = sb("zp", [128, 130, 64], BF16); zp_t = Tl(const=True)
            band = sb("band", [128, 5, 128], BF16); pw = sb("pw", [64, 64], BF16); pc_t = Tl(const=True)
            psc = sb("psc", [64, 1]); pcs_t = Tl(const=True)
            mxb = sb("mxb", [64, 2, 512], BF16); mxb_r = Ring(2)
            pob = sb("pob", [64, 2, 512], BF16); pob_r = Ring(2)
            for i in range(0, 130, 26):
                DMA('sp', zp[:, i:i + 26, :], zp_d[:, i:i + 26, :], (), [zp_t], 'l')
            for i in range(5):
                DMA('pool', band[:, i, :], band_d[i], (), [pc_t], 'l')
            DMA('pool', pw[:], pw_d, (), [pc_t], 'l')
            DMA('sp', psc[:], psc_d, (), [pcs_t], 'l')
            mx_t = ot_t[0][1]
            py_t = ot_t[1][1]
            seqs = [(0, pool_tiles, 256 // 1)] if False else None
            plan = []
            for (tb, nt, tok0) in ((0, pool_tiles, CTX), (128, 2, 0)):
                for g0 in range(0, nt, 4):
                    plan.append((tb, nt, tok0, g0, min(4, nt - g0)))
            for (tb, nt, tok0, g0, gn) in plan:
                for j in range(gn):
                    ti = g0 + j
                    terms = []
                    if ti == 0:
                        terms.append((ti, 1))
                    elif ti == nt - 1:
                        terms.append((ti, 2))
                    else:
                        terms.append((ti, 0))
                    if ti > 0:
                        terms.append((ti - 1, 3))
                    if ti < nt - 1:
                        terms.append((ti + 1, 4))
                    for k, (src, bi) in enumerate(terms):
                        MM(ps[0:64, 6, j * 128:(j + 1) * 128], zp[:, tb + src, :], band[:, bi, :], k == 0,
                           k == len(terms) - 1, [zp_t, pc_t], [mx_t])
                mi, m_t = mxb_r.next()
                CP(mxb[:, mi, :gn * 128], ps[0:64, 6, :gn * 128], [mx_t], [m_t])
                MM(ps[0:64, 7, :gn * 128], pw[:], mxb[:, mi, :gn * 128], True, True, [pc_t, m_t], [py_t])
                pi, p_t = pob_r.next()
                ACT(pob[:, pi, :gn * 128], ps[0:64, 7, :gn * 128], AF.Copy, [py_t, pcs_t], [p_t], scale=psc[:, 0:1])
                DMA('pool', opT_d[:, tok0 + g0 * 128:tok0 + (g0 + gn) * 128], pob[:, pi, :gn * 128], [p_t], (), 's')
        P.emit(nc)
    return nc


def h_consts():
    reset = np.ones((64, 512), np.float32)
    reset[:, ::64] = 0.0
    s = np.arange(64)
    mask = (s[:, None] <= s[None, :]).astype(np.float32)
    return dict(reset=reset, mask=mask, ident=np.eye(64, dtype=np.float32))


def zp_tiles(z_lat, z_ctx):
    a = z_lat.reshape(-1, 128, 64).transpose(1, 0, 2)
    b = z_ctx.reshape(-1, 128, 64).transpose(1, 0, 2)
    return np.ascontiguousarray(np.concatenate([a, b], 1))


def band_mats(w):
    h = w // 2
    n = 128 * 3
    t = np.arange(n)
    full = np.zeros((n, n), np.float64)
    for tt in range(n):
        lo, hi = tt - h, tt + h
        for ss in range(max(lo, 0), min(hi, n)):
            full[ss, tt] = 1.0 / w
    Bc = full[128:256, 128:256] - np.eye(128)
    Bp = full[0:128, 128:256]
    Bn = full[256:384, 128:256]
    first = np.zeros((128, 128))
    last = np.zeros((128, 128))
    for tt in range(128):
        lo, hi = max(tt - h, 0), tt + h
        cnt = hi - lo
        for ss in range(lo, min(hi, 128)):
            first[ss, tt] = 1.0 / cnt
        lo2, hi2 = tt - h, min(tt + h, 128)
        cnt2 = hi2 - lo2
        for ss in range(max(lo2, 0), hi2):
            last[ss, tt] = 1.0 / cnt2
    first -= np.eye(128)
    last -= np.eye(128)
    return np.ascontiguousarray(np.stack([Bc, first, last, Bp, Bn]).astype(np.float32))


_PROGS = {}


def _prog(key, fn):
    if key not in _PROGS:
        _PROGS[key] = fn()
    return _PROGS[key]


def _run(nc, maps):
    res = run_bass_kernel_spmd(nc, maps, core_ids=list(range(NCORE)))
    return [{k: np.asarray(v) for k, v in r.items()} for r in res.results]


def _gather_tok(rs, b, key, rows):
    lat = np.concatenate([rs[b * 4 + q][key][rows, :TLAT] for q in range(4)], 1)
    ctx = np.concatenate([rs[b * 4 + q][key][rows, TLAT:] for q in range(4)], 1)
    return lat, ctx


def _mixer_inputs(inp, rs, l):
    lam_init = 0.8 - 0.6 * float(np.exp(-0.3 * l))
    lamv = np.stack([inp['da_lambda_q1'][l], inp['da_lambda_k1'][l], inp['da_lambda_q2'][l], inp['da_lambda_k2'][l]])
    hc = h_consts()
    ones = np.ones((128, 128), np.float32)
    mapsA, mapsH = [], []
    for c in range(NCORE):
        b, h = core_bq(c)
        r128 = slice(h * 128, (h + 1) * 128)
        r64 = slice(h * 64, (h + 1) * 64)
        ql, qc = _gather_tok(rs, b, 'qT', r128)
        kl, kc = _gather_tok(rs, b, 'kT', r128)
        vl, vc = _gather_tok(rs, b, 'vT', r128)
        vtok = np.concatenate([vl, vc], 1).T
        dA = dict(qT=np.ascontiguousarray(np.concatenate([ql, qc], 1)),
                  kT=np.ascontiguousarray(np.concatenate([kl, kc], 1)),
                  v=np.ascontiguousarray(vtok.reshape(130, 128, 128).transpose(1, 0, 2)),
                  lamv=np.ascontiguousarray(np.broadcast_to(lamv[None], (128, 4, 64))).astype(np.float32),
                  lami=np.ascontiguousarray(np.broadcast_to(
                      np.array([lam_init, 1.0 - lam_init], np.float32)[None], (128, 2))),
                  gain=np.ascontiguousarray(inp['da_subln'][l][:, None]), ones=ones)
        mapsA.append(dA)
        dH = dict(hc)

        def scan_order(key, flip):
            lat, ctx = _gather_tok(rs, b, key, r64)
            if flip:
                lat, ctx = lat[:, ::-1], ctx[:, ::-1]
            return np.ascontiguousarray(np.concatenate([ctx, lat], 1))

        for s, (kkey, lkey) in enumerate((('kkfT', 'lffT'), ('kkbT', 'lfbT'))):
            dH[f'hq{s}'] = scan_order('hqT', s == 1)
            dH[f'kk{s}'] = scan_order(kkey, s == 1)
            dH[f'lf{s}'] = scan_order(lkey, s == 1)
            vi = scan_order('hiT', s == 1).T
            dH[f'vt{s}'] = np.ascontiguousarray(vi.reshape(NCH, 64, 64).transpose(1, 0, 2))
        zl, zc = _gather_tok(rs, b, 'zpT', r64)
        dH['zp'] = zp_tiles(np.ascontiguousarray(zl.T), np.ascontiguousarray(zc.T))
        dH['band'] = band_mats(2 ** (h + 1))
        dH['pw'] = np.ascontiguousarray(inp['pool_w'][l][h])
        dH['psc'] = np.ascontiguousarray(inp['pool_scale'][l][r64][:, None])
        mapsH.append(dH)
    return mapsA, mapsH


def _merge_inputs(rs, rA, rH, with_ctx):
    out = []
    for c in range(NCORE):
        b, q = core_bq(c)
        lat = slice(q * TLAT, (q + 1) * TLAT)
        cx = slice(q * TCTX, (q + 1) * TCTX)

        def cat(lat_part, ctx_part):
            return np.ascontiguousarray(np.concatenate([lat_part, ctx_part], 1) if with_ctx else lat_part)

        oda = [cat(rA[b * 4 + h]['oT'][:, lat], rA[b * 4 + h]['oT'][:, SEQ + q * TCTX:SEQ + (q + 1) * TCTX]) for h in range(4)]
        ohf, ohb, opl = [], [], []
        for h in range(4):
            r = rH[b * 4 + h]
            f = r['oT0']
            ohf.append(cat(f[:, CTX:][:, lat], f[:, :CTX][:, cx]))
            bw = r['oT1']
            ohb.append(cat(bw[:, CTX:][:, ::-1][:, lat], bw[:, :CTX][:, ::-1][:, cx]))
            p = r['opT']
            opl.append(cat(p[:, CTX:][:, lat], p[:, :CTX][:, cx]))
        n = NT if with_ctx else TLAT
        d = dict(odaT=np.concatenate(oda, 0), ohfT=np.concatenate(ohf, 0), ohbT=np.concatenate(ohb, 0),
                 opoolT=np.concatenate(opl, 0),
                 sgT_in=np.ascontiguousarray(rs[c]['sgT'][:, :n]),
                 gateT_in=np.ascontiguousarray(rs[c]['gateT'][:, :n]),
                 hT_in=np.ascontiguousarray(rs[c]['hT_out'][:, :n]))
        out.append(d)
    return out


def kernel(**inputs):
    return _forward(inputs)


def _forward(inputs, dbg=None):
    inp = {k: np.asarray(v) for k, v in inputs.items()}
    dbg = dbg or (lambda name, val: None)
    H = HostW(inp)
    ncT1 = _prog('T1', lambda: build_T(None, 0, False, True))
    maps = []
    for c in range(NCORE):
        d = H.common(c, [0])
        d.update(H.pre(c, 0))
        d['hT_in'] = initial_hT(inp, c)
        maps.append(d)
    r1 = _run(ncT1, maps)
    dbg('r1', r1)
    ncA = _prog('A', build_A)
    ncH = _prog('H', build_H)
    mA, mH = _mixer_inputs(inp, r1, 0)
    rA = _run(ncA, mA)
    dbg('rA0', rA)
    rH = _run(ncH, mH)
    dbg('rH0', rH)
    del mA, mH
    ncT2 = _prog('T2', lambda: build_T(0, 1, False, True))
    mm = _merge_inputs(r1, rA, rH, True)
    maps = []
    for c in range(NCORE):
        d = H.common(c, [0, 1])
        d.update(H.mrg(c, 0))
        d.update(H.pre(c, 1))
        d.update(mm[c])
        maps.append(d)
    del r1, rA, rH
    r2 = _run(ncT2, maps)
    dbg('r2', r2)
    mA, mH = _mixer_inputs(inp, r2, 1)
    rA = _run(ncA, mA)
    dbg('rA1', rA)
    rH = _run(ncH, mH)
    dbg('rH1', rH)
    del mA, mH
    ncT3 = _prog('T3', lambda: build_T(1, None, True, False))
    mm = _merge_inputs(r2, rA, rH, False)
    maps = []
    for c in range(NCORE):
        d = H.common(c, [1])
        d.update(H.mrg(c, 1))
        d.update(mm[c])
        d['fnorm'] = vec_p(inp['final_norm'])
        maps.append(d)
    r3 = _run(ncT3, maps)
    out = np.empty((BATCH, SEQ, D), np.float32)
    for c in range(NCORE):
        b, q = core_bq(c)
        out[b, q * TLAT:(q + 1) * TLAT] = r3[c]['outT'].T
    return out
```
